# Optimizing a Trainium2 kernel written in Bass

```python
import jax, jax.numpy as jnp
from jax import lax
import numpy as np

D_MODEL = 1024
BATCH = 16
SEQ = 2048
DEPTH = 2

N_A_LAYERS = DEPTH // 2
N_B_LAYERS = DEPTH - N_A_LAYERS

D_FF = 2816
MACARON_WEIGHT = 0.5

HG_HEADS = 8
HG_DK = 128
HG_FD = HG_HEADS * HG_DK
HG_DV = D_MODEL // HG_HEADS
HG_CHUNK = 64

ATT_HEADS = 16
ATT_KV_HEADS = 2
ATT_GROUP = ATT_HEADS // ATT_KV_HEADS
ATT_HEAD_DIM = 64
WINDOW = 128
ATT_SCALE = ATT_HEAD_DIM ** -0.5

ROPE_THETA = 500000.0
ROT_DIM = ATT_HEAD_DIM // 4

EPS = 1e-6

kernel_name = "yoco_hgrn2_swa_sink_macaron"


def rms_norm(x, g):
    xf = x.astype(jnp.float32)
    y = xf * lax.rsqrt(jnp.mean(xf * xf, axis=-1, keepdims=True) + EPS)
    return (y * g.astype(jnp.float32)).astype(x.dtype)


def swiglu(h, w_gate_up, w_down):
    gate, up = jnp.split(h @ w_gate_up, 2, axis=-1)
    return (jax.nn.silu(gate) * up) @ w_down


def partial_rope(x, pos):
    half = ROT_DIM // 2
    inv_freq = jnp.power(ROPE_THETA, -jnp.arange(half, dtype=jnp.float32) * (2.0 / ROT_DIM))
    ang = pos.astype(jnp.float32)[:, None] * inv_freq[None, :]
    cos = jnp.cos(ang)[None, :, None, :]
    sin = jnp.sin(ang)[None, :, None, :]
    xf = x.astype(jnp.float32)
    x1 = xf[..., :half]
    x2 = xf[..., half:ROT_DIM]
    out = jnp.concatenate([x1 * cos - x2 * sin, x2 * cos + x1 * sin, xf[..., ROT_DIM:]], axis=-1)
    return out.astype(x.dtype)


def hgrn2_mixer(h, w_in, lower_bound, onorm_g, w_out):
    B, S, _ = h.shape
    n_chunks = S // HG_CHUNK
    proj = h @ w_in
    q, f, i, g = jnp.split(proj, [HG_FD, 2 * HG_FD, 2 * HG_FD + D_MODEL], axis=-1)
    q = jax.nn.silu(q.astype(jnp.float32))
    forget = lower_bound + (1.0 - lower_bound) * jax.nn.sigmoid(f.astype(jnp.float32))
    k = 1.0 - forget
    log_f = jnp.log(forget)

    def to_chunks(t, d):
        return t.reshape(B, n_chunks, HG_CHUNK, HG_HEADS, d).transpose(1, 0, 3, 2, 4)

    qc = to_chunks(q, HG_DK)
    kc = to_chunks(k, HG_DK)
    vc = to_chunks(i.astype(jnp.float32), HG_DV)
    lc = to_chunks(log_f, HG_DK)
    causal = jnp.tril(jnp.ones((HG_CHUNK, HG_CHUNK), dtype=bool))

    def step(state, inp):
        q_t, k_t, v_t, l_t = inp
        b = jnp.cumsum(l_t, axis=-2)
        b_end = b[..., -1:, :]
        q_e = q_t * jnp.exp(b)
        k_e = k_t * jnp.exp(-b)
        k_d = k_t * jnp.exp(b_end - b)
        scores = jnp.einsum('bhtk,bhsk->bhts', q_e, k_e)
        scores = jnp.where(causal, scores, 0.0)
        o = (jnp.einsum('bhtk,bhkv->bhtv', q_e, state)
             + jnp.einsum('bhts,bhsv->bhtv', scores, v_t))
        new_state = (jnp.exp(b_end)[..., 0, :, None] * state
                     + jnp.einsum('bhsk,bhsv->bhkv', k_d, v_t))
        return new_state, o

    state0 = jnp.zeros((B, HG_HEADS, HG_DK, HG_DV), dtype=jnp.float32)
    _, o = lax.scan(step, state0, (qc, kc, vc, lc))
    o = o.transpose(1, 0, 3, 2, 4).reshape(B, S, HG_HEADS, HG_DV)
    gate = jax.nn.silu(g.astype(jnp.float32)).reshape(B, S, HG_HEADS, HG_DV)
    o = rms_norm(o, onorm_g) * gate
    return o.reshape(B, S, D_MODEL).astype(h.dtype) @ w_out


def shared_kv(h, kv_norm_g, kv_w, k_norm_g, pos):
    B, S, _ = h.shape
    hn = rms_norm(h, kv_norm_g)
    k, v = jnp.split(hn @ kv_w, 2, axis=-1)
    k = k.reshape(B, S, ATT_KV_HEADS, ATT_HEAD_DIM)
    k = partial_rope(rms_norm(k, k_norm_g), pos)
    v = v.reshape(B, S, ATT_KV_HEADS, ATT_HEAD_DIM)
    return k, v


def swa_sink_mixer(h, k, v, w_q, q_norm_g, sinks, w_out, pos):
    B, S, _ = h.shape
    n_blocks = S // WINDOW
    q = (h @ w_q).reshape(B, S, ATT_HEADS, ATT_HEAD_DIM)
    q = partial_rope(rms_norm(q, q_norm_g), pos)
    qb = q.reshape(B, n_blocks, WINDOW, ATT_KV_HEADS, ATT_GROUP, ATT_HEAD_DIM).astype(jnp.float32)
    kb = k.reshape(B, n_blocks, WINDOW, ATT_KV_HEADS, ATT_HEAD_DIM).astype(jnp.float32)
    vb = v.reshape(B, n_blocks, WINDOW, ATT_KV_HEADS, ATT_HEAD_DIM).astype(jnp.float32)
    pad = jnp.zeros_like(kb[:, :1])
    k_band = jnp.concatenate([jnp.concatenate([pad, kb[:, :-1]], axis=1), kb], axis=2)
    v_band = jnp.concatenate([jnp.concatenate([pad, vb[:, :-1]], axis=1), vb], axis=2)
    scores = jnp.einsum('bnqkgd,bnskd->bnkgqs', qb, k_band) * ATT_SCALE
    q_pos = jnp.arange(S).reshape(n_blocks, WINDOW)[:, :, None]
    k_pos = (jnp.arange(n_blocks)[:, None] * WINDOW - WINDOW + jnp.arange(2 * WINDOW)[None, :])[:, None, :]
    delta = q_pos - k_pos
    allowed = (delta >= 0) & (delta < WINDOW) & (k_pos >= 0)
    scores = jnp.where(allowed[None, :, None, None], scores, -jnp.inf)
    sink = jnp.broadcast_to(
        sinks.astype(jnp.float32).reshape(ATT_KV_HEADS, ATT_GROUP)[None, None, :, :, None, None],
        scores.shape[:-1] + (1,))
    probs = jax.nn.softmax(jnp.concatenate([scores, sink], axis=-1), axis=-1)[..., :-1]
    o = jnp.einsum('bnkgqs,bnskd->bnqkgd', probs, v_band)
    o = o.reshape(B, S, ATT_HEADS * ATT_HEAD_DIM).astype(h.dtype)
    return o @ w_out


def setup_inputs(seed: int = 0) -> dict:
    key = jax.random.key(seed)
    ks = jax.random.split(key, 16)
    f32 = jnp.float32

    def w(k, shape, fan_in):
        return jax.random.normal(k, shape, f32) * (fan_in ** -0.5)

    def gain(k, shape):
        return 1.0 + 0.01 * jax.random.normal(k, shape, f32)

    return {
        "x": jax.random.normal(ks[0], (BATCH, SEQ, D_MODEL), f32),
        "ffn_norm_g": gain(ks[1], (DEPTH, 2, D_MODEL)),
        "ffn_w_gate_up": w(ks[2], (DEPTH, 2, D_MODEL, 2 * D_FF), D_MODEL),
        "ffn_w_down": w(ks[3], (DEPTH, 2, D_FF, D_MODEL), D_FF),
        "mix_norm_g": gain(ks[4], (DEPTH, D_MODEL)),
        "hgrn_w_in": w(ks[5], (N_A_LAYERS, D_MODEL, 2 * HG_FD + 2 * D_MODEL), D_MODEL),
        "hgrn_lb_logits": 0.1 * jax.random.normal(ks[6], (N_A_LAYERS + 1, HG_FD), f32),
        "hgrn_onorm_g": gain(ks[7], (N_A_LAYERS, HG_DV)),
        "hgrn_w_out": w(ks[8], (N_A_LAYERS, D_MODEL, D_MODEL), D_MODEL),
        "kv_norm_g": gain(ks[9], (D_MODEL,)),
        "kv_w": w(ks[10], (D_MODEL, 2 * ATT_KV_HEADS * ATT_HEAD_DIM), D_MODEL),
        "k_norm_g": gain(ks[11], (ATT_HEAD_DIM,)),
        "attn_w_q": w(ks[12], (N_B_LAYERS, D_MODEL, ATT_HEADS * ATT_HEAD_DIM), D_MODEL),
        "q_norm_g": gain(ks[13], (N_B_LAYERS, ATT_HEAD_DIM)),
        "attn_sinks": jax.random.normal(ks[14], (N_B_LAYERS, ATT_HEADS), f32),
        "attn_w_out": w(ks[15], (N_B_LAYERS, ATT_HEADS * ATT_HEAD_DIM, D_MODEL), ATT_HEADS * ATT_HEAD_DIM),
    }


def reference(x, ffn_norm_g, ffn_w_gate_up, ffn_w_down, mix_norm_g, hgrn_w_in, hgrn_lb_logits,
              hgrn_onorm_g, hgrn_w_out, kv_norm_g, kv_w, k_norm_g, attn_w_q, q_norm_g,
              attn_sinks, attn_w_out):
    pos = jnp.arange(x.shape[1])
    lower_bounds = jnp.cumsum(jax.nn.softmax(hgrn_lb_logits.astype(jnp.float32), axis=0), axis=0)
    h = x
    k_sh = None
    v_sh = None
    for layer in range(DEPTH):
        h = h + MACARON_WEIGHT * swiglu(rms_norm(h, ffn_norm_g[layer, 0]),
                                        ffn_w_gate_up[layer, 0], ffn_w_down[layer, 0])
        hn = rms_norm(h, mix_norm_g[layer])
        if layer < N_A_LAYERS:
            a = layer
            h = h + hgrn2_mixer(hn, hgrn_w_in[a], lower_bounds[a], hgrn_onorm_g[a], hgrn_w_out[a])
        else:
            b = layer - N_A_LAYERS
            h = h + swa_sink_mixer(hn, k_sh, v_sh, attn_w_q[b], q_norm_g[b], attn_sinks[b],
                                   attn_w_out[b], pos)
        h = h + MACARON_WEIGHT * swiglu(rms_norm(h, ffn_norm_g[layer, 1]),
                                        ffn_w_gate_up[layer, 1], ffn_w_down[layer, 1])
        if layer == N_A_LAYERS - 1:
            k_sh, v_sh = shared_kv(h, kv_norm_g, kv_w, k_norm_g, pos)
    return h
```

```python
import math
from contextlib import ExitStack

import numpy as np
import concourse.bass as bass
import concourse.mybir as mybir
from concourse.ap import AP
from concourse.bass_utils import run_bass_kernel_spmd

F32 = mybir.dt.float32
BF16 = mybir.dt.bfloat16
I32 = mybir.dt.int32
AF = mybir.ActivationFunctionType
ALU = mybir.AluOpType
AX = mybir.AxisListType

NCORES = 8
NSEQ = 2
S = 2048
D = 1024
DFF = 2816
NFC = DFF // 128
NT = S // 128
NG = NT // 4
EPS = 1e-6
DEBUG = False
STRICT = True
MASK_ENG = "dve"
SWA_DBG = 5
ALL_STAGES = ("ffn00", "hgrn", "ffn01", "kv", "ffn10", "swa", "ffn11")

C_G = 0
C_ONG = 56
C_IDENT = 64
C_UINC = 192
C_UEXC = 320
C_CIND = 448
C_MD = 456
C_MP = 584
C_SINK = 712
C_EPS = 728
C_ONE = 729
C_GK = 736
NCST = 864


def bc_mid(ap, n):
    a = ap.ap
    return AP(ap.tensor, ap.offset, [list(a[0]), [0, n], list(a[1])])


def bc_last(ap, n):
    a = ap.ap
    return AP(ap.tensor, ap.offset, [list(a[0]), list(a[1]), [0, n]])


class Op:
    __slots__ = ("eng", "fn", "deps", "dma", "token", "signal", "idx")

    def __init__(self, eng, fn, dma):
        self.eng = eng
        self.fn = fn
        self.deps = []
        self.dma = dma
        self.token = None
        self.signal = False


class Sched:
    COMPUTE = ("pe", "act", "dve", "pool")

    def __init__(self):
        self.ops = []
        self.state = {}
        self.last = {}

    def op(self, eng, fn, reads=(), writes=(), dma=None, join=False):
        o = Op(eng, fn, dma)
        raw, other = [], []
        for k in reads:
            st = self.state.get(k)
            if st:
                raw.extend(st[0])
        for k in writes:
            st = self.state.get(k)
            if st is None:
                st = self.state[k] = [[], {}, []]
            if not join:
                other.extend(st[0])
            other.extend(st[1].values())
            other.extend(st[2])
        seen = set()
        for d in raw:
            if id(d) not in seen:
                seen.add(id(d))
                o.deps.append(d)
        for d in other:
            if id(d) in seen:
                continue
            if (not STRICT) and d.dma is None and dma is None and d.eng == eng:
                continue
            seen.add(id(d))
            o.deps.append(d)
        for d in o.deps:
            d.signal = True
        for k in reads:
            st = self.state.get(k)
            if st is None:
                st = self.state[k] = [[], {}, []]
            if dma is None:
                st[1][eng] = o
            else:
                st[2].append(o)
        for k in writes:
            st = self.state[k]
            if join:
                st[0].append(o)
            else:
                st[0] = [o]
            st[1] = {}
            st[2] = []
        self.ops.append(o)
        if dma is None and fn is not None:
            self.last[eng] = o
        return o

    def barrier(self, engines=("pe", "act", "dve", "pool", "sp")):
        lasts = [self.last[e] for e in self.COMPUTE if e in self.last]
        for e in engines:
            o = Op(e, None, None)
            o.deps = [d for d in lasts if d.eng != e]
            for d in o.deps:
                d.signal = True
            self.ops.append(o)

    def wait_all(self, eng, ops):
        o = Op(eng, None, None)
        o.deps = list(ops)
        for d in o.deps:
            d.signal = True
        self.ops.append(o)

    def assign(self):
        EPOCH = 30000
        cnt = {}
        names = []
        for o in self.ops:
            if o.fn is None or not (o.signal or o.dma is not None):
                continue
            key = ("dma", o.dma) if o.dma is not None else ("eng", o.eng)
            c = cnt.get(key, 0)
            inc = 16 if o.dma is not None else 1
            c += inc
            cnt[key] = c
            ep, val = divmod(c - inc, EPOCH)
            name = (key, ep)
            if name not in names:
                names.append(name)
            o.token = (name, val + inc)
        return names

    def emit(self, eng_name, eng, sems):
        waited = {}
        for o in self.ops:
            if o.eng != eng_name:
                continue
            for d in o.deps:
                if d.token is None:
                    continue
                nm, val = d.token
                if waited.get(nm, 0) < val:
                    eng.wait_ge(sems[nm], val)
                    waited[nm] = val
            if o.fn is None:
                continue
            if o.dma is not None and o.token is not None:
                nm, val = o.token
                if val > 16 and waited.get(nm, 0) < val - 16:
                    eng.wait_ge(sems[nm], val - 16)
                    waited[nm] = val - 16
            ins = o.fn(eng)
            if o.token is not None:
                nm, val = o.token
                ins.then_inc(sems[nm], 16 if o.dma is not None else 1)


def build_program(nseq=NSEQ, stages=ALL_STAGES):
    nc = bass.Bass("TRN2", target_bir_lowering=False)
    sc = Sched()

    def dram_in(name, shape, dt=F32):
        return nc.dram_tensor(name, list(shape), dt, kind="ExternalInput").ap()

    def dram_scr(name, shape, dt=BF16):
        return nc.dram_tensor(name, list(shape), dt, kind="Internal").ap()

    x = dram_in("x", [nseq, S, D])
    wgu = dram_in("ffn_w_gate_up", [2, 2, D, 2 * DFF])
    wdn = dram_in("ffn_w_down", [2, 2, DFF, D])
    w_in = dram_in("hgrn_w_in", [D, 4 * D])
    hwo = dram_in("hgrn_w_out", [D, D])
    kvw = dram_in("kv_w", [D, 256])
    wq = dram_in("attn_w_q", [D, D])
    awo = dram_in("attn_w_out", [D, D])
    cst_d = dram_in("cst", [128, NCST])
    lbl_d = dram_in("lbl", [128, 2, D])
    gq_d = dram_in("gqb", [128, D])
    out = nc.dram_tensor("out", [nseq, S, D], F32, kind="ExternalOutput").ap()

    wgu_b = [dram_scr(f"wgu_b{i}", [NFC, 128, 2048]) for i in range(4)]
    wd_b = [dram_scr(f"wd_b{i}", [128, NFC * D]) for i in range(4)]
    win_b = dram_scr("win_b", [16, 128, 2048])
    hwo_b = dram_scr("hwo_b", [4, 128, 2048])
    kvw_b = dram_scr("kvw_b", [1, 128, 2048])
    wq_b = dram_scr("wq_b", [4, 128, 2048])
    awo_b = dram_scr("awo_b", [4, 128, 2048])

    es = ExitStack()
    with es:
        NW = 53200
        arena = es.enter_context(nc.sbuf_tensor("arena", [128, NW], F32))
        ps = es.enter_context(nc.psum_tensor("ps", [128, 8, 512], F32))
        cur = [0]

        def alloc(nwords):
            o = cur[0]
            cur[0] += nwords
            assert cur[0] <= NW, cur[0]
            return o

        def f32v(off, n):
            return arena[:, off:off + n]

        def bf16v(off, nwords):
            return arena[:, off:off + nwords].bitcast(BF16)

        h_off = alloc(NT * D)
        h = f32v(h_off, NT * D).rearrange("p (t d) -> p t d", t=NT)
        cst = f32v(alloc(NCST), NCST)
        hnT = [bf16v(alloc(2048), 2048).rearrange("p (k n) -> p k n", k=8) for _ in range(2)]
        wblk = [bf16v(alloc(1024), 1024).rearrange("p (k n) -> p k n", k=8) for _ in range(4)]
        xn = [bf16v(alloc(512), 512) for _ in range(2)]
        junk = bf16v(alloc(512), 512)
        ssum = f32v(alloc(16), 16)
        lnv = f32v(alloc(16), 16)
        rstd = f32v(alloc(16), 16)
        cbf = bf16v(alloc(256), 256)
        ident_bf = cbf[:, 0:128]
        uinc_bf = cbf[:, 128:256]
        maskd_bf = cbf[:, 256:384]
        maskp_bf = cbf[:, 384:512]
        cs_tab = f32v(alloc(256), 256).rearrange("p (a t i) -> p a t i", a=2, t=NT)
        kT = bf16v(alloc(2048), 2048).rearrange("p (h n) -> p h n", h=2)
        vsb = bf16v(alloc(1056), 1056).rearrange("p (t h d) -> p t h d", t=NT, h=2)
        esink = f32v(alloc(16), 16)
        cq_sf = [f32v(alloc(1024), 1024) for _ in range(2)]
        cq_sb = [bf16v(alloc(512), 512) for _ in range(2)]
        PH = alloc(0)
        PH_WORDS = NW - PH

        def ph(off, n):
            assert off + n <= PH_WORDS, (off, n, PH_WORDS)
            return PH + off

        ident_f = cst[:, C_IDENT:C_IDENT + 128]
        uinc_f = cst[:, C_UINC:C_UINC + 128]
        uexc_f = cst[:, C_UEXC:C_UEXC + 128]
        cind_f = cst[:, C_CIND:C_CIND + 2]
        eps_c = cst[:, C_EPS:C_EPS + 1]
        one_c = cst[:, C_ONE:C_ONE + 1]

        def pbank(b, n=1):
            return ps[:, b:b + n, :].rearrange("p b n -> p (b n)")

        def pbank_bf(b):
            return ps[:, b, :].bitcast(BF16)

        class WStream:
            def __init__(self, slots, tag):
                self.q = []
                self.loaded = 0
                self.cur = 0
                self.slots = slots
                self.tag = tag
                self.n = len(slots)

            def push(self, items):
                for wn in sorted({w_ for (_, w_) in items}):
                    CQ.require(wn)
                self.q.extend(items)

            def _load(self, i):
                slot = i % self.n
                src, wname = self.q[i]
                dst = self.slots[slot]
                sc.op("sp", lambda e, d=dst, s=src: e.dma_start(out=d.rearrange("p k n -> p (k n)"), in_=s),
                      reads=[("scr", wname)], writes=[(self.tag, slot)], dma=(self.tag, slot))

            def next(self):
                i = self.cur
                while self.loaded < min(len(self.q), i + self.n):
                    self._load(self.loaded)
                    self.loaded += 1
                self.cur += 1
                return self.slots[i % self.n], (self.tag, i % self.n)

        WS = WStream(wblk, "wblk")

        sc.op("sp", lambda e: e.dma_start(out=cst, in_=cst_d), writes=["cst"], dma="cst")
        sc.op("dve", lambda e: e.tensor_copy(out=ident_bf, in_=ident_f), reads=["cst"], writes=["cbf"])
        sc.op("dve", lambda e: e.tensor_copy(out=uinc_bf, in_=uinc_f), reads=["cst"], writes=["cbf"], join=True)
        sc.op("dve", lambda e: e.tensor_copy(out=maskd_bf, in_=cst[:, C_MD:C_MD + 128]), reads=["cst"],
              writes=["cbf"], join=True)
        sc.op("dve", lambda e: e.tensor_copy(out=maskp_bf, in_=cst[:, C_MP:C_MP + 128]), reads=["cst"],
              writes=["cbf"], join=True)

        class ConvQueue:
            def __init__(self):
                self.jobs = []
                self.nrec = 0
                self.castno = 0
                self.sf = cq_sf
                self.sb = cq_sb

            def add(self, wname, loads, casts, stores):
                self.jobs.append((wname, loads, casts, stores))

            def _L(self, j):
                slot = j % 2
                _, loads, _, _ = self.jobs[j]
                for li, (dview, src) in enumerate(loads):
                    sc.op("sp", lambda e, d=dview(self.sf[slot]), s=src: e.dma_start(out=d, in_=s),
                          writes=[("sf", slot)], dma=("sf", slot, li), join=(li > 0))

            def _C(self, j):
                slot = j % 2
                _, _, casts, _ = self.jobs[j]
                for ci, (oview, iview, gc) in enumerate(casts):
                    o_ap, i_ap = oview(self.sb[slot]), iview(self.sf[slot])
                    n = self.castno
                    self.castno += 1
                    if n % 2 == 0:
                        if gc is None:
                            fn = lambda e, o_ap=o_ap, i_ap=i_ap: e.tensor_copy(out=o_ap, in_=i_ap)
                        else:
                            fn = lambda e, o_ap=o_ap, i_ap=i_ap, gc=gc: e.tensor_scalar(
                                out=o_ap, in0=i_ap, scalar1=gc, scalar2=None, op0=ALU.mult)
                        eng = "dve"
                    else:
                        if gc is None:
                            fn = lambda e, o_ap=o_ap, i_ap=i_ap: e.activation(out=o_ap, in_=i_ap, func=AF.Copy)
                        else:
                            fn = lambda e, o_ap=o_ap, i_ap=i_ap, gc=gc: e.activation(out=o_ap, in_=i_ap,
                                                                                    func=AF.Copy, scale=gc)
                        eng = "act"
                    sc.op(eng, fn, reads=[("sf", slot), "cst"], writes=[("sb", slot)], join=(ci > 0))

            def _S(self, j):
                slot = j % 2
                wname, _, _, stores = self.jobs[j]
                for si, (dst, sview) in enumerate(stores):
                    sc.op("sp", lambda e, d=dst, s=sview(self.sb[slot]): e.dma_start(out=d, in_=s),
                          reads=[("sb", slot)], writes=[("scr", wname)], dma=("sbst", slot, si), join=True)

            def pump(self, n=1):
                for _ in range(n):
                    k = self.nrec
                    nj = len(self.jobs)
                    if k > nj + 1:
                        return
                    if k == 0 and nj > 0:
                        self._L(0)
                    if k + 1 < nj:
                        self._L(k + 1)
                    if k < nj:
                        self._C(k)
                    if 1 <= k <= nj:
                        self._S(k - 1)
                    self.nrec += 1

            def require(self, wname):
                last = max([i for i, jb in enumerate(self.jobs) if jb[0] == wname], default=-1)
                forced = 0
                while self.nrec < last + 2:
                    self.pump()
                    forced += 1
                if forced > 2 and DEBUG:
                    print("CQ.require", wname, "forced pumps", forced, "of", len(self.jobs))

        CQ = ConvQueue()

        def gcol(gi, kc):
            return cst[:, C_G + gi * 8 + kc:C_G + gi * 8 + kc + 1]

        def add_ffn_jobs(i4):
            l, j = divmod(i4, 2)
            W = wgu[l, j].rearrange("(k p) n -> p k n", p=128)
            wname = f"ffn{i4}"
            for fc in range(NFC):
                for k0 in (0, 4):
                    loads = [
                        (lambda t: t.rearrange("p (k u c) -> p k u c", k=4, u=2)[:, :, 0, :],
                         W[:, k0:k0 + 4, fc * 128:(fc + 1) * 128]),
                        (lambda t: t.rearrange("p (k u c) -> p k u c", k=4, u=2)[:, :, 1, :],
                         W[:, k0:k0 + 4, DFF + fc * 128:DFF + (fc + 1) * 128]),
                    ]
                    casts = [(lambda t, kk=kk: t[:, kk * 256:(kk + 1) * 256],
                              lambda t, kk=kk: t[:, kk * 256:(kk + 1) * 256], gcol(i4, k0 + kk)) for kk in range(4)]
                    stores = [(wgu_b[i4][fc][:, k0 * 256:(k0 + 4) * 256], lambda t: t)]
                    CQ.add(wname, loads, casts, stores)
            Wd = wdn[l, j].rearrange("(f p) m -> p f m", p=128)
            for f0 in range(NFC):
                loads = [(lambda t: t, Wd[:, f0, :])]
                casts = [(lambda t: t, lambda t: t, None)]
                stores = [(wd_b[i4][:, f0 * D:(f0 + 1) * D], lambda t: t)]
                CQ.add(wname, loads, casts, stores)

        def add_rows_jobs(wname, Wsrc, ncols, dst, gi, ong=False):
            Wv = Wsrc.rearrange("(k p) n -> p k n", p=128)
            dall = dst.rearrange("b p (k c) -> p k b c", k=8)
            if ncols >= 1024:
                nq = ncols // 1024
                for kc in range(8):
                    g_ = cst[:, C_ONG:C_ONG + 1] if ong else (None if gi is None else gcol(gi, kc))
                    for hh in range(nq):
                        loads = [(lambda t: t, Wv[:, kc, hh * 1024:(hh + 1) * 1024])]
                        casts = [(lambda t: t, lambda t: t, g_)]
                        stores = [(dall[:, kc, hh * 4:(hh + 1) * 4, :], lambda t: t.rearrange("p (b c) -> p b c", b=4))]
                        CQ.add(wname, loads, casts, stores)
            else:
                kper = 1024 // ncols
                ncb = ncols // 256
                for k0 in range(0, 8, kper):
                    loads = [(lambda t: t.rearrange("p (k n) -> p k n", k=kper), Wv[:, k0:k0 + kper, :])]
                    casts = []
                    for kk in range(kper):
                        g_ = cst[:, C_ONG:C_ONG + 1] if ong else (None if gi is None else gcol(gi, k0 + kk))
                        casts.append((lambda t, kk=kk: t[:, kk * ncols:(kk + 1) * ncols],
                                      lambda t, kk=kk: t[:, kk * ncols:(kk + 1) * ncols], g_))
                    stores = [(dall[:, k0 + kk, :, :],
                               lambda t, kk=kk: t[:, kk * ncols:(kk + 1) * ncols].rearrange("p (b c) -> p b c", b=ncb))
                              for kk in range(kper)]
                    CQ.add(wname, loads, casts, stores)

        for st in stages:
            if st.startswith("ffn"):
                add_ffn_jobs(int(st[3]) * 2 + int(st[4]))
            elif st == "hgrn":
                add_rows_jobs("win", w_in, 4096, win_b, 4)
                add_rows_jobs("hwo", hwo, 1024, hwo_b, None, ong=True)
            elif st == "kv":
                add_rows_jobs("kvw", kvw, 256, kvw_b, 6)
            elif st == "swa":
                add_rows_jobs("wq", wq, 1024, wq_b, 5)
                add_rows_jobs("awo", awo, 1024, awo_b, None)

        def rope_tables():
            pos_i = arena[:, PH:PH + 16].bitcast(I32)
            pos_f = f32v(ph(16, 16), 16)
            ang = f32v(ph(32, 256), 256).rearrange("p (a t i) -> p a t i", a=2, t=NT)
            kf = f32v(ph(288, 256), 256)
            ki = arena[:, PH + 544:PH + 800].bitcast(I32)
            red = f32v(ph(800, 256), 256)
            sc.op("pool", lambda e: e.iota(pos_i, pattern=[[128, 16]], base=0, channel_multiplier=1),
                  writes=["pos_i"])
            sc.op("dve", lambda e: e.tensor_copy(out=pos_f, in_=pos_i), reads=["pos_i"], writes=["pos_f"])
            first = True
            for i in range(8):
                fr = float(np.float32(500000.0) ** np.float32(-i * (2.0 / 16.0)))
                sc.op("dve", lambda e, i=i, fr=fr: e.tensor_scalar(out=ang[:, 0, :, i], in0=pos_f, scalar1=fr,
                                                                   scalar2=None, op0=ALU.mult),
                      reads=["pos_f"], writes=["ang0"], join=not first)
                first = False
            sc.op("dve", lambda e: e.tensor_scalar(out=ang[:, 1, :, :], in0=ang[:, 0, :, :], scalar1=math.pi / 2,
                                                   scalar2=None, op0=ALU.add), reads=["ang0"], writes=["ang1"])
            angf = f32v(ph(32, 256), 256)
            sc.op("dve", lambda e: e.tensor_scalar(out=kf, in0=angf, scalar1=1.0 / (2 * math.pi), scalar2=None,
                                                   op0=ALU.mult), reads=["ang0", "ang1"], writes=["kf"])
            sc.op("dve", lambda e: e.tensor_copy(out=ki, in_=kf), reads=["kf"], writes=["ki"])
            sc.op("dve", lambda e: e.tensor_copy(out=kf, in_=ki), reads=["ki"], writes=["kf"])
            sc.op("dve", lambda e: e.scalar_tensor_tensor(out=red, in0=kf, scalar=-2 * math.pi, in1=angf,
                                                          op0=ALU.mult, op1=ALU.add),
                  reads=["kf", "ang0", "ang1"], writes=["red"])
            wr = f32v(ph(1056, 256), 256)
            sc.op("dve", lambda e: e.tensor_scalar(out=wr, in0=red, scalar1=math.pi, scalar2=2 * math.pi,
                                                   op0=ALU.is_gt, op1=ALU.mult), reads=["red"], writes=["wr"])
            sc.op("dve", lambda e: e.tensor_tensor(out=red, in0=red, in1=wr, op=ALU.subtract),
                  reads=["red", "wr"], writes=["red"])
            sc.op("dve", lambda e: e.tensor_scalar(out=wr, in0=red, scalar1=-math.pi, scalar2=2 * math.pi,
                                                   op0=ALU.is_lt, op1=ALU.mult), reads=["red"], writes=["wr"])
            sc.op("dve", lambda e: e.tensor_tensor(out=red, in0=red, in1=wr, op=ALU.add),
                  reads=["red", "wr"], writes=["red"])
            sc.op("dve", lambda e: e.tensor_scalar(out=red, in0=red, scalar1=-math.pi, scalar2=math.pi,
                                                   op0=ALU.max, op1=ALU.min), reads=["red"], writes=["red"])
            sc.op("act", lambda e: e.activation(out=cs_tab.rearrange("p a t i -> p (a t i)"), in_=red, func=AF.Sin),
                  reads=["red"], writes=["cs_tab"])

        need_attn = ("kv" in stages) or ("swa" in stages)
        if need_attn:
            rope_tables()
            sc.op("act", lambda e: e.activation(out=esink, in_=cst[:, C_SINK:C_SINK + 16], func=AF.Exp),
                  reads=["cst"], writes=["esink"])
            sc.op("pool", lambda e: e.memset(vsb[:, :, :, 64:66], 1.0), writes=["vones"])
            sc.barrier()

        def sq_stat(tt):
            sc.op("act", lambda e: e.activation(out=junk, in_=h[:, tt, :], func=AF.Square,
                                                accum_out=ssum[:, tt:tt + 1]),
                  reads=[("h", tt)], writes=[("ss", tt)])

        def finish_stats():
            sc.op("act", lambda e: e.activation(out=lnv, in_=ssum, func=AF.Ln, scale=1.0 / D, bias=eps_c),
                  reads=[("ss", t) for t in range(NT)] + ["cst"], writes=["lnv"])
            sc.op("act", lambda e: e.activation(out=rstd, in_=lnv, func=AF.Exp, scale=-0.5),
                  reads=["lnv"], writes=["rstd"])

        tcount = [0]
        hcount = [0]

        def norm_group(g):
            hs = hcount[0] % 2
            hcount[0] += 1
            for t4 in range(4):
                tt = g * 4 + t4
                xs = tcount[0] % 2
                tb = tcount[0] % 2
                tcount[0] += 1
                sc.op("dve", lambda e, tt=tt, xs=xs: e.tensor_scalar(out=xn[xs], in0=h[:, tt, :],
                                                                     scalar1=rstd[:, tt:tt + 1], scalar2=None,
                                                                     op0=ALU.mult),
                      reads=[("h", tt), "rstd"], writes=[("xn", xs)])

                def tr(e, xs=xs, tb=tb):
                    pv = pbank_bf(tb).rearrange("p (k n) -> p k n", k=8)
                    ins = None
                    for k in range(8):
                        ins = e.transpose(out=pv[:, k, :], in_=xn[xs][:, k * 128:(k + 1) * 128], identity=ident_bf)
                    return ins
                sc.op("pe", tr, reads=[("xn", xs), "cbf"], writes=[("ps", tb)])
                sc.op("act", lambda e, hs=hs, t4=t4, tb=tb: e.activation(
                    out=hnT[hs][:, :, t4 * 128:(t4 + 1) * 128],
                    in_=pbank_bf(tb).rearrange("p (k n) -> p k n", k=8), func=AF.Copy),
                    reads=[("ps", tb)], writes=[("hnT", hs)], join=(t4 > 0))
            return hs

        def ffn(i4, last_phase, seq):
            wd_sb = bf16v(ph(0, 11264), 11264).rearrange("p (f m) -> p f m", f=NFC)
            actT = bf16v(ph(11264, 5632), 5632).rearrange("p (f n) -> p f n", f=NFC)
            sil = [f32v(ph(16896 + i * 512, 512), 512) for i in range(2)]
            finish_stats()
            WS.push([(wgu_b[i4][fc], f"ffn{i4}") for _ in range(NG) for fc in range(NFC)])
            gucount = 0
            dcount = 0
            for g in range(NG):
                hs = norm_group(g)
                for fc in range(NFC):
                    wb, wkey = WS.next()
                    if fc == 0:
                        for half in range(2):
                            f0, f1 = (0, 11) if half == 0 else (11, NFC)
                            sc.op("sp", lambda e, f0=f0, f1=f1: e.dma_start(
                                out=wd_sb[:, f0:f1, :].rearrange("p f m -> p (f m)"),
                                in_=wd_b[i4][:, f0 * D:f1 * D]),
                                reads=[("scr", f"ffn{i4}")], writes=["wd"], dma=("wd", half), join=(half == 1))
                    CQ.pump()
                    gs = gucount % 2
                    gucount += 1
                    bg, bu = 2 + gs, 4 + gs

                    def gu(e, wb=wb, hs=hs, bg=bg, bu=bu):
                        ins = None
                        for k in range(8):
                            ins = e.matmul(pbank(bg), lhsT=wb[:, k, 0:128], rhs=hnT[hs][:, k, :],
                                           start=(k == 0), stop=(k == 7))
                        for k in range(8):
                            ins = e.matmul(pbank(bu), lhsT=wb[:, k, 128:256], rhs=hnT[hs][:, k, :],
                                           start=(k == 0), stop=(k == 7))
                        return ins
                    sc.op("pe", gu, reads=[wkey, ("hnT", hs)], writes=[("ps", bg), ("ps", bu)])
                    sc.op("act", lambda e, gs=gs, bg=bg: e.activation(out=sil[gs], in_=pbank(bg), func=AF.Silu),
                          reads=[("ps", bg)], writes=[("sil", gs)])
                    sc.op("dve", lambda e, gs=gs, bu=bu, fc=fc: e.tensor_tensor(
                        out=actT[:, fc, :], in0=pbank(bu), in1=sil[gs], op=ALU.mult),
                        reads=[("ps", bu), ("sil", gs)], writes=[("actT", fc)])
                for t4 in range(4):
                    tt = g * 4 + t4
                    for half in range(2):
                        db = 6 + dcount % 2
                        dcount += 1

                        def dn(e, t4=t4, half=half, db=db):
                            ins = None
                            for fc in range(NFC):
                                ins = e.matmul(pbank(db), lhsT=actT[:, fc, t4 * 128:(t4 + 1) * 128],
                                               rhs=wd_sb[:, fc, half * 512:(half + 1) * 512],
                                               start=(fc == 0), stop=(fc == NFC - 1))
                            return ins
                        sc.op("pe", dn, reads=[("actT", fc) for fc in range(NFC)] + ["wd"], writes=[("ps", db)])
                        hv = h[:, tt, half * 512:(half + 1) * 512]
                        sc.op("dve", lambda e, hv=hv, db=db: e.scalar_tensor_tensor(
                            out=hv, in0=pbank(db), scalar=0.5, in1=hv, op0=ALU.mult, op1=ALU.add),
                            reads=[("ps", db), ("h", tt)], writes=[("h", tt)])
                    if last_phase:
                        store_tile(seq, tt)
                    else:
                        sq_stat(tt)

        stores = []

        def store_tile(seq, tt):
            o = sc.op("sp", lambda e: e.dma_start(out=out[seq, tt * 128:(tt + 1) * 128, :], in_=h[:, tt, :]),
                      reads=[("h", tt)], writes=[], dma=("st", tt))
            stores.append(o)

        def interleave(g1, g2):
            gens = [g for g in (g1, g2) if g is not None]
            while gens:
                for g in list(gens):
                    try:
                        next(g)
                    except StopIteration:
                        gens.remove(g)

        def hgrn(seq):
            o_ = [0]

            def A_(n):
                o = o_[0]
                o_[0] += n
                return ph(o, n)
            oml = f32v(A_(1024), 1024)
            kk = f32v(A_(1024), 1024)
            lf = f32v(A_(1024), 1024)
            e_off = A_(2048)
            E = [f32v(e_off + i * 1024, 1024) for i in range(2)]
            lbl = f32v(e_off, 2048).rearrange("p (a n) -> p a n", a=2)
            qs = lf
            osq = f32v(A_(1024), 1024)
            q_e = bf16v(A_(512), 512)
            k_e = bf16v(A_(512), 512)
            q_eTm = [bf16v(A_(1024), 1024).rearrange("p (h c t) -> p h c t", h=8, c=2) for _ in range(2)]
            k_eT = [bf16v(A_(512), 512).rearrange("p (h t) -> p h t", h=8) for _ in range(2)]
            k_d = [bf16v(A_(512), 512) for _ in range(2)]
            vv = [bf16v(A_(512), 512) for _ in range(2)]
            gate = [bf16v(A_(512), 512) for _ in range(2)]
            dec = [f32v(A_(16), 16) for _ in range(2)]
            scT = bf16v(A_(512), 512).rearrange("p (h t) -> p h t", h=8)
            S32 = f32v(A_(1024), 1024)
            Sbf = [bf16v(A_(512), 512) for _ in range(2)]
            ssq = f32v(A_(8), 8)
            lno = f32v(A_(8), 8)
            rso = f32v(A_(8), 8)
            yb = bf16v(A_(512), 512)
            yT = bf16v(A_(512), 512).rearrange("p (h t) -> p h t", h=8)
            wb2 = [bf16v(A_(1024), 1024).rearrange("p (k n) -> p k n", k=8) for _ in range(2)]
            WSB = WStream(wb2, "wb2")

            finish_stats()
            WS.push([(win_b[cb], "win") for tt in range(NT)
                     for cb in (4, 5, 6, 7, 0, 1, 2, 3, 8, 9, 10, 11, 12, 13, 14, 15)])
            WSB.push([(hwo_b[cb], "hwo") for tt in range(NT) for cb in range(4)])

            sc.op("sp", lambda e: e.dma_start(out=lbl.rearrange("p a n -> p (a n)"),
                                              in_=lbl_d.rearrange("p a n -> p (a n)")),
                  writes=["lbl"], dma="lbl")
            sc.op("dve", lambda e: e.tensor_tensor(out=kk, in0=lbl[:, 1, :], in1=lbl[:, 0, :], op=ALU.subtract),
                  reads=["lbl"], writes=["kk"])
            sc.op("act", lambda e: e.activation(out=oml, in_=kk, func=AF.Sigmoid), reads=["kk"], writes=["oml"])
            sc.barrier(("act", "dve"))
            sc.op("pool", lambda e: e.memset(S32, 0.0), writes=[("S32", hh) for hh in range(8)])
            sc.op("pool", lambda e: e.memset(Sbf[0], 0.0), writes=[("Sbf", 0)])
            for p_ in range(2):
                sc.op("pool", lambda e, p_=p_: e.memset(q_eTm[p_].rearrange("p h c t -> p (h c t)"), 0.0),
                      writes=[("q_eTm", p_)])

            P01 = pbank(0, 2)
            P23 = pbank(2, 2)
            P45 = pbank(4, 2)
            P67 = pbank(6, 2)
            hs_of = {}

            def proj(hs, t4):
                for i in range(4):
                    wb, wkey = WS.next()
                    bnk = i // 2
                    c0 = (i % 2) * 256

                    def f(e, wb=wb, bnk=bnk, c0=c0):
                        ins = None
                        for k in range(8):
                            ins = e.matmul(ps[:, bnk, c0:c0 + 256], lhsT=hnT[hs][:, k, t4 * 128:(t4 + 1) * 128],
                                           rhs=wb[:, k, :], start=(k == 0), stop=(k == 7))
                        return ins
                    sc.op("pe", f, reads=[wkey, ("hnT", hs)], writes=[("ps", bnk)], join=(i % 2 == 1))
                    yield

            def stageA(tt):
                g, t4 = divmod(tt, 4)
                p = tt % 2
                if t4 == 0:
                    hs_of[g] = norm_group(g)
                    yield
                hs = hs_of[g]
                yield from proj(hs, t4)
                sc.op("act", lambda e: e.activation(out=kk, in_=P01, func=AF.Sigmoid, scale=-1.0),
                      reads=[("ps", 0), ("ps", 1)], writes=["kk"])
                yield
                sc.op("dve", lambda e: e.tensor_tensor(out=kk, in0=kk, in1=oml, op=ALU.mult),
                      reads=["kk", "oml"], writes=["kk"])
                yield
                sc.op("act", lambda e: e.activation(out=lf, in_=kk, func=AF.Ln, scale=-1.0, bias=one_c),
                      reads=["kk", "cst"], writes=["lf"])
                yield

                def cums(e):
                    ins = None
                    for hf in range(2):
                        ins = e.matmul(ps[:, 2 + hf, :], lhsT=uinc_f, rhs=lf[:, hf * 512:(hf + 1) * 512],
                                       start=True, stop=True)
                    for hf in range(2):
                        ins = e.matmul(ps[:, 4 + hf, :], lhsT=uexc_f, rhs=lf[:, hf * 512:(hf + 1) * 512],
                                       start=True, stop=True)
                    for hh in range(8):
                        ins = e.matmul(ps[:, 0, hh * 2:hh * 2 + 2], lhsT=lf[:, hh * 128:(hh + 1) * 128],
                                       rhs=cind_f, start=True, stop=True)
                    return ins
                sc.op("pe", cums, reads=["lf", "cst"], writes=[("ps", 2), ("ps", 3), ("ps", 4), ("ps", 5), ("ps", 0)])
                yield
                sc.op("act", lambda e: e.activation(out=dec[p], in_=ps[:, 0, 0:16], func=AF.Exp),
                      reads=[("ps", 0)], writes=[("dec", p)])
                yield
                CQ.pump(1)
                yield
                yield from proj(hs, t4)
                CQ.pump(1)
                sc.op("act", lambda e: e.activation(out=qs, in_=P01, func=AF.Silu),
                      reads=[("ps", 0), ("ps", 1)], writes=["lf"])
                yield
                sc.op("act", lambda e: e.activation(out=E[0], in_=P23, func=AF.Exp),
                      reads=[("ps", 2), ("ps", 3)], writes=[("E", 0)])
                yield
                sc.op("dve", lambda e: e.tensor_tensor(out=q_e, in0=qs, in1=E[0], op=ALU.mult),
                      reads=["lf", ("E", 0)], writes=["q_e"])
                yield
                sc.op("act", lambda e: e.activation(out=E[1], in_=P23, func=AF.Exp, scale=-1.0),
                      reads=[("ps", 2), ("ps", 3)], writes=[("E", 1)])
                yield
                sc.op("dve", lambda e: e.tensor_tensor(out=k_e, in0=kk, in1=E[1], op=ALU.mult),
                      reads=["kk", ("E", 1)], writes=["k_e"])
                yield
                sc.op("act", lambda e: e.activation(out=E[0], in_=P45, func=AF.Exp),
                      reads=[("ps", 4), ("ps", 5)], writes=[("E", 0)])
                yield
                sc.op("dve", lambda e: e.tensor_tensor(out=k_d[p], in0=kk, in1=E[0], op=ALU.mult),
                      reads=["kk", ("E", 0)], writes=[("k_d", p)])
                yield
                yield from proj(hs, t4)
                CQ.pump(1)
                sc.op("act", lambda e: e.activation(out=vv[p], in_=P01, func=AF.Copy),
                      reads=[("ps", 0), ("ps", 1)], writes=[("vv", p)])
                yield
                yield from proj(hs, t4)
                CQ.pump(1)
                sc.op("act", lambda e: e.activation(out=gate[p], in_=P01, func=AF.Silu),
                      reads=[("ps", 0), ("ps", 1)], writes=[("gate", p)])
                yield

                def trq(e):
                    pv = pbank_bf(2).rearrange("p (k n) -> p k n", k=8)
                    ins = None
                    for k in range(8):
                        ins = e.transpose(out=pv[:, k, :], in_=q_e[:, k * 128:(k + 1) * 128], identity=ident_bf)
                    return ins
                sc.op("pe", trq, reads=["q_e", "cbf"], writes=[("ps", 2)])
                yield
                pv2 = pbank_bf(2).rearrange("p (k n) -> p k n", k=8)
                sc.op("dve", lambda e: e.tensor_copy(out=q_eTm[p][:, :, 0, 0:64], in_=pv2[:, :, 0:64]),
                      reads=[("ps", 2)], writes=[("q_eTm", p)])
                sc.op("dve", lambda e: e.tensor_copy(out=q_eTm[p][:, :, 1, 64:128], in_=pv2[:, :, 64:128]),
                      reads=[("ps", 2)], writes=[("q_eTm", p)], join=True)
                yield

                def trk(e):
                    pv = pbank_bf(3).rearrange("p (k n) -> p k n", k=8)
                    ins = None
                    for k in range(8):
                        ins = e.transpose(out=pv[:, k, :], in_=k_e[:, k * 128:(k + 1) * 128], identity=ident_bf)
                    return ins
                sc.op("pe", trk, reads=["k_e", "cbf"], writes=[("ps", 3)])
                yield
                sc.op("dve", lambda e: e.tensor_copy(out=k_eT[p], in_=pbank_bf(3).rearrange("p (k n) -> p k n", k=8)),
                      reads=[("ps", 3)], writes=[("k_eT", p)])
                yield

            def stageB(tt):
                p = tt % 2
                qm, kt_, kd_, v_, gt_, dc_ = q_eTm[p], k_eT[p], k_d[p], vv[p], gate[p], dec[p]

                def scores(e):
                    ins = None
                    for hh in range(8):
                        o_ap = ps[:, 6 + hh // 4, (hh % 4) * 128:(hh % 4 + 1) * 128]
                        ins = e.matmul(o_ap, lhsT=kt_[:, hh, :], rhs=qm[:, hh, 0, :], start=True, stop=False)
                        ins = e.matmul(o_ap, lhsT=kt_[:, hh, :], rhs=qm[:, hh, 1, :], start=False, stop=True)
                    return ins
                sc.op("pe", scores, reads=[("k_eT", p), ("q_eTm", p)], writes=[("ps", 6), ("ps", 7)])
                yield
                sc.op("dve", lambda e: e.tensor_tensor(out=scT, in0=P67.rearrange("p (h t) -> p h t", h=8),
                                                       in1=bc_mid(uinc_bf, 8), op=ALU.mult),
                      reads=[("ps", 6), ("ps", 7), "cbf"], writes=["scT"])
                yield
                for c in range(2):
                    def upd(e, c=c):
                        ins = None
                        for hh in range(8):
                            ins = e.matmul(ps[:, 6 + hh // 4, (hh % 4) * 128:(hh % 4 + 1) * 128],
                                           lhsT=kd_[c * 64:(c + 1) * 64, hh * 128:(hh + 1) * 128],
                                           rhs=v_[c * 64:(c + 1) * 64, hh * 128:(hh + 1) * 128],
                                           start=True, stop=True)
                        return ins
                    sc.op("pe", upd, reads=[("k_d", p), ("vv", p)], writes=[("ps", 6), ("ps", 7)])
                    yield
                    for hh in range(8):
                        sv = S32[:, hh * 128:(hh + 1) * 128]
                        sc.op("dve", lambda e, sv=sv, hh=hh, c=c: e.scalar_tensor_tensor(
                            out=sv, in0=sv, scalar=dc_[:, hh * 2 + c:hh * 2 + c + 1],
                            in1=ps[:, 6 + hh // 4, (hh % 4) * 128:(hh % 4 + 1) * 128],
                            op0=ALU.mult, op1=ALU.add),
                            reads=[("S32", hh), ("dec", p), ("ps", 6 + hh // 4)], writes=[("S32", hh)])
                        if hh % 4 == 3:
                            yield
                    if c == 0:
                        sc.op("act", lambda e: e.activation(out=Sbf[1], in_=S32, func=AF.Copy),
                              reads=[("S32", hh) for hh in range(8)], writes=[("Sbf", 1)])
                        yield

                def omm(e):
                    ins = None
                    for hh in range(8):
                        o_ap = ps[:, 6 + hh // 4, (hh % 4) * 128:(hh % 4 + 1) * 128]
                        ins = e.matmul(o_ap, lhsT=qm[:, hh, 0, :], rhs=Sbf[0][:, hh * 128:(hh + 1) * 128],
                                       start=True, stop=False)
                        ins = e.matmul(o_ap, lhsT=qm[:, hh, 1, :], rhs=Sbf[1][:, hh * 128:(hh + 1) * 128],
                                       start=False, stop=False)
                        ins = e.matmul(o_ap, lhsT=scT[:, hh, :], rhs=v_[:, hh * 128:(hh + 1) * 128],
                                       start=False, stop=True)
                    return ins
                sc.op("pe", omm, reads=[("q_eTm", p), ("Sbf", 0), ("Sbf", 1), "scT", ("vv", p)],
                      writes=[("ps", 6), ("ps", 7)])
                yield
                sc.op("act", lambda e: e.activation(out=Sbf[0], in_=S32, func=AF.Copy),
                      reads=[("S32", hh) for hh in range(8)], writes=[("Sbf", 0)])
                yield
                sc.op("act", lambda e: e.activation(out=osq, in_=P67, func=AF.Square),
                      reads=[("ps", 6), ("ps", 7)], writes=["osq"])
                yield
                sc.op("dve", lambda e: e.tensor_reduce(out=ssq, in_=osq.rearrange("p (h d) -> p h d", h=8),
                                                       axis=AX.X, op=ALU.add),
                      reads=["osq"], writes=["ssq"])
                yield
                sc.op("act", lambda e: e.activation(out=lno, in_=ssq, func=AF.Ln, scale=1.0 / 128, bias=eps_c),
                      reads=["ssq", "cst"], writes=["lno"])
                yield
                sc.op("act", lambda e: e.activation(out=rso, in_=lno, func=AF.Exp, scale=-0.5),
                      reads=["lno"], writes=["rso"])
                yield
                sc.op("dve", lambda e: e.tensor_tensor(out=osq.rearrange("p (h d) -> p h d", h=8),
                                                       in0=P67.rearrange("p (h d) -> p h d", h=8),
                                                       in1=bc_last(rso, 128), op=ALU.mult),
                      reads=[("ps", 6), ("ps", 7), "rso"], writes=["osq"])
                yield
                sc.op("dve", lambda e: e.tensor_tensor(out=yb, in0=osq, in1=gt_, op=ALU.mult),
                      reads=["osq", ("gate", p)], writes=["yb"])
                yield

                def try_(e):
                    pv = pbank_bf(6).rearrange("p (k n) -> p k n", k=8)
                    ins = None
                    for k in range(8):
                        ins = e.transpose(out=pv[:, k, :], in_=yb[:, k * 128:(k + 1) * 128], identity=ident_bf)
                    return ins
                sc.op("pe", try_, reads=["yb", "cbf"], writes=[("ps", 6)])
                yield
                sc.op("act", lambda e: e.activation(out=yT, in_=pbank_bf(6).rearrange("p (k n) -> p k n", k=8),
                                                    func=AF.Copy), reads=[("ps", 6)], writes=["yT"])
                yield
                for i in range(4):
                    wb, wkey = WSB.next()
                    bnk = 6 + i // 2
                    c0 = (i % 2) * 256

                    def f(e, wb=wb, bnk=bnk, c0=c0):
                        ins = None
                        for k in range(8):
                            ins = e.matmul(ps[:, bnk, c0:c0 + 256], lhsT=yT[:, k, :], rhs=wb[:, k, :],
                                           start=(k == 0), stop=(k == 7))
                        return ins
                    sc.op("pe", f, reads=[wkey, "yT"], writes=[("ps", bnk)], join=(i % 2 == 1))
                    yield
                sc.op("dve", lambda e: e.tensor_tensor(out=h[:, tt, :], in0=P67, in1=h[:, tt, :], op=ALU.add),
                      reads=[("ps", 6), ("ps", 7), ("h", tt)], writes=[("h", tt)])
                sq_stat(tt)
                yield

            interleave(stageA(0), None)
            for tt in range(NT):
                interleave(stageB(tt), stageA(tt + 1) if tt + 1 < NT else None)

        def rope(xv, nh, tt, tmp, key):
            sinv = bc_mid(cs_tab[:, 0, tt, :], nh)
            cosv = bc_mid(cs_tab[:, 1, tt, :], nh)
            x1 = xv[:, :, 0:8]
            x2 = xv[:, :, 8:16]
            sc.op("dve", lambda e: e.tensor_tensor(out=tmp[:, 0], in0=x1, in1=cosv, op=ALU.mult),
                  reads=[key, "cs_tab"], writes=[("rt", 0)])
            sc.op("dve", lambda e: e.tensor_tensor(out=tmp[:, 1], in0=x2, in1=sinv, op=ALU.mult),
                  reads=[key, "cs_tab"], writes=[("rt", 1)])
            sc.op("dve", lambda e: e.tensor_tensor(out=tmp[:, 2], in0=x2, in1=cosv, op=ALU.mult),
                  reads=[key, "cs_tab"], writes=[("rt", 2)])
            sc.op("dve", lambda e: e.tensor_tensor(out=tmp[:, 3], in0=x1, in1=sinv, op=ALU.mult),
                  reads=[key, "cs_tab"], writes=[("rt", 3)])
            sc.op("dve", lambda e: e.tensor_tensor(out=x1, in0=tmp[:, 0], in1=tmp[:, 1], op=ALU.subtract),
                  reads=[("rt", 0), ("rt", 1), ("rt", 3)], writes=[key])
            sc.op("dve", lambda e: e.tensor_tensor(out=x2, in0=tmp[:, 2], in1=tmp[:, 3], op=ALU.add),
                  reads=[("rt", 2), ("rt", 3), key], writes=[key])

        def shared_kv(seq):
            ssk = f32v(ph(0, 2), 2)
            lnk = f32v(ph(2, 2), 2)
            rk = f32v(ph(4, 2), 2)
            kn = f32v(ph(8, 128), 128)
            tmp = f32v(ph(136, 64), 64).rearrange("p (a h i) -> p a h i", a=4, h=2)
            kb = bf16v(ph(200, 128), 128)
            finish_stats()
            WS.push([(kvw_b[0], "kvw") for _ in range(NT)])
            knv = kn.rearrange("p (h d) -> p h d", h=2)
            kbv = kb.rearrange("p (h u d) -> p h u d", h=2, u=2)
            for g in range(NG):
                hs = norm_group(g)
                for t4 in range(4):
                    tt = g * 4 + t4
                    wb, wkey = WS.next()

                    def f(e, wb=wb, hs=hs, t4=t4):
                        ins = None
                        for k in range(8):
                            ins = e.matmul(ps[:, 2, 0:256], lhsT=hnT[hs][:, k, t4 * 128:(t4 + 1) * 128],
                                           rhs=wb[:, k, :], start=(k == 0), stop=(k == 7))
                        return ins
                    sc.op("pe", f, reads=[wkey, ("hnT", hs)], writes=[("ps", 2)])
                    for hh in range(2):
                        sc.op("act", lambda e, hh=hh: e.activation(out=junk[:, 0:64], in_=ps[:, 2, hh * 64:(hh + 1) * 64],
                                                                   func=AF.Square, accum_out=ssk[:, hh:hh + 1]),
                              reads=[("ps", 2)], writes=["ssk"], join=(hh == 1))
                    sc.op("act", lambda e, tt=tt: e.activation(out=vsb[:, tt, :, 0:64],
                                                               in_=ps[:, 2, 128:256].rearrange("p (h d) -> p h d", h=2),
                                                               func=AF.Copy),
                          reads=[("ps", 2)], writes=[("vsb", tt)])
                    sc.op("act", lambda e: e.activation(out=lnk, in_=ssk, func=AF.Ln, scale=1.0 / 64, bias=eps_c),
                          reads=["ssk", "cst"], writes=["lnk"])
                    sc.op("act", lambda e: e.activation(out=rk, in_=lnk, func=AF.Exp, scale=-0.5),
                          reads=["lnk"], writes=["rk"])
                    sc.op("dve", lambda e: e.tensor_tensor(out=knv, in0=ps[:, 2, 0:128].rearrange("p (h d) -> p h d", h=2),
                                                           in1=bc_last(rk, 64), op=ALU.mult),
                          reads=[("ps", 2), "rk"], writes=["kn"])
                    sc.op("dve", lambda e: e.tensor_tensor(out=kn, in0=kn, in1=cst[:, C_GK:C_GK + 128], op=ALU.mult),
                          reads=["kn", "cst"], writes=["kn"])
                    rope(knv, 2, tt, tmp, "kn")
                    sc.op("dve", lambda e: e.tensor_copy(out=kbv[:, :, 0, :], in_=knv), reads=["kn"], writes=["kb"])
                    sc.op("dve", lambda e: e.tensor_copy(out=kbv[:, :, 1, :], in_=knv), reads=["kn"], writes=["kb"],
                          join=True)

                    def trk(e):
                        pv = pbank_bf(3)
                        ins = None
                        for hh in range(2):
                            ins = e.transpose(out=pv[:, hh * 128:(hh + 1) * 128], in_=kb[:, hh * 128:(hh + 1) * 128],
                                              identity=ident_bf)
                        return ins
                    sc.op("pe", trk, reads=["kb", "cbf"], writes=[("ps", 3)])
                    sc.op("act", lambda e, tt=tt: e.activation(
                        out=kT[:, :, tt * 128:(tt + 1) * 128],
                        in_=pbank_bf(3)[:, 0:256].rearrange("p (h n) -> p h n", h=2), func=AF.Copy),
                        reads=[("ps", 3)], writes=[("kT", tt)])

        def swa(seq):
            o_ = [0]

            def A(n):
                o = o_[0]
                o_[0] += n
                return ph(o, n)
            gq = f32v(A(1024), 1024)
            sqf = f32v(A(1024), 1024)
            qn = f32v(A(1024), 1024)
            ssq = f32v(A(16), 16)
            lnq = f32v(A(16), 16)
            rq = f32v(A(16), 16)
            tmp = f32v(A(512), 512).rearrange("p (a h i) -> p a h i", a=4, h=16)
            qb = bf16v(A(512), 512)
            qTm = bf16v(A(1024), 1024).rearrange("p (j c t) -> p j c t", j=8, c=2)
            pe_ = [bf16v(A(512), 512).rearrange("p (h t) -> p h t", h=8) for _ in range(4)]
            den = f32v(A(16), 16)
            rden = f32v(A(16), 16)
            ob = bf16v(A(512), 512)
            obT = bf16v(A(512), 512).rearrange("p (k t) -> p k t", k=8)

            wres = [bf16v(A(1024), 1024).rearrange("p (k n) -> p k n", k=8) for _ in range(8)]
            CQ.require("wq")
            CQ.require("awo")
            for bi in range(8):
                src = wq_b[bi] if bi < 4 else awo_b[bi - 4]
                wn = "wq" if bi < 4 else "awo"
                sc.op("sp", lambda e, d=wres[bi], s_=src: e.dma_start(out=d.rearrange("p k n -> p (k n)"), in_=s_),
                      reads=[("scr", wn)], writes=[("wres", bi)], dma=("wres", bi))
            finish_stats()
            sc.op("sp", lambda e: e.dma_start(out=gq, in_=gq_d), writes=["gq"], dma="gq")
            sc.op("pool", lambda e: e.memset(qTm.rearrange("p j c t -> p (j c t)"), 0.0), writes=["qT"])
            qnv = qn.rearrange("p (h d) -> p h d", h=16)
            P01 = pbank(0, 2)
            pcount = 0
            obanks = (0, 1, 6, 7)
            for g in range(NG):
                hs = norm_group(g)
                for t4 in range(4):
                    tt = g * 4 + t4
                    for i in range(4):
                        wb, wkey = wres[i], ("wres", i)
                        b = i // 2
                        c0 = (i % 2) * 256

                        def f(e, wb=wb, b=b, c0=c0, hs=hs, t4=t4):
                            ins = None
                            for k in range(8):
                                ins = e.matmul(ps[:, b, c0:c0 + 256], lhsT=hnT[hs][:, k, t4 * 128:(t4 + 1) * 128],
                                               rhs=wb[:, k, :], start=(k == 0), stop=(k == 7))
                            return ins
                        sc.op("pe", f, reads=[wkey, ("hnT", hs)], writes=[("ps", b)], join=(i % 2 == 1))
                    CQ.pump(4)
                    sc.op("act", lambda e: e.activation(out=sqf, in_=P01, func=AF.Square),
                          reads=[("ps", 0), ("ps", 1)], writes=["sqf"])
                    sc.op("dve", lambda e: e.tensor_reduce(out=ssq, in_=sqf.rearrange("p (h d) -> p h d", h=16),
                                                           axis=AX.X, op=ALU.add), reads=["sqf"], writes=["ssq"])
                    sc.op("act", lambda e: e.activation(out=lnq, in_=ssq, func=AF.Ln, scale=1.0 / 64, bias=eps_c),
                          reads=["ssq", "cst"], writes=["lnq"])
                    sc.op("act", lambda e: e.activation(out=rq, in_=lnq, func=AF.Exp, scale=-0.5),
                          reads=["lnq"], writes=["rq"])
                    sc.op("dve", lambda e: e.tensor_tensor(out=qnv, in0=P01.rearrange("p (h d) -> p h d", h=16),
                                                           in1=bc_last(rq, 64), op=ALU.mult),
                          reads=[("ps", 0), ("ps", 1), "rq"], writes=["qn"])
                    sc.op("dve", lambda e: e.tensor_tensor(out=qn, in0=qn, in1=gq, op=ALU.mult),
                          reads=["qn", "gq"], writes=["qn"])
                    rope(qnv, 16, tt, tmp, "qn")
                    sc.op("dve", lambda e: e.tensor_copy(out=qb, in_=qn), reads=["qn"], writes=["qb"])

                    def trq(e):
                        pv = pbank_bf(2).rearrange("p (k n) -> p k n", k=8)
                        ins = None
                        for k in range(8):
                            ins = e.transpose(out=pv[:, k, :], in_=qb[:, k * 128:(k + 1) * 128], identity=ident_bf)
                        return ins
                    sc.op("pe", trq, reads=["qb", "cbf"], writes=[("ps", 2)])
                    pv2 = pbank_bf(2).rearrange("p (k n) -> p k n", k=8)
                    sc.op("act", lambda e: e.activation(out=qTm[0:64, :, 0, :], in_=pv2[0:64, :, :], func=AF.Copy),
                          reads=[("ps", 2)], writes=["qT"])
                    sc.op("act", lambda e: e.activation(out=qTm[64:128, :, 1, :], in_=pv2[64:128, :, :], func=AF.Copy),
                          reads=[("ps", 2)], writes=["qT"], join=True)
                    if SWA_DBG < 2:
                        continue
                    kts = [tt - 1, tt] if tt > 0 else [tt]
                    for kvh in range(2):
                        slots = []
                        for ki, kt in enumerate(kts):
                            sb = 2 + 2 * (pcount % 2)
                            pslot = (pcount % 2) * 2 + ki
                            slots.append(pslot)

                            def scm(e, kvh=kvh, kt=kt, sb=sb):
                                ins = None
                                for hl in range(8):
                                    hd = kvh * 8 + hl
                                    j, ehalf = hd // 2, hd % 2
                                    ins = e.matmul(ps[:, sb + hl // 4, (hl % 4) * 128:(hl % 4 + 1) * 128],
                                                   lhsT=kT[:, kvh, kt * 128:(kt + 1) * 128],
                                                   rhs=qTm[:, j, ehalf, :], start=True, stop=True)
                                return ins
                            sc.op("pe", scm, reads=[("kT", kt), "qT"], writes=[("ps", sb), ("ps", sb + 1)])
                            sc.op("act", lambda e, sb=sb, pslot=pslot: e.activation(
                                out=pe_[pslot], in_=pbank(sb, 2).rearrange("p (h t) -> p h t", h=8),
                                func=AF.Exp, scale=0.125),
                                reads=[("ps", sb), ("ps", sb + 1)], writes=[("pe", pslot)])
                            mk = maskd_bf if kt == tt else maskp_bf
                            sc.op(MASK_ENG, lambda e, pslot=pslot, mk=mk: e.tensor_tensor(
                                out=pe_[pslot], in0=pe_[pslot], in1=bc_mid(mk, 8), op=ALU.mult),
                                reads=[("pe", pslot), "cbf"], writes=[("pe", pslot)])
                            pcount += 1
                        if SWA_DBG < 3:
                            continue

                        def pv(e, kvh=kvh, kts=tuple(kts), slots=tuple(slots)):
                            ins = None
                            for hl in range(8):
                                hd = kvh * 8 + hl
                                ob_ = obanks[hd // 4]
                                c0 = (hd % 4) * 65
                                for ki, kt in enumerate(kts):
                                    ins = e.matmul(ps[:, ob_, c0:c0 + 65], lhsT=pe_[slots[ki]][:, hl, :],
                                                   rhs=vsb[:, kt, kvh, 0:65], start=(ki == 0),
                                                   stop=(ki == len(kts) - 1))
                            return ins
                        sc.op("pe", pv, reads=[("pe", sl) for sl in slots] + [("vsb", kt) for kt in kts] + ["vones"],
                              writes=[("ps", obanks[kvh * 2]), ("ps", obanks[kvh * 2 + 1])])
                    if SWA_DBG < 4:
                        continue
                    for pr in range(2):
                        b0 = obanks[pr * 2]
                        ov = ps[:, b0:b0 + 2, 0:260].rearrange("p b (h d) -> p b h d", h=4)
                        sc.op("dve", lambda e, ov=ov, pr=pr: e.tensor_tensor(
                            out=den[:, pr * 8:(pr + 1) * 8].rearrange("p (b h) -> p b h", b=2),
                            in0=ov[:, :, :, 64], in1=esink[:, pr * 8:(pr + 1) * 8].rearrange("p (b h) -> p b h", b=2),
                            op=ALU.add), reads=[("ps", b0), ("ps", b0 + 1), "esink"], writes=["den"], join=(pr == 1))
                    sc.op("dve", lambda e: e.reciprocal(out=rden, in_=den), reads=["den"], writes=["rden"])
                    for pr in range(2):
                        b0 = obanks[pr * 2]
                        for bb in range(2):
                            ov = ps[:, b0 + bb, 0:260].rearrange("p (h d) -> p h d", h=4)[:, :, 0:64]
                            hd0 = pr * 8 + bb * 4
                            sc.op("dve", lambda e, ov=ov, hd0=hd0: e.tensor_tensor(
                                out=ob[:, hd0 * 64:(hd0 + 4) * 64].rearrange("p (h d) -> p h d", h=4),
                                in0=ov, in1=bc_last(rden[:, hd0:hd0 + 4], 64), op=ALU.mult),
                                reads=[("ps", b0 + bb), "rden"], writes=["ob"], join=not (pr == 0 and bb == 0))

                    if SWA_DBG < 5:
                        continue

                    def tro(e):
                        pv_ = pbank_bf(2).rearrange("p (k n) -> p k n", k=8)
                        ins = None
                        for k in range(8):
                            ins = e.transpose(out=pv_[:, k, :], in_=ob[:, k * 128:(k + 1) * 128], identity=ident_bf)
                        return ins
                    sc.op("pe", tro, reads=["ob", "cbf"], writes=[("ps", 2)])
                    sc.op("act", lambda e: e.activation(out=obT, in_=pbank_bf(2).rearrange("p (k n) -> p k n", k=8),
                                                        func=AF.Copy), reads=[("ps", 2)], writes=["obT"])
                    for i in range(4):
                        wb, wkey = wres[4 + i], ("wres", 4 + i)
                        b = 4 + i // 2
                        c0 = (i % 2) * 256

                        def f(e, wb=wb, b=b, c0=c0):
                            ins = None
                            for k in range(8):
                                ins = e.matmul(ps[:, b, c0:c0 + 256], lhsT=obT[:, k, :], rhs=wb[:, k, :],
                                               start=(k == 0), stop=(k == 7))
                            return ins
                        sc.op("pe", f, reads=[wkey, "obT"], writes=[("ps", b)], join=(i % 2 == 1))
                    sc.op("dve", lambda e, tt=tt: e.tensor_tensor(out=h[:, tt, :], in0=pbank(4, 2), in1=h[:, tt, :],
                                                                  op=ALU.add),
                          reads=[("ps", 4), ("ps", 5), ("h", tt)], writes=[("h", tt)])
                    sq_stat(tt)

        for seq in range(nseq):
            for tt in range(NT):
                sc.op("sp", lambda e, seq=seq, tt=tt: e.dma_start(out=h[:, tt, :], in_=x[seq, tt * 128:(tt + 1) * 128, :]),
                      writes=[("h", tt)], dma=("ld", tt))
                sq_stat(tt)
            for si, st in enumerate(stages):
                lastp = (si == len(stages) - 1)
                if st.startswith("ffn"):
                    ffn(int(st[3]) * 2 + int(st[4]), lastp, seq)
                elif st == "hgrn":
                    hgrn(seq)
                elif st == "kv":
                    shared_kv(seq)
                elif st == "swa":
                    swa(seq)
                if lastp and not st.startswith("ffn"):
                    for tt in range(NT):
                        store_tile(seq, tt)
                sc.barrier()
        sc.wait_all("sp", stores)

        names = sc.assign()
        sems = {}
        for i, nm in enumerate(names):
            sems[nm] = es.enter_context(nc.semaphore(f"s{i}"))
        block = es.enter_context(nc.Block())

        @block.sync
        def _(e):
            sc.emit("sp", e, sems)

        @block.tensor
        def _(e):
            sc.emit("pe", e, sems)

        @block.scalar
        def _(e):
            sc.emit("act", e, sems)

        @block.vector
        def _(e):
            sc.emit("dve", e, sems)

        @block.gpsimd
        def _(e):
            sc.emit("pool", e, sems)
    return nc


def make_consts(ffn_norm_g, mix_norm_g, kv_norm_g, hgrn_onorm_g, attn_sinks, k_norm_g):
    c = np.zeros((128, NCST), np.float32)
    gains = [ffn_norm_g[0, 0], ffn_norm_g[0, 1], ffn_norm_g[1, 0], ffn_norm_g[1, 1],
             mix_norm_g[0], mix_norm_g[1], kv_norm_g]
    for gi, gv in enumerate(gains):
        c[:, C_G + gi * 8:C_G + gi * 8 + 8] = np.asarray(gv, np.float32).reshape(8, 128).T
    c[:, C_ONG] = np.asarray(hgrn_onorm_g, np.float32).reshape(128)
    idx = np.arange(128)
    s_ = idx[:, None]
    t_ = idx[None, :]
    c[:, C_IDENT:C_IDENT + 128] = (s_ == t_)
    same = (s_ // 64) == (t_ // 64)
    c[:, C_UINC:C_UINC + 128] = same & (s_ <= t_)
    c[:, C_UEXC:C_UEXC + 128] = same & (s_ > t_)
    c[:, C_CIND] = idx < 64
    c[:, C_CIND + 1] = idx >= 64
    c[:, C_MD:C_MD + 128] = (s_ <= t_)
    c[:, C_MP:C_MP + 128] = (s_ > t_)
    c[:, C_SINK:C_SINK + 16] = np.asarray(attn_sinks, np.float32).reshape(1, 16)
    c[:, C_EPS] = EPS
    c[:, C_ONE] = 1.0
    c[:, C_GK:C_GK + 128] = np.tile(np.asarray(k_norm_g, np.float32).reshape(64), 2)[None, :]
    return c


_PROG_CACHE = {}


def kernel(x, ffn_norm_g, ffn_w_gate_up, ffn_w_down, mix_norm_g, hgrn_w_in, hgrn_lb_logits,
           hgrn_onorm_g, hgrn_w_out, kv_norm_g, kv_w, k_norm_g, attn_w_q, q_norm_g,
           attn_sinks, attn_w_out, _stages=ALL_STAGES, _ncores=NCORES, _nseq=NSEQ):
    f = lambda a: np.ascontiguousarray(np.asarray(a, dtype=np.float32))
    x = f(x)
    key = (tuple(_stages), _nseq)
    if key not in _PROG_CACHE:
        _PROG_CACHE[key] = build_program(_nseq, tuple(_stages))
    nc = _PROG_CACHE[key]
    cst = make_consts(f(ffn_norm_g), f(mix_norm_g), f(kv_norm_g), f(hgrn_onorm_g)[0], f(attn_sinks)[0], f(k_norm_g))
    lbl = np.ascontiguousarray(np.broadcast_to(f(hgrn_lb_logits)[None, :, :], (128, 2, D)))
    gqb = np.ascontiguousarray(np.broadcast_to(np.tile(f(q_norm_g)[0], 16)[None, :], (128, D)))
    shared = {
        "ffn_w_gate_up": f(ffn_w_gate_up), "ffn_w_down": f(ffn_w_down),
        "hgrn_w_in": f(hgrn_w_in)[0], "hgrn_w_out": f(hgrn_w_out)[0], "kv_w": f(kv_w),
        "attn_w_q": f(attn_w_q)[0], "attn_w_out": f(attn_w_out)[0],
        "cst": cst, "lbl": lbl, "gqb": gqb,
    }
    in_maps = []
    for c in range(_ncores):
        m = dict(shared)
        m["x"] = np.ascontiguousarray(x[c * _nseq:(c + 1) * _nseq])
        in_maps.append(m)
    res = run_bass_kernel_spmd(nc, in_maps, core_ids=list(range(_ncores)))
    return np.concatenate([r["out"] for r in res.results], axis=0)
```

```python
import math
from contextlib import ExitStack

import numpy as np
import concourse.bass as bass
import concourse.mybir as mybir
from concourse.ap import AP
from concourse.bass_utils import run_bass_kernel_spmd

F32 = mybir.dt.float32
BF16 = mybir.dt.bfloat16
I32 = mybir.dt.int32
AF = mybir.ActivationFunctionType
ALU = mybir.AluOpType
AX = mybir.AxisListType

NCORES = 8
NSEQ = 2
S = 2048
D = 1024
DFF = 2816
NFC = DFF // 128
NT = S // 128
NG = NT // 4
EPS = 1e-6
DEBUG = False
MERGE_KV = True
STRICT = True
MASK_ENG = "dve"
SWA_DBG = 5
ALL_STAGES = ("ffn00", "hgrn", "ffn01", "kv", "ffn10", "swa", "ffn11")

C_G = 0
C_ONG = 56
C_IDENT = 64
C_UINC = 192
C_UEXC = 320
C_CIND = 448
C_MD = 456
C_MP = 584
C_SINK = 712
C_EPS = 728
C_ONE = 729
C_GK = 736
NCST = 864


def bc_mid(ap, n):
    a = ap.ap
    return AP(ap.tensor, ap.offset, [list(a[0]), [0, n], list(a[1])])


def bc_last(ap, n):
    a = ap.ap
    return AP(ap.tensor, ap.offset, [list(a[0]), list(a[1]), [0, n]])


class Op:
    __slots__ = ("eng", "fn", "deps", "dma", "token", "signal", "idx")

    def __init__(self, eng, fn, dma):
        self.eng = eng
        self.fn = fn
        self.deps = []
        self.dma = dma
        self.token = None
        self.signal = False


class Sched:
    COMPUTE = ("pe", "act", "dve", "pool")

    def __init__(self):
        self.ops = []
        self.state = {}
        self.last = {}

    def op(self, eng, fn, reads=(), writes=(), dma=None, join=False):
        o = Op(eng, fn, dma)
        raw, other = [], []
        for k in reads:
            st = self.state.get(k)
            if st:
                raw.extend(st[0])
        for k in writes:
            st = self.state.get(k)
            if st is None:
                st = self.state[k] = [[], {}, [], [], []]
            if join:
                other.extend(st[3])
                other.extend(st[4])
            else:
                other.extend(st[0])
            other.extend(st[1].values())
            other.extend(st[2])
        seen = set()
        for d in raw:
            if id(d) not in seen:
                seen.add(id(d))
                o.deps.append(d)
        for d in other:
            if id(d) in seen:
                continue
            if (not STRICT) and d.dma is None and dma is None and d.eng == eng:
                continue
            seen.add(id(d))
            o.deps.append(d)
        for d in o.deps:
            d.signal = True
        for k in reads:
            st = self.state.get(k)
            if st is None:
                st = self.state[k] = [[], {}, [], [], []]
            if dma is None:
                st[1][eng] = o
            else:
                st[2].append(o)
        for k in writes:
            st = self.state[k]
            cur_readers = list(st[1].values()) + list(st[2])
            if join:
                st[0].append(o)
                st[4] = st[4] + cur_readers
            else:
                st[3] = st[0]
                st[4] = cur_readers
                st[0] = [o]
            st[1] = {}
            st[2] = []
        self.ops.append(o)
        if dma is None and fn is not None:
            self.last[eng] = o
        return o

    def barrier(self, engines=("pe", "act", "dve", "pool", "sp")):
        lasts = [self.last[e] for e in self.COMPUTE if e in self.last]
        for e in engines:
            o = Op(e, None, None)
            o.deps = [d for d in lasts if d.eng != e]
            for d in o.deps:
                d.signal = True
            self.ops.append(o)

    def wait_all(self, eng, ops):
        o = Op(eng, None, None)
        o.deps = list(ops)
        for d in o.deps:
            d.signal = True
        self.ops.append(o)

    def assign(self):
        EPOCH = 30000
        cnt = {}
        names = []
        for o in self.ops:
            if o.fn is None or not (o.signal or o.dma is not None):
                continue
            key = ("dma", o.dma) if o.dma is not None else ("eng", o.eng)
            c = cnt.get(key, 0)
            inc = 16 if o.dma is not None else 1
            c += inc
            cnt[key] = c
            ep, val = divmod(c - inc, EPOCH)
            name = (key, ep)
            if name not in names:
                names.append(name)
            o.token = (name, val + inc)
        return names

    def emit(self, eng_name, eng, sems):
        waited = {}
        for o in self.ops:
            if o.eng != eng_name:
                continue
            for d in o.deps:
                if d.token is None:
                    continue
                nm, val = d.token
                if waited.get(nm, 0) < val:
                    eng.wait_ge(sems[nm], val)
                    waited[nm] = val
            if o.fn is None:
                continue
            if o.dma is not None and o.token is not None:
                nm, val = o.token
                if val > 16 and waited.get(nm, 0) < val - 16:
                    eng.wait_ge(sems[nm], val - 16)
                    waited[nm] = val - 16
            ins = o.fn(eng)
            if o.token is not None:
                nm, val = o.token
                ins.then_inc(sems[nm], 16 if o.dma is not None else 1)


def build_program(nseq=NSEQ, stages=ALL_STAGES):
    nc = bass.Bass("TRN2", target_bir_lowering=False)
    sc = Sched()

    def dram_in(name, shape, dt=F32):
        return nc.dram_tensor(name, list(shape), dt, kind="ExternalInput").ap()

    def dram_scr(name, shape, dt=BF16):
        return nc.dram_tensor(name, list(shape), dt, kind="Internal").ap()

    x = dram_in("x", [nseq, S, D])
    wgu = dram_in("ffn_w_gate_up", [2, 2, D, 2 * DFF])
    wdn = dram_in("ffn_w_down", [2, 2, DFF, D])
    w_in = dram_in("hgrn_w_in", [D, 4 * D])
    hwo = dram_in("hgrn_w_out", [D, D])
    kvw = dram_in("kv_w", [D, 256])
    wq = dram_in("attn_w_q", [D, D])
    awo = dram_in("attn_w_out", [D, D])
    cst_d = dram_in("cst", [128, NCST])
    lbl_d = dram_in("lbl", [128, 2, D])
    gq_d = dram_in("gqb", [128, D])
    out = nc.dram_tensor("out", [nseq, S, D], F32, kind="ExternalOutput").ap()

    wgu_b = [dram_scr(f"wgu_b{i}", [NFC, 128, 2048]) for i in range(4)]
    wd_b = [dram_scr(f"wd_b{i}", [128, NFC * D]) for i in range(4)]
    win_b = dram_scr("win_b", [16, 128, 2048])
    hwo_b = dram_scr("hwo_b", [4, 128, 2048])
    kvw_b = dram_scr("kvw_b", [1, 128, 2048])
    wq_b = dram_scr("wq_b", [4, 128, 2048])
    awo_b = dram_scr("awo_b", [4, 128, 2048])

    es = ExitStack()
    with es:
        NW = 53200
        arena = es.enter_context(nc.sbuf_tensor("arena", [128, NW], F32))
        ps = es.enter_context(nc.psum_tensor("ps", [128, 8, 512], F32))
        cur = [0]

        def alloc(nwords):
            o = cur[0]
            cur[0] += nwords
            assert cur[0] <= NW, cur[0]
            return o

        def f32v(off, n):
            return arena[:, off:off + n]

        def bf16v(off, nwords):
            return arena[:, off:off + nwords].bitcast(BF16)

        h_off = alloc(NT * D)
        h = f32v(h_off, NT * D).rearrange("p (t d) -> p t d", t=NT)
        cst = f32v(alloc(NCST), NCST)
        hnT = [bf16v(alloc(2048), 2048).rearrange("p (k n) -> p k n", k=8) for _ in range(2)]
        wblk = [bf16v(alloc(1024), 1024).rearrange("p (k n) -> p k n", k=8) for _ in range(4)]
        xn = [bf16v(alloc(512), 512) for _ in range(2)]
        junk = bf16v(alloc(512), 512)
        ssum = f32v(alloc(16), 16)
        lnv = f32v(alloc(16), 16)
        rstd = f32v(alloc(16), 16)
        cbf = bf16v(alloc(256), 256)
        ident_bf = cbf[:, 0:128]
        uinc_bf = cbf[:, 128:256]
        maskd_bf = cbf[:, 256:384]
        maskp_bf = cbf[:, 384:512]
        cs_tab = f32v(alloc(256), 256).rearrange("p (a t i) -> p a t i", a=2, t=NT)
        kT = bf16v(alloc(2048), 2048).rearrange("p (h n) -> p h n", h=2)
        vsb = bf16v(alloc(1056), 1056).rearrange("p (t h d) -> p t h d", t=NT, h=2)
        esink = f32v(alloc(16), 16)
        cq_sf = [f32v(alloc(1024), 1024) for _ in range(2)]
        cq_sb = [bf16v(alloc(512), 512) for _ in range(2)]
        PH = alloc(0)
        PH_WORDS = NW - PH

        def ph(off, n):
            assert off + n <= PH_WORDS, (off, n, PH_WORDS)
            return PH + off

        ident_f = cst[:, C_IDENT:C_IDENT + 128]
        uinc_f = cst[:, C_UINC:C_UINC + 128]
        uexc_f = cst[:, C_UEXC:C_UEXC + 128]
        cind_f = cst[:, C_CIND:C_CIND + 2]
        eps_c = cst[:, C_EPS:C_EPS + 1]
        one_c = cst[:, C_ONE:C_ONE + 1]

        def pbank(b, n=1):
            return ps[:, b:b + n, :].rearrange("p b n -> p (b n)")

        def pbank_bf(b):
            return ps[:, b, :].bitcast(BF16)

        class WStream:
            def __init__(self, slots, tag):
                self.q = []
                self.loaded = 0
                self.cur = 0
                self.slots = slots
                self.tag = tag
                self.n = len(slots)

            def push(self, items):
                for wn in sorted({w_ for (_, w_) in items}):
                    CQ.require(wn)
                self.q.extend(items)

            def _load(self, i):
                slot = i % self.n
                src, wname = self.q[i]
                dst = self.slots[slot]
                sc.op("sp", lambda e, d=dst, s=src: e.dma_start(out=d.rearrange("p k n -> p (k n)"), in_=s),
                      reads=[("scr", wname)], writes=[(self.tag, slot)], dma=(self.tag, slot))

            def next(self):
                i = self.cur
                while self.loaded < min(len(self.q), i + self.n):
                    self._load(self.loaded)
                    self.loaded += 1
                self.cur += 1
                return self.slots[i % self.n], (self.tag, i % self.n)

        WS = WStream(wblk, "wblk")

        sc.op("sp", lambda e: e.dma_start(out=cst, in_=cst_d), writes=["cst"], dma="cst")
        sc.op("dve", lambda e: e.tensor_copy(out=ident_bf, in_=ident_f), reads=["cst"], writes=["cbf"])
        sc.op("dve", lambda e: e.tensor_copy(out=uinc_bf, in_=uinc_f), reads=["cst"], writes=["cbf"], join=True)
        sc.op("dve", lambda e: e.tensor_copy(out=maskd_bf, in_=cst[:, C_MD:C_MD + 128]), reads=["cst"],
              writes=["cbf"], join=True)
        sc.op("dve", lambda e: e.tensor_copy(out=maskp_bf, in_=cst[:, C_MP:C_MP + 128]), reads=["cst"],
              writes=["cbf"], join=True)

        class ConvQueue:
            def __init__(self):
                self.jobs = []
                self.nrec = 0
                self.castno = 0
                self.sf = cq_sf
                self.sb = cq_sb

            def add(self, wname, loads, casts, stores):
                self.jobs.append((wname, loads, casts, stores))

            def _L(self, j):
                slot = j % 2
                _, loads, _, _ = self.jobs[j]
                for li, (dview, src) in enumerate(loads):
                    sc.op("sp", lambda e, d=dview(self.sf[slot]), s=src: e.dma_start(out=d, in_=s),
                          writes=[("sf", slot)], dma=("sf", slot, li), join=(li > 0))

            def _C(self, j):
                slot = j % 2
                _, _, casts, _ = self.jobs[j]
                for ci, (oview, iview, gc) in enumerate(casts):
                    o_ap, i_ap = oview(self.sb[slot]), iview(self.sf[slot])
                    n = self.castno
                    self.castno += 1
                    if n % 2 == 0:
                        if gc is None:
                            fn = lambda e, o_ap=o_ap, i_ap=i_ap: e.tensor_copy(out=o_ap, in_=i_ap)
                        else:
                            fn = lambda e, o_ap=o_ap, i_ap=i_ap, gc=gc: e.tensor_scalar(
                                out=o_ap, in0=i_ap, scalar1=gc, scalar2=None, op0=ALU.mult)
                        eng = "dve"
                    else:
                        if gc is None:
                            fn = lambda e, o_ap=o_ap, i_ap=i_ap: e.activation(out=o_ap, in_=i_ap, func=AF.Copy)
                        else:
                            fn = lambda e, o_ap=o_ap, i_ap=i_ap, gc=gc: e.activation(out=o_ap, in_=i_ap,
                                                                                    func=AF.Copy, scale=gc)
                        eng = "act"
                    sc.op(eng, fn, reads=[("sf", slot), "cst"], writes=[("sb", slot)], join=(ci > 0))

            def _S(self, j):
                slot = j % 2
                wname, _, _, stores = self.jobs[j]
                for si, (dst, sview) in enumerate(stores):
                    sc.op("sp", lambda e, d=dst, s=sview(self.sb[slot]): e.dma_start(out=d, in_=s),
                          reads=[("sb", slot)], writes=[("scr", wname)], dma=("sbst", slot, si), join=True)

            def pump(self, n=1):
                for _ in range(n):
                    k = self.nrec
                    nj = len(self.jobs)
                    if k > nj + 1:
                        return
                    if k == 0 and nj > 0:
                        self._L(0)
                    if k + 1 < nj:
                        self._L(k + 1)
                    if k < nj:
                        self._C(k)
                    if 1 <= k <= nj:
                        self._S(k - 1)
                    self.nrec += 1

            def require(self, wname):
                last = max([i for i, jb in enumerate(self.jobs) if jb[0] == wname], default=-1)
                forced = 0
                while self.nrec < last + 2:
                    self.pump()
                    forced += 1
                if forced > 2 and DEBUG:
                    print("CQ.require", wname, "forced pumps", forced, "of", len(self.jobs))

        CQ = ConvQueue()

        def gcol(gi, kc):
            return cst[:, C_G + gi * 8 + kc:C_G + gi * 8 + kc + 1]

        def add_ffn_jobs(i4):
            l, j = divmod(i4, 2)
            W = wgu[l, j].rearrange("(k p) n -> p k n", p=128)
            wname = f"ffn{i4}"
            for fc in range(NFC):
                for k0 in (0, 4):
                    loads = [
                        (lambda t: t.rearrange("p (k u c) -> p k u c", k=4, u=2)[:, :, 0, :],
                         W[:, k0:k0 + 4, fc * 128:(fc + 1) * 128]),
                        (lambda t: t.rearrange("p (k u c) -> p k u c", k=4, u=2)[:, :, 1, :],
                         W[:, k0:k0 + 4, DFF + fc * 128:DFF + (fc + 1) * 128]),
                    ]
                    casts = [(lambda t, kk=kk: t[:, kk * 256:(kk + 1) * 256],
                              lambda t, kk=kk: t[:, kk * 256:(kk + 1) * 256], gcol(i4, k0 + kk)) for kk in range(4)]
                    stores = [(wgu_b[i4][fc][:, k0 * 256:(k0 + 4) * 256], lambda t: t)]
                    CQ.add(wname, loads, casts, stores)
            Wd = wdn[l, j].rearrange("(f p) m -> p f m", p=128)
            for f0 in range(NFC):
                loads = [(lambda t: t, Wd[:, f0, :])]
                casts = [(lambda t: t, lambda t: t, None)]
                stores = [(wd_b[i4][:, f0 * D:(f0 + 1) * D], lambda t: t)]
                CQ.add(wname, loads, casts, stores)

        def add_rows_jobs(wname, Wsrc, ncols, dst, gi, ong=False):
            Wv = Wsrc.rearrange("(k p) n -> p k n", p=128)
            dall = dst.rearrange("b p (k c) -> p k b c", k=8)
            if ncols >= 1024:
                nq = ncols // 1024
                for kc in range(8):
                    g_ = cst[:, C_ONG:C_ONG + 1] if ong else (None if gi is None else gcol(gi, kc))
                    for hh in range(nq):
                        loads = [(lambda t: t, Wv[:, kc, hh * 1024:(hh + 1) * 1024])]
                        casts = [(lambda t: t, lambda t: t, g_)]
                        stores = [(dall[:, kc, hh * 4:(hh + 1) * 4, :], lambda t: t.rearrange("p (b c) -> p b c", b=4))]
                        CQ.add(wname, loads, casts, stores)
            else:
                kper = 1024 // ncols
                ncb = ncols // 256
                for k0 in range(0, 8, kper):
                    loads = [(lambda t: t.rearrange("p (k n) -> p k n", k=kper), Wv[:, k0:k0 + kper, :])]
                    casts = []
                    for kk in range(kper):
                        g_ = cst[:, C_ONG:C_ONG + 1] if ong else (None if gi is None else gcol(gi, k0 + kk))
                        casts.append((lambda t, kk=kk: t[:, kk * ncols:(kk + 1) * ncols],
                                      lambda t, kk=kk: t[:, kk * ncols:(kk + 1) * ncols], g_))
                    stores = [(dall[:, k0 + kk, :, :],
                               lambda t, kk=kk: t[:, kk * ncols:(kk + 1) * ncols].rearrange("p (b c) -> p b c", b=ncb))
                              for kk in range(kper)]
                    CQ.add(wname, loads, casts, stores)

        for st in stages:
            if st.startswith("ffn"):
                add_ffn_jobs(int(st[3]) * 2 + int(st[4]))
            elif st == "hgrn":
                add_rows_jobs("win", w_in, 4096, win_b, 4)
                add_rows_jobs("hwo", hwo, 1024, hwo_b, None, ong=True)
            elif st == "kv":
                add_rows_jobs("kvw", kvw, 256, kvw_b, 6)
            elif st == "swa":
                add_rows_jobs("wq", wq, 1024, wq_b, 5)
                add_rows_jobs("awo", awo, 1024, awo_b, None)

        def rope_tables():
            pos_i = arena[:, PH:PH + 16].bitcast(I32)
            pos_f = f32v(ph(16, 16), 16)
            ang = f32v(ph(32, 256), 256).rearrange("p (a t i) -> p a t i", a=2, t=NT)
            kf = f32v(ph(288, 256), 256)
            ki = arena[:, PH + 544:PH + 800].bitcast(I32)
            red = f32v(ph(800, 256), 256)
            sc.op("pool", lambda e: e.iota(pos_i, pattern=[[128, 16]], base=0, channel_multiplier=1),
                  writes=["pos_i"])
            sc.op("dve", lambda e: e.tensor_copy(out=pos_f, in_=pos_i), reads=["pos_i"], writes=["pos_f"])
            first = True
            for i in range(8):
                fr = float(np.float32(500000.0) ** np.float32(-i * (2.0 / 16.0)))
                sc.op("dve", lambda e, i=i, fr=fr: e.tensor_scalar(out=ang[:, 0, :, i], in0=pos_f, scalar1=fr,
                                                                   scalar2=None, op0=ALU.mult),
                      reads=["pos_f"], writes=["ang0"], join=not first)
                first = False
            sc.op("dve", lambda e: e.tensor_scalar(out=ang[:, 1, :, :], in0=ang[:, 0, :, :], scalar1=math.pi / 2,
                                                   scalar2=None, op0=ALU.add), reads=["ang0"], writes=["ang1"])
            angf = f32v(ph(32, 256), 256)
            sc.op("dve", lambda e: e.tensor_scalar(out=kf, in0=angf, scalar1=1.0 / (2 * math.pi), scalar2=None,
                                                   op0=ALU.mult), reads=["ang0", "ang1"], writes=["kf"])
            sc.op("dve", lambda e: e.tensor_copy(out=ki, in_=kf), reads=["kf"], writes=["ki"])
            sc.op("dve", lambda e: e.tensor_copy(out=kf, in_=ki), reads=["ki"], writes=["kf"])
            sc.op("dve", lambda e: e.scalar_tensor_tensor(out=red, in0=kf, scalar=-2 * math.pi, in1=angf,
                                                          op0=ALU.mult, op1=ALU.add),
                  reads=["kf", "ang0", "ang1"], writes=["red"])
            wr = f32v(ph(1056, 256), 256)
            sc.op("dve", lambda e: e.tensor_scalar(out=wr, in0=red, scalar1=math.pi, scalar2=2 * math.pi,
                                                   op0=ALU.is_gt, op1=ALU.mult), reads=["red"], writes=["wr"])
            sc.op("dve", lambda e: e.tensor_tensor(out=red, in0=red, in1=wr, op=ALU.subtract),
                  reads=["red", "wr"], writes=["red"])
            sc.op("dve", lambda e: e.tensor_scalar(out=wr, in0=red, scalar1=-math.pi, scalar2=2 * math.pi,
                                                   op0=ALU.is_lt, op1=ALU.mult), reads=["red"], writes=["wr"])
            sc.op("dve", lambda e: e.tensor_tensor(out=red, in0=red, in1=wr, op=ALU.add),
                  reads=["red", "wr"], writes=["red"])
            sc.op("dve", lambda e: e.tensor_scalar(out=red, in0=red, scalar1=-math.pi, scalar2=math.pi,
                                                   op0=ALU.max, op1=ALU.min), reads=["red"], writes=["red"])
            sc.op("act", lambda e: e.activation(out=cs_tab.rearrange("p a t i -> p (a t i)"), in_=red, func=AF.Sin),
                  reads=["red"], writes=["cs_tab"])

        need_attn = ("kv" in stages) or ("swa" in stages)
        if need_attn:
            rope_tables()
            sc.op("act", lambda e: e.activation(out=esink, in_=cst[:, C_SINK:C_SINK + 16], func=AF.Exp),
                  reads=["cst"], writes=["esink"])
            sc.op("pool", lambda e: e.memset(vsb[:, :, :, 64:66], 1.0), writes=["vones"])
            sc.barrier()

        def sq_stat(tt):
            sc.op("act", lambda e: e.activation(out=junk, in_=h[:, tt, :], func=AF.Square,
                                                accum_out=ssum[:, tt:tt + 1]),
                  reads=[("h", tt)], writes=[("ss", tt)])

        def finish_stats():
            sc.op("act", lambda e: e.activation(out=lnv, in_=ssum, func=AF.Ln, scale=1.0 / D, bias=eps_c),
                  reads=[("ss", t) for t in range(NT)] + ["cst"], writes=["lnv"])
            sc.op("act", lambda e: e.activation(out=rstd, in_=lnv, func=AF.Exp, scale=-0.5),
                  reads=["lnv"], writes=["rstd"])

        tcount = [0]
        hcount = [0]

        def norm_group(g):
            hs = hcount[0] % 2
            hcount[0] += 1
            for t4 in range(4):
                tt = g * 4 + t4
                xs = tcount[0] % 2
                tb = tcount[0] % 2
                tcount[0] += 1
                sc.op("dve", lambda e, tt=tt, xs=xs: e.tensor_scalar(out=xn[xs], in0=h[:, tt, :],
                                                                     scalar1=rstd[:, tt:tt + 1], scalar2=None,
                                                                     op0=ALU.mult),
                      reads=[("h", tt), "rstd"], writes=[("xn", xs)])

                def tr(e, xs=xs, tb=tb):
                    pv = pbank_bf(tb).rearrange("p (k n) -> p k n", k=8)
                    ins = None
                    for k in range(8):
                        ins = e.transpose(out=pv[:, k, :], in_=xn[xs][:, k * 128:(k + 1) * 128], identity=ident_bf)
                    return ins
                sc.op("pe", tr, reads=[("xn", xs), "cbf"], writes=[("ps", tb)])
                sc.op("act", lambda e, hs=hs, t4=t4, tb=tb: e.activation(
                    out=hnT[hs][:, :, t4 * 128:(t4 + 1) * 128],
                    in_=pbank_bf(tb).rearrange("p (k n) -> p k n", k=8), func=AF.Copy),
                    reads=[("ps", tb)], writes=[("hnT", hs)], join=(t4 > 0))
            return hs

        def ffn(i4, last_phase, seq):
            wd_sb = bf16v(ph(0, 11264), 11264).rearrange("p (f m) -> p f m", f=NFC)
            actT = bf16v(ph(11264, 5632), 5632).rearrange("p (f n) -> p f n", f=NFC)
            sil = [f32v(ph(16896 + i * 512, 512), 512) for i in range(2)]
            finish_stats()
            do_kv = (MERGE_KV and i4 == 2 and "kv" in stages)
            if do_kv:
                kvw_sb = bf16v(ph(17920, 1024), 1024).rearrange("p (k n) -> p k n", k=8)
                CQ.require("kvw")
                sc.op("sp", lambda e: e.dma_start(out=kvw_sb.rearrange("p k n -> p (k n)"), in_=kvw_b[0]),
                      reads=[("scr", "kvw")], writes=["kvw_sb"], dma="kvw_sb")
            WS.push([(wgu_b[i4][fc], f"ffn{i4}") for _ in range(NG) for fc in range(NFC)])
            gucount = 0
            dcount = 0

            def kv_group_gen(hs, g):
                for t4 in range(4):
                    yield from kv_tile_gen(hs, t4, g * 4 + t4, kvw_sb, "kvw_sb", 18944, 6, 7)

            for g in range(NG):
                hs = norm_group(g)
                kvg = kv_group_gen(hs, g) if do_kv else iter(())
                for fc in range(NFC):
                    for _ in range(4):
                        next(kvg, None)
                    wb, wkey = WS.next()
                    if fc == 0:
                        for half in range(2):
                            f0, f1 = (0, 11) if half == 0 else (11, NFC)
                            sc.op("sp", lambda e, f0=f0, f1=f1: e.dma_start(
                                out=wd_sb[:, f0:f1, :].rearrange("p f m -> p (f m)"),
                                in_=wd_b[i4][:, f0 * D:f1 * D]),
                                reads=[("scr", f"ffn{i4}")], writes=["wd"], dma=("wd", half), join=(half == 1))
                    CQ.pump()
                    gs = gucount % 2
                    gucount += 1
                    bg, bu = 2 + gs, 4 + gs

                    def gu(e, wb=wb, hs=hs, bg=bg, bu=bu):
                        ins = None
                        for k in range(8):
                            ins = e.matmul(pbank(bg), lhsT=wb[:, k, 0:128], rhs=hnT[hs][:, k, :],
                                           start=(k == 0), stop=(k == 7))
                        for k in range(8):
                            ins = e.matmul(pbank(bu), lhsT=wb[:, k, 128:256], rhs=hnT[hs][:, k, :],
                                           start=(k == 0), stop=(k == 7))
                        return ins
                    sc.op("pe", gu, reads=[wkey, ("hnT", hs)], writes=[("ps", bg), ("ps", bu)])
                    sc.op("act", lambda e, gs=gs, bg=bg: e.activation(out=sil[gs], in_=pbank(bg), func=AF.Silu),
                          reads=[("ps", bg)], writes=[("sil", gs)])
                    sc.op("dve", lambda e, gs=gs, bu=bu, fc=fc: e.tensor_tensor(
                        out=actT[:, fc, :], in0=pbank(bu), in1=sil[gs], op=ALU.mult),
                        reads=[("ps", bu), ("sil", gs)], writes=[("actT", fc)])
                for _ in kvg:
                    pass
                for t4 in range(4):
                    tt = g * 4 + t4
                    for half in range(2):
                        db = 6 + dcount % 2
                        dcount += 1

                        def dn(e, t4=t4, half=half, db=db):
                            ins = None
                            for fc in range(NFC):
                                ins = e.matmul(pbank(db), lhsT=actT[:, fc, t4 * 128:(t4 + 1) * 128],
                                               rhs=wd_sb[:, fc, half * 512:(half + 1) * 512],
                                               start=(fc == 0), stop=(fc == NFC - 1))
                            return ins
                        sc.op("pe", dn, reads=[("actT", fc) for fc in range(NFC)] + ["wd"], writes=[("ps", db)])
                        hv = h[:, tt, half * 512:(half + 1) * 512]
                        sc.op("dve", lambda e, hv=hv, db=db: e.scalar_tensor_tensor(
                            out=hv, in0=pbank(db), scalar=0.5, in1=hv, op0=ALU.mult, op1=ALU.add),
                            reads=[("ps", db), ("h", tt)], writes=[("h", tt)])
                    if last_phase:
                        store_tile(seq, tt)
                    else:
                        sq_stat(tt)

        stores = []

        def store_tile(seq, tt):
            o = sc.op("sp", lambda e: e.dma_start(out=out[seq, tt * 128:(tt + 1) * 128, :], in_=h[:, tt, :]),
                      reads=[("h", tt)], writes=[], dma=("st", tt))
            stores.append(o)

        def interleave(g1, g2):
            gens = [g for g in (g1, g2) if g is not None]
            while gens:
                for g in list(gens):
                    try:
                        next(g)
                    except StopIteration:
                        gens.remove(g)

        def hgrn(seq):
            o_ = [0]

            def A_(n):
                o = o_[0]
                o_[0] += n
                return ph(o, n)
            oml = f32v(A_(1024), 1024)
            kk = f32v(A_(1024), 1024)
            lf = f32v(A_(1024), 1024)
            e_off = A_(2048)
            E = [f32v(e_off + i * 1024, 1024) for i in range(2)]
            lbl = f32v(e_off, 2048).rearrange("p (a n) -> p a n", a=2)
            qs = lf
            osq = f32v(A_(1024), 1024)
            q_e = bf16v(A_(512), 512)
            k_e = bf16v(A_(512), 512)
            q_eTm = [bf16v(A_(1024), 1024).rearrange("p (h c t) -> p h c t", h=8, c=2) for _ in range(2)]
            k_eT = [bf16v(A_(512), 512).rearrange("p (h t) -> p h t", h=8) for _ in range(2)]
            k_d = [bf16v(A_(512), 512) for _ in range(2)]
            vv = [bf16v(A_(512), 512) for _ in range(2)]
            gate = [bf16v(A_(512), 512) for _ in range(2)]
            dec = [f32v(A_(16), 16) for _ in range(2)]
            scT = bf16v(A_(512), 512).rearrange("p (h t) -> p h t", h=8)
            S32 = f32v(A_(1024), 1024)
            Sbf = [bf16v(A_(512), 512) for _ in range(2)]
            ssq = f32v(A_(8), 8)
            lno = f32v(A_(8), 8)
            rso = f32v(A_(8), 8)
            yb = bf16v(A_(512), 512)
            yT = bf16v(A_(512), 512).rearrange("p (h t) -> p h t", h=8)
            wb2 = [bf16v(A_(1024), 1024).rearrange("p (k n) -> p k n", k=8) for _ in range(2)]
            WSB = WStream(wb2, "wb2")

            finish_stats()
            WS.push([(win_b[cb], "win") for tt in range(NT)
                     for cb in (4, 5, 6, 7, 0, 1, 2, 3, 8, 9, 10, 11, 12, 13, 14, 15)])
            WSB.push([(hwo_b[cb], "hwo") for tt in range(NT) for cb in range(4)])

            sc.op("sp", lambda e: e.dma_start(out=lbl.rearrange("p a n -> p (a n)"),
                                              in_=lbl_d.rearrange("p a n -> p (a n)")),
                  writes=["lbl"], dma="lbl")
            sc.op("dve", lambda e: e.tensor_tensor(out=kk, in0=lbl[:, 1, :], in1=lbl[:, 0, :], op=ALU.subtract),
                  reads=["lbl"], writes=["kk"])
            sc.op("act", lambda e: e.activation(out=oml, in_=kk, func=AF.Sigmoid), reads=["kk"], writes=["oml"])
            sc.barrier(("act", "dve"))
            sc.op("pool", lambda e: e.memset(S32, 0.0), writes=[("S32", hh) for hh in range(8)])
            sc.op("pool", lambda e: e.memset(Sbf[0], 0.0), writes=[("Sbf", 0)])
            for p_ in range(2):
                sc.op("pool", lambda e, p_=p_: e.memset(q_eTm[p_].rearrange("p h c t -> p (h c t)"), 0.0),
                      writes=[("q_eTm", p_)])

            P01 = pbank(0, 2)
            P23 = pbank(2, 2)
            P45 = pbank(4, 2)
            P67 = pbank(6, 2)
            hs_of = {}

            def proj(hs, t4):
                for i in range(4):
                    wb, wkey = WS.next()
                    bnk = i // 2
                    c0 = (i % 2) * 256

                    def f(e, wb=wb, bnk=bnk, c0=c0):
                        ins = None
                        for k in range(8):
                            ins = e.matmul(ps[:, bnk, c0:c0 + 256], lhsT=hnT[hs][:, k, t4 * 128:(t4 + 1) * 128],
                                           rhs=wb[:, k, :], start=(k == 0), stop=(k == 7))
                        return ins
                    sc.op("pe", f, reads=[wkey, ("hnT", hs)], writes=[("ps", bnk)], join=(i % 2 == 1))
                    yield

            def stageA(tt):
                g, t4 = divmod(tt, 4)
                p = tt % 2
                if t4 == 0:
                    hs_of[g] = norm_group(g)
                    yield
                hs = hs_of[g]
                yield from proj(hs, t4)
                sc.op("act", lambda e: e.activation(out=kk, in_=P01, func=AF.Sigmoid, scale=-1.0),
                      reads=[("ps", 0), ("ps", 1)], writes=["kk"])
                yield
                sc.op("dve", lambda e: e.tensor_tensor(out=kk, in0=kk, in1=oml, op=ALU.mult),
                      reads=["kk", "oml"], writes=["kk"])
                yield
                sc.op("act", lambda e: e.activation(out=lf, in_=kk, func=AF.Ln, scale=-1.0, bias=one_c),
                      reads=["kk", "cst"], writes=["lf"])
                yield

                def cums(e):
                    ins = None
                    for hf in range(2):
                        ins = e.matmul(ps[:, 2 + hf, :], lhsT=uinc_f, rhs=lf[:, hf * 512:(hf + 1) * 512],
                                       start=True, stop=True)
                    for hf in range(2):
                        ins = e.matmul(ps[:, 4 + hf, :], lhsT=uexc_f, rhs=lf[:, hf * 512:(hf + 1) * 512],
                                       start=True, stop=True)
                    for hh in range(8):
                        ins = e.matmul(ps[:, 0, hh * 2:hh * 2 + 2], lhsT=lf[:, hh * 128:(hh + 1) * 128],
                                       rhs=cind_f, start=True, stop=True)
                    return ins
                sc.op("pe", cums, reads=["lf", "cst"], writes=[("ps", 2), ("ps", 3), ("ps", 4), ("ps", 5), ("ps", 0)])
                yield
                sc.op("act", lambda e: e.activation(out=dec[p], in_=ps[:, 0, 0:16], func=AF.Exp),
                      reads=[("ps", 0)], writes=[("dec", p)])
                yield
                CQ.pump(1)
                yield
                yield from proj(hs, t4)
                CQ.pump(1)
                sc.op("act", lambda e: e.activation(out=qs, in_=P01, func=AF.Silu),
                      reads=[("ps", 0), ("ps", 1)], writes=["lf"])
                yield
                sc.op("act", lambda e: e.activation(out=E[0], in_=P23, func=AF.Exp),
                      reads=[("ps", 2), ("ps", 3)], writes=[("E", 0)])
                yield
                sc.op("dve", lambda e: e.tensor_tensor(out=q_e, in0=qs, in1=E[0], op=ALU.mult),
                      reads=["lf", ("E", 0)], writes=["q_e"])
                yield
                sc.op("act", lambda e: e.activation(out=E[1], in_=P23, func=AF.Exp, scale=-1.0),
                      reads=[("ps", 2), ("ps", 3)], writes=[("E", 1)])
                yield
                sc.op("dve", lambda e: e.tensor_tensor(out=k_e, in0=kk, in1=E[1], op=ALU.mult),
                      reads=["kk", ("E", 1)], writes=["k_e"])
                yield
                sc.op("act", lambda e: e.activation(out=E[0], in_=P45, func=AF.Exp),
                      reads=[("ps", 4), ("ps", 5)], writes=[("E", 0)])
                yield
                sc.op("dve", lambda e: e.tensor_tensor(out=k_d[p], in0=kk, in1=E[0], op=ALU.mult),
                      reads=["kk", ("E", 0)], writes=[("k_d", p)])
                yield
                yield from proj(hs, t4)
                CQ.pump(1)
                sc.op("act", lambda e: e.activation(out=vv[p], in_=P01, func=AF.Copy),
                      reads=[("ps", 0), ("ps", 1)], writes=[("vv", p)])
                yield
                yield from proj(hs, t4)
                CQ.pump(1)
                sc.op("act", lambda e: e.activation(out=gate[p], in_=P01, func=AF.Silu),
                      reads=[("ps", 0), ("ps", 1)], writes=[("gate", p)])
                yield

                def trq(e):
                    pv = pbank_bf(2).rearrange("p (k n) -> p k n", k=8)
                    ins = None
                    for k in range(8):
                        ins = e.transpose(out=pv[:, k, :], in_=q_e[:, k * 128:(k + 1) * 128], identity=ident_bf)
                    return ins
                sc.op("pe", trq, reads=["q_e", "cbf"], writes=[("ps", 2)])
                yield
                pv2 = pbank_bf(2).rearrange("p (k n) -> p k n", k=8)
                sc.op("dve", lambda e: e.tensor_copy(out=q_eTm[p][:, :, 0, 0:64], in_=pv2[:, :, 0:64]),
                      reads=[("ps", 2)], writes=[("q_eTm", p)])
                sc.op("dve", lambda e: e.tensor_copy(out=q_eTm[p][:, :, 1, 64:128], in_=pv2[:, :, 64:128]),
                      reads=[("ps", 2)], writes=[("q_eTm", p)], join=True)
                yield

                def trk(e):
                    pv = pbank_bf(3).rearrange("p (k n) -> p k n", k=8)
                    ins = None
                    for k in range(8):
                        ins = e.transpose(out=pv[:, k, :], in_=k_e[:, k * 128:(k + 1) * 128], identity=ident_bf)
                    return ins
                sc.op("pe", trk, reads=["k_e", "cbf"], writes=[("ps", 3)])
                yield
                sc.op("dve", lambda e: e.tensor_copy(out=k_eT[p], in_=pbank_bf(3).rearrange("p (k n) -> p k n", k=8)),
                      reads=[("ps", 3)], writes=[("k_eT", p)])
                yield

            def stageB(tt):
                p = tt % 2
                qm, kt_, kd_, v_, gt_, dc_ = q_eTm[p], k_eT[p], k_d[p], vv[p], gate[p], dec[p]

                def scores(e):
                    ins = None
                    for hh in range(8):
                        o_ap = ps[:, 6 + hh // 4, (hh % 4) * 128:(hh % 4 + 1) * 128]
                        ins = e.matmul(o_ap, lhsT=kt_[:, hh, :], rhs=qm[:, hh, 0, :], start=True, stop=False)
                        ins = e.matmul(o_ap, lhsT=kt_[:, hh, :], rhs=qm[:, hh, 1, :], start=False, stop=True)
                    return ins
                sc.op("pe", scores, reads=[("k_eT", p), ("q_eTm", p)], writes=[("ps", 6), ("ps", 7)])
                yield
                sc.op("dve", lambda e: e.tensor_tensor(out=scT, in0=P67.rearrange("p (h t) -> p h t", h=8),
                                                       in1=bc_mid(uinc_bf, 8), op=ALU.mult),
                      reads=[("ps", 6), ("ps", 7), "cbf"], writes=["scT"])
                yield
                for c in range(2):
                    def upd(e, c=c):
                        ins = None
                        for hh in range(8):
                            ins = e.matmul(ps[:, 6 + hh // 4, (hh % 4) * 128:(hh % 4 + 1) * 128],
                                           lhsT=kd_[c * 64:(c + 1) * 64, hh * 128:(hh + 1) * 128],
                                           rhs=v_[c * 64:(c + 1) * 64, hh * 128:(hh + 1) * 128],
                                           start=True, stop=True)
                        return ins
                    sc.op("pe", upd, reads=[("k_d", p), ("vv", p)], writes=[("ps", 6), ("ps", 7)])
                    yield
                    for hh in range(8):
                        sv = S32[:, hh * 128:(hh + 1) * 128]
                        sc.op("dve", lambda e, sv=sv, hh=hh, c=c: e.scalar_tensor_tensor(
                            out=sv, in0=sv, scalar=dc_[:, hh * 2 + c:hh * 2 + c + 1],
                            in1=ps[:, 6 + hh // 4, (hh % 4) * 128:(hh % 4 + 1) * 128],
                            op0=ALU.mult, op1=ALU.add),
                            reads=[("S32", hh), ("dec", p), ("ps", 6 + hh // 4)], writes=[("S32", hh)])
                        if hh % 4 == 3:
                            yield
                    if c == 0:
                        sc.op("act", lambda e: e.activation(out=Sbf[1], in_=S32, func=AF.Copy),
                              reads=[("S32", hh) for hh in range(8)], writes=[("Sbf", 1)])
                        yield

                def omm(e):
                    ins = None
                    for hh in range(8):
                        o_ap = ps[:, 6 + hh // 4, (hh % 4) * 128:(hh % 4 + 1) * 128]
                        ins = e.matmul(o_ap, lhsT=qm[:, hh, 0, :], rhs=Sbf[0][:, hh * 128:(hh + 1) * 128],
                                       start=True, stop=False)
                        ins = e.matmul(o_ap, lhsT=qm[:, hh, 1, :], rhs=Sbf[1][:, hh * 128:(hh + 1) * 128],
                                       start=False, stop=False)
                        ins = e.matmul(o_ap, lhsT=scT[:, hh, :], rhs=v_[:, hh * 128:(hh + 1) * 128],
                                       start=False, stop=True)
                    return ins
                sc.op("pe", omm, reads=[("q_eTm", p), ("Sbf", 0), ("Sbf", 1), "scT", ("vv", p)],
                      writes=[("ps", 6), ("ps", 7)])
                yield
                sc.op("act", lambda e: e.activation(out=Sbf[0], in_=S32, func=AF.Copy),
                      reads=[("S32", hh) for hh in range(8)], writes=[("Sbf", 0)])
                yield
                sc.op("act", lambda e: e.activation(out=osq, in_=P67, func=AF.Square),
                      reads=[("ps", 6), ("ps", 7)], writes=["osq"])
                yield
                sc.op("dve", lambda e: e.tensor_reduce(out=ssq, in_=osq.rearrange("p (h d) -> p h d", h=8),
                                                       axis=AX.X, op=ALU.add),
                      reads=["osq"], writes=["ssq"])
                yield
                sc.op("act", lambda e: e.activation(out=lno, in_=ssq, func=AF.Ln, scale=1.0 / 128, bias=eps_c),
                      reads=["ssq", "cst"], writes=["lno"])
                yield
                sc.op("act", lambda e: e.activation(out=rso, in_=lno, func=AF.Exp, scale=-0.5),
                      reads=["lno"], writes=["rso"])
                yield
                sc.op("dve", lambda e: e.tensor_tensor(out=osq.rearrange("p (h d) -> p h d", h=8),
                                                       in0=P67.rearrange("p (h d) -> p h d", h=8),
                                                       in1=bc_last(rso, 128), op=ALU.mult),
                      reads=[("ps", 6), ("ps", 7), "rso"], writes=["osq"])
                yield
                sc.op("dve", lambda e: e.tensor_tensor(out=yb, in0=osq, in1=gt_, op=ALU.mult),
                      reads=["osq", ("gate", p)], writes=["yb"])
                yield

                def try_(e):
                    pv = pbank_bf(6).rearrange("p (k n) -> p k n", k=8)
                    ins = None
                    for k in range(8):
                        ins = e.transpose(out=pv[:, k, :], in_=yb[:, k * 128:(k + 1) * 128], identity=ident_bf)
                    return ins
                sc.op("pe", try_, reads=["yb", "cbf"], writes=[("ps", 6)])
                yield
                sc.op("act", lambda e: e.activation(out=yT, in_=pbank_bf(6).rearrange("p (k n) -> p k n", k=8),
                                                    func=AF.Copy), reads=[("ps", 6)], writes=["yT"])
                yield
                for i in range(4):
                    wb, wkey = WSB.next()
                    bnk = 6 + i // 2
                    c0 = (i % 2) * 256

                    def f(e, wb=wb, bnk=bnk, c0=c0):
                        ins = None
                        for k in range(8):
                            ins = e.matmul(ps[:, bnk, c0:c0 + 256], lhsT=yT[:, k, :], rhs=wb[:, k, :],
                                           start=(k == 0), stop=(k == 7))
                        return ins
                    sc.op("pe", f, reads=[wkey, "yT"], writes=[("ps", bnk)], join=(i % 2 == 1))
                    yield
                sc.op("dve", lambda e: e.tensor_tensor(out=h[:, tt, :], in0=P67, in1=h[:, tt, :], op=ALU.add),
                      reads=[("ps", 6), ("ps", 7), ("h", tt)], writes=[("h", tt)])
                sq_stat(tt)
                yield

            interleave(stageA(0), None)
            for tt in range(NT):
                interleave(stageB(tt), stageA(tt + 1) if tt + 1 < NT else None)

        def rope(xv, nh, tt, tmp, key):
            sinv = bc_mid(cs_tab[:, 0, tt, :], nh)
            cosv = bc_mid(cs_tab[:, 1, tt, :], nh)
            x1 = xv[:, :, 0:8]
            x2 = xv[:, :, 8:16]
            sc.op("dve", lambda e: e.tensor_tensor(out=tmp[:, 0], in0=x1, in1=cosv, op=ALU.mult),
                  reads=[key, "cs_tab"], writes=[("rt", 0)])
            sc.op("dve", lambda e: e.tensor_tensor(out=tmp[:, 1], in0=x2, in1=sinv, op=ALU.mult),
                  reads=[key, "cs_tab"], writes=[("rt", 1)])
            sc.op("dve", lambda e: e.tensor_tensor(out=tmp[:, 2], in0=x2, in1=cosv, op=ALU.mult),
                  reads=[key, "cs_tab"], writes=[("rt", 2)])
            sc.op("dve", lambda e: e.tensor_tensor(out=tmp[:, 3], in0=x1, in1=sinv, op=ALU.mult),
                  reads=[key, "cs_tab"], writes=[("rt", 3)])
            sc.op("dve", lambda e: e.tensor_tensor(out=x1, in0=tmp[:, 0], in1=tmp[:, 1], op=ALU.subtract),
                  reads=[("rt", 0), ("rt", 1), ("rt", 3)], writes=[key])
            sc.op("dve", lambda e: e.tensor_tensor(out=x2, in0=tmp[:, 2], in1=tmp[:, 3], op=ALU.add),
                  reads=[("rt", 2), ("rt", 3), key], writes=[key])

        def kv_tile_gen(hs, t4, tt, wb, wkey, base, bp, bt):
            ssk = f32v(ph(base + 0, 2), 2)
            lnk = f32v(ph(base + 2, 2), 2)
            rk = f32v(ph(base + 4, 2), 2)
            kn = f32v(ph(base + 8, 128), 128)
            tmp = f32v(ph(base + 136, 64), 64).rearrange("p (a h i) -> p a h i", a=4, h=2)
            kb = bf16v(ph(base + 200, 128), 128)
            knv = kn.rearrange("p (h d) -> p h d", h=2)
            kbv = kb.rearrange("p (h u d) -> p h u d", h=2, u=2)

            def f(e):
                ins = None
                for k in range(8):
                    ins = e.matmul(ps[:, bp, 0:256], lhsT=hnT[hs][:, k, t4 * 128:(t4 + 1) * 128],
                                   rhs=wb[:, k, :], start=(k == 0), stop=(k == 7))
                return ins
            sc.op("pe", f, reads=[wkey, ("hnT", hs)], writes=[("ps", bp)])
            yield
            for hh in range(2):
                sc.op("act", lambda e, hh=hh: e.activation(out=junk[:, 0:64], in_=ps[:, bp, hh * 64:(hh + 1) * 64],
                                                           func=AF.Square, accum_out=ssk[:, hh:hh + 1]),
                      reads=[("ps", bp)], writes=["ssk"], join=(hh == 1))
            yield
            sc.op("act", lambda e: e.activation(out=vsb[:, tt, :, 0:64],
                                                in_=ps[:, bp, 128:256].rearrange("p (h d) -> p h d", h=2),
                                                func=AF.Copy),
                  reads=[("ps", bp)], writes=[("vsb", tt)])
            yield
            sc.op("act", lambda e: e.activation(out=lnk, in_=ssk, func=AF.Ln, scale=1.0 / 64, bias=eps_c),
                  reads=["ssk", "cst"], writes=["lnk"])
            yield
            sc.op("act", lambda e: e.activation(out=rk, in_=lnk, func=AF.Exp, scale=-0.5),
                  reads=["lnk"], writes=["rk"])
            yield
            sc.op("dve", lambda e: e.tensor_tensor(out=knv, in0=ps[:, bp, 0:128].rearrange("p (h d) -> p h d", h=2),
                                                   in1=bc_last(rk, 64), op=ALU.mult),
                  reads=[("ps", bp), "rk"], writes=["kn"])
            yield
            sc.op("dve", lambda e: e.tensor_tensor(out=kn, in0=kn, in1=cst[:, C_GK:C_GK + 128], op=ALU.mult),
                  reads=["kn", "cst"], writes=["kn"])
            yield
            rope(knv, 2, tt, tmp, "kn")
            yield
            sc.op("dve", lambda e: e.tensor_copy(out=kbv[:, :, 0, :], in_=knv), reads=["kn"], writes=["kb"])
            sc.op("dve", lambda e: e.tensor_copy(out=kbv[:, :, 1, :], in_=knv), reads=["kn"], writes=["kb"],
                  join=True)
            yield

            def trk(e):
                pv = pbank_bf(bt)
                ins = None
                for hh in range(2):
                    ins = e.transpose(out=pv[:, hh * 128:(hh + 1) * 128], in_=kb[:, hh * 128:(hh + 1) * 128],
                                      identity=ident_bf)
                return ins
            sc.op("pe", trk, reads=["kb", "cbf"], writes=[("ps", bt)])
            yield
            sc.op("act", lambda e: e.activation(
                out=kT[:, :, tt * 128:(tt + 1) * 128],
                in_=pbank_bf(bt)[:, 0:256].rearrange("p (h n) -> p h n", h=2), func=AF.Copy),
                reads=[("ps", bt)], writes=[("kT", tt)])
            yield

        def shared_kv(seq):
            finish_stats()
            WS.push([(kvw_b[0], "kvw") for _ in range(NT)])
            for g in range(NG):
                hs = norm_group(g)
                for t4 in range(4):
                    wb, wkey = WS.next()
                    for _ in kv_tile_gen(hs, t4, g * 4 + t4, wb, wkey, 0, 2, 3):
                        pass

        def swa(seq):
            o_ = [0]

            def A(n):
                o = o_[0]
                o_[0] += n
                return ph(o, n)
            gq = f32v(A(1024), 1024)
            sqf = f32v(A(1024), 1024)
            qn = f32v(A(1024), 1024)
            ssq = f32v(A(16), 16)
            lnq = f32v(A(16), 16)
            rq = f32v(A(16), 16)
            tmp = f32v(A(512), 512).rearrange("p (a h i) -> p a h i", a=4, h=16)
            qb = bf16v(A(512), 512)
            qTm = bf16v(A(1024), 1024).rearrange("p (j c t) -> p j c t", j=8, c=2)
            pe_ = [bf16v(A(512), 512).rearrange("p (h t) -> p h t", h=8) for _ in range(4)]
            den = f32v(A(16), 16)
            rden = f32v(A(16), 16)
            ob = bf16v(A(512), 512)
            obT = bf16v(A(512), 512).rearrange("p (k t) -> p k t", k=8)

            wres = [bf16v(A(1024), 1024).rearrange("p (k n) -> p k n", k=8) for _ in range(8)]
            CQ.require("wq")
            CQ.require("awo")
            for bi in range(8):
                src = wq_b[bi] if bi < 4 else awo_b[bi - 4]
                wn = "wq" if bi < 4 else "awo"
                sc.op("sp", lambda e, d=wres[bi], s_=src: e.dma_start(out=d.rearrange("p k n -> p (k n)"), in_=s_),
                      reads=[("scr", wn)], writes=[("wres", bi)], dma=("wres", bi))
            finish_stats()
            sc.op("sp", lambda e: e.dma_start(out=gq, in_=gq_d), writes=["gq"], dma="gq")
            sc.op("pool", lambda e: e.memset(qTm.rearrange("p j c t -> p (j c t)"), 0.0), writes=["qT"])
            qnv = qn.rearrange("p (h d) -> p h d", h=16)
            P01 = pbank(0, 2)
            pcount = 0
            obanks = (0, 1, 6, 7)
            for g in range(NG):
                hs = norm_group(g)
                for t4 in range(4):
                    tt = g * 4 + t4
                    for i in range(4):
                        wb, wkey = wres[i], ("wres", i)
                        b = i // 2
                        c0 = (i % 2) * 256

                        def f(e, wb=wb, b=b, c0=c0, hs=hs, t4=t4):
                            ins = None
                            for k in range(8):
                                ins = e.matmul(ps[:, b, c0:c0 + 256], lhsT=hnT[hs][:, k, t4 * 128:(t4 + 1) * 128],
                                               rhs=wb[:, k, :], start=(k == 0), stop=(k == 7))
                            return ins
                        sc.op("pe", f, reads=[wkey, ("hnT", hs)], writes=[("ps", b)], join=(i % 2 == 1))
                    CQ.pump(4)
                    sc.op("act", lambda e: e.activation(out=sqf, in_=P01, func=AF.Square),
                          reads=[("ps", 0), ("ps", 1)], writes=["sqf"])
                    sc.op("dve", lambda e: e.tensor_reduce(out=ssq, in_=sqf.rearrange("p (h d) -> p h d", h=16),
                                                           axis=AX.X, op=ALU.add), reads=["sqf"], writes=["ssq"])
                    sc.op("act", lambda e: e.activation(out=lnq, in_=ssq, func=AF.Ln, scale=1.0 / 64, bias=eps_c),
                          reads=["ssq", "cst"], writes=["lnq"])
                    sc.op("act", lambda e: e.activation(out=rq, in_=lnq, func=AF.Exp, scale=-0.5),
                          reads=["lnq"], writes=["rq"])
                    sc.op("dve", lambda e: e.tensor_tensor(out=qnv, in0=P01.rearrange("p (h d) -> p h d", h=16),
                                                           in1=bc_last(rq, 64), op=ALU.mult),
                          reads=[("ps", 0), ("ps", 1), "rq"], writes=["qn"])
                    sc.op("dve", lambda e: e.tensor_tensor(out=qn, in0=qn, in1=gq, op=ALU.mult),
                          reads=["qn", "gq"], writes=["qn"])
                    rope(qnv, 16, tt, tmp, "qn")
                    sc.op("dve", lambda e: e.tensor_copy(out=qb, in_=qn), reads=["qn"], writes=["qb"])

                    def trq(e):
                        pv = pbank_bf(2).rearrange("p (k n) -> p k n", k=8)
                        ins = None
                        for k in range(8):
                            ins = e.transpose(out=pv[:, k, :], in_=qb[:, k * 128:(k + 1) * 128], identity=ident_bf)
                        return ins
                    sc.op("pe", trq, reads=["qb", "cbf"], writes=[("ps", 2)])
                    pv2 = pbank_bf(2).rearrange("p (k n) -> p k n", k=8)
                    sc.op("act", lambda e: e.activation(out=qTm[0:64, :, 0, :], in_=pv2[0:64, :, :], func=AF.Copy),
                          reads=[("ps", 2)], writes=["qT"])
                    sc.op("act", lambda e: e.activation(out=qTm[64:128, :, 1, :], in_=pv2[64:128, :, :], func=AF.Copy),
                          reads=[("ps", 2)], writes=["qT"], join=True)
                    if SWA_DBG < 2:
                        continue
                    kts = [tt - 1, tt] if tt > 0 else [tt]
                    for kvh in range(2):
                        slots = []
                        for ki, kt in enumerate(kts):
                            sb = 2 + 2 * (pcount % 2)
                            pslot = (pcount % 2) * 2 + ki
                            slots.append(pslot)

                            def scm(e, kvh=kvh, kt=kt, sb=sb):
                                ins = None
                                for hl in range(8):
                                    hd = kvh * 8 + hl
                                    j, ehalf = hd // 2, hd % 2
                                    ins = e.matmul(ps[:, sb + hl // 4, (hl % 4) * 128:(hl % 4 + 1) * 128],
                                                   lhsT=kT[:, kvh, kt * 128:(kt + 1) * 128],
                                                   rhs=qTm[:, j, ehalf, :], start=True, stop=True)
                                return ins
                            sc.op("pe", scm, reads=[("kT", kt), "qT"], writes=[("ps", sb), ("ps", sb + 1)])
                            sc.op("act", lambda e, sb=sb, pslot=pslot: e.activation(
                                out=pe_[pslot], in_=pbank(sb, 2).rearrange("p (h t) -> p h t", h=8),
                                func=AF.Exp, scale=0.125),
                                reads=[("ps", sb), ("ps", sb + 1)], writes=[("pe", pslot)])
                            mk = maskd_bf if kt == tt else maskp_bf
                            sc.op(MASK_ENG, lambda e, pslot=pslot, mk=mk: e.tensor_tensor(
                                out=pe_[pslot], in0=pe_[pslot], in1=bc_mid(mk, 8), op=ALU.mult),
                                reads=[("pe", pslot), "cbf"], writes=[("pe", pslot)])
                            pcount += 1
                        if SWA_DBG < 3:
                            continue

                        def pv(e, kvh=kvh, kts=tuple(kts), slots=tuple(slots)):
                            ins = None
                            for hl in range(8):
                                hd = kvh * 8 + hl
                                ob_ = obanks[hd // 4]
                                c0 = (hd % 4) * 65
                                for ki, kt in enumerate(kts):
                                    ins = e.matmul(ps[:, ob_, c0:c0 + 65], lhsT=pe_[slots[ki]][:, hl, :],
                                                   rhs=vsb[:, kt, kvh, 0:65], start=(ki == 0),
                                                   stop=(ki == len(kts) - 1))
                            return ins
                        sc.op("pe", pv, reads=[("pe", sl) for sl in slots] + [("vsb", kt) for kt in kts] + ["vones"],
                              writes=[("ps", obanks[kvh * 2]), ("ps", obanks[kvh * 2 + 1])])
                    if SWA_DBG < 4:
                        continue
                    for pr in range(2):
                        b0 = obanks[pr * 2]
                        ov = ps[:, b0:b0 + 2, 0:260].rearrange("p b (h d) -> p b h d", h=4)
                        sc.op("dve", lambda e, ov=ov, pr=pr: e.tensor_tensor(
                            out=den[:, pr * 8:(pr + 1) * 8].rearrange("p (b h) -> p b h", b=2),
                            in0=ov[:, :, :, 64], in1=esink[:, pr * 8:(pr + 1) * 8].rearrange("p (b h) -> p b h", b=2),
                            op=ALU.add), reads=[("ps", b0), ("ps", b0 + 1), "esink"], writes=["den"], join=(pr == 1))
                    sc.op("dve", lambda e: e.reciprocal(out=rden, in_=den), reads=["den"], writes=["rden"])
                    for pr in range(2):
                        b0 = obanks[pr * 2]
                        for bb in range(2):
                            ov = ps[:, b0 + bb, 0:260].rearrange("p (h d) -> p h d", h=4)[:, :, 0:64]
                            hd0 = pr * 8 + bb * 4
                            sc.op("dve", lambda e, ov=ov, hd0=hd0: e.tensor_tensor(
                                out=ob[:, hd0 * 64:(hd0 + 4) * 64].rearrange("p (h d) -> p h d", h=4),
                                in0=ov, in1=bc_last(rden[:, hd0:hd0 + 4], 64), op=ALU.mult),
                                reads=[("ps", b0 + bb), "rden"], writes=["ob"], join=not (pr == 0 and bb == 0))

                    if SWA_DBG < 5:
                        continue

                    def tro(e):
                        pv_ = pbank_bf(2).rearrange("p (k n) -> p k n", k=8)
                        ins = None
                        for k in range(8):
                            ins = e.transpose(out=pv_[:, k, :], in_=ob[:, k * 128:(k + 1) * 128], identity=ident_bf)
                        return ins
                    sc.op("pe", tro, reads=["ob", "cbf"], writes=[("ps", 2)])
                    sc.op("act", lambda e: e.activation(out=obT, in_=pbank_bf(2).rearrange("p (k n) -> p k n", k=8),
                                                        func=AF.Copy), reads=[("ps", 2)], writes=["obT"])
                    for i in range(4):
                        wb, wkey = wres[4 + i], ("wres", 4 + i)
                        b = 4 + i // 2
                        c0 = (i % 2) * 256

                        def f(e, wb=wb, b=b, c0=c0):
                            ins = None
                            for k in range(8):
                                ins = e.matmul(ps[:, b, c0:c0 + 256], lhsT=obT[:, k, :], rhs=wb[:, k, :],
                                               start=(k == 0), stop=(k == 7))
                            return ins
                        sc.op("pe", f, reads=[wkey, "obT"], writes=[("ps", b)], join=(i % 2 == 1))
                    sc.op("dve", lambda e, tt=tt: e.tensor_tensor(out=h[:, tt, :], in0=pbank(4, 2), in1=h[:, tt, :],
                                                                  op=ALU.add),
                          reads=[("ps", 4), ("ps", 5), ("h", tt)], writes=[("h", tt)])
                    sq_stat(tt)

        for seq in range(nseq):
            for tt in range(NT):
                sc.op("sp", lambda e, seq=seq, tt=tt: e.dma_start(out=h[:, tt, :], in_=x[seq, tt * 128:(tt + 1) * 128, :]),
                      writes=[("h", tt)], dma=("ld", tt))
                sq_stat(tt)
            for si, st in enumerate(stages):
                lastp = (si == len(stages) - 1)
                if st.startswith("ffn"):
                    ffn(int(st[3]) * 2 + int(st[4]), lastp, seq)
                elif st == "hgrn":
                    hgrn(seq)
                elif st == "kv":
                    if MERGE_KV and "ffn10" in stages:
                        continue
                    shared_kv(seq)
                elif st == "swa":
                    swa(seq)
                if lastp and not st.startswith("ffn"):
                    for tt in range(NT):
                        store_tile(seq, tt)
                sc.barrier()
        sc.wait_all("sp", stores)

        names = sc.assign()
        sems = {}
        for i, nm in enumerate(names):
            sems[nm] = es.enter_context(nc.semaphore(f"s{i}"))
        block = es.enter_context(nc.Block())

        @block.sync
        def _(e):
            sc.emit("sp", e, sems)

        @block.tensor
        def _(e):
            sc.emit("pe", e, sems)

        @block.scalar
        def _(e):
            sc.emit("act", e, sems)

        @block.vector
        def _(e):
            sc.emit("dve", e, sems)

        @block.gpsimd
        def _(e):
            sc.emit("pool", e, sems)
    return nc


def make_consts(ffn_norm_g, mix_norm_g, kv_norm_g, hgrn_onorm_g, attn_sinks, k_norm_g):
    c = np.zeros((128, NCST), np.float32)
    gains = [ffn_norm_g[0, 0], ffn_norm_g[0, 1], ffn_norm_g[1, 0], ffn_norm_g[1, 1],
             mix_norm_g[0], mix_norm_g[1], kv_norm_g]
    for gi, gv in enumerate(gains):
        c[:, C_G + gi * 8:C_G + gi * 8 + 8] = np.asarray(gv, np.float32).reshape(8, 128).T
    c[:, C_ONG] = np.asarray(hgrn_onorm_g, np.float32).reshape(128)
    idx = np.arange(128)
    s_ = idx[:, None]
    t_ = idx[None, :]
    c[:, C_IDENT:C_IDENT + 128] = (s_ == t_)
    same = (s_ // 64) == (t_ // 64)
    c[:, C_UINC:C_UINC + 128] = same & (s_ <= t_)
    c[:, C_UEXC:C_UEXC + 128] = same & (s_ > t_)
    c[:, C_CIND] = idx < 64
    c[:, C_CIND + 1] = idx >= 64
    c[:, C_MD:C_MD + 128] = (s_ <= t_)
    c[:, C_MP:C_MP + 128] = (s_ > t_)
    c[:, C_SINK:C_SINK + 16] = np.asarray(attn_sinks, np.float32).reshape(1, 16)
    c[:, C_EPS] = EPS
    c[:, C_ONE] = 1.0
    c[:, C_GK:C_GK + 128] = np.tile(np.asarray(k_norm_g, np.float32).reshape(64), 2)[None, :]
    return c


_PROG_CACHE = {}


def kernel(x, ffn_norm_g, ffn_w_gate_up, ffn_w_down, mix_norm_g, hgrn_w_in, hgrn_lb_logits,
           hgrn_onorm_g, hgrn_w_out, kv_norm_g, kv_w, k_norm_g, attn_w_q, q_norm_g,
           attn_sinks, attn_w_out, _stages=ALL_STAGES, _ncores=NCORES, _nseq=NSEQ):
    f = lambda a: np.ascontiguousarray(np.asarray(a, dtype=np.float32))
    x = f(x)
    key = (tuple(_stages), _nseq)
    if key not in _PROG_CACHE:
        _PROG_CACHE[key] = build_program(_nseq, tuple(_stages))
    nc = _PROG_CACHE[key]
    cst = make_consts(f(ffn_norm_g), f(mix_norm_g), f(kv_norm_g), f(hgrn_onorm_g)[0], f(attn_sinks)[0], f(k_norm_g))
    lbl = np.ascontiguousarray(np.broadcast_to(f(hgrn_lb_logits)[None, :, :], (128, 2, D)))
    gqb = np.ascontiguousarray(np.broadcast_to(np.tile(f(q_norm_g)[0], 16)[None, :], (128, D)))
    shared = {
        "ffn_w_gate_up": f(ffn_w_gate_up), "ffn_w_down": f(ffn_w_down),
        "hgrn_w_in": f(hgrn_w_in)[0], "hgrn_w_out": f(hgrn_w_out)[0], "kv_w": f(kv_w),
        "attn_w_q": f(attn_w_q)[0], "attn_w_out": f(attn_w_out)[0],
        "cst": cst, "lbl": lbl, "gqb": gqb,
    }
    in_maps = []
    for c in range(_ncores):
        m = dict(shared)
        m["x"] = np.ascontiguousarray(x[c * _nseq:(c + 1) * _nseq])
        in_maps.append(m)
    res = run_bass_kernel_spmd(nc, in_maps, core_ids=list(range(_ncores)))
    return np.concatenate([r["out"] for r in res.results], axis=0)
```

```python
import math
from contextlib import ExitStack

import numpy as np
import concourse.bass as bass
import concourse.mybir as mybir
from concourse.ap import AP
from concourse.bass_utils import run_bass_kernel_spmd

F32 = mybir.dt.float32
BF16 = mybir.dt.bfloat16
I32 = mybir.dt.int32
AF = mybir.ActivationFunctionType
ALU = mybir.AluOpType
AX = mybir.AxisListType

NCORES = 8
NSEQ = 2
S = 2048
D = 1024
DFF = 2816
NFC = DFF // 128
NT = S // 128
NG = NT // 4
EPS = 1e-6
DEBUG = False
STRICT = True
MASK_ENG = "dve"
SWA_DBG = 5
ALL_STAGES = ("ffn00", "hgrn", "ffn01", "kv", "ffn10", "swa", "ffn11")

C_G = 0
C_ONG = 56
C_IDENT = 64
C_UINC = 192
C_UEXC = 320
C_CIND = 448
C_MD = 456
C_MP = 584
C_SINK = 712
C_EPS = 728
C_ONE = 729
C_GK = 736
NCST = 864


def bc_mid(ap, n):
    a = ap.ap
    return AP(ap.tensor, ap.offset, [list(a[0]), [0, n], list(a[1])])


def bc_last(ap, n):
    a = ap.ap
    return AP(ap.tensor, ap.offset, [list(a[0]), list(a[1]), [0, n]])


class Op:
    __slots__ = ("eng", "fn", "deps", "dma", "token", "signal", "idx")

    def __init__(self, eng, fn, dma):
        self.eng = eng
        self.fn = fn
        self.deps = []
        self.dma = dma
        self.token = None
        self.signal = False


class Sched:
    COMPUTE = ("pe", "act", "dve", "pool")

    def __init__(self):
        self.ops = []
        self.state = {}
        self.last = {}

    def op(self, eng, fn, reads=(), writes=(), dma=None, join=False):
        o = Op(eng, fn, dma)
        raw, other = [], []
        for k in reads:
            st = self.state.get(k)
            if st:
                raw.extend(st[0])
        for k in writes:
            st = self.state.get(k)
            if st is None:
                st = self.state[k] = [[], {}, [], [], []]
            if join:
                other.extend(st[3])
                other.extend(st[4])
            else:
                other.extend(st[0])
            other.extend(st[1].values())
            other.extend(st[2])
        seen = set()
        for d in raw:
            if id(d) not in seen:
                seen.add(id(d))
                o.deps.append(d)
        for d in other:
            if id(d) in seen:
                continue
            if (not STRICT) and d.dma is None and dma is None and d.eng == eng:
                continue
            seen.add(id(d))
            o.deps.append(d)
        for d in o.deps:
            d.signal = True
        for k in reads:
            st = self.state.get(k)
            if st is None:
                st = self.state[k] = [[], {}, [], [], []]
            if dma is None:
                st[1][eng] = o
            else:
                st[2].append(o)
        for k in writes:
            st = self.state[k]
            cur_readers = list(st[1].values()) + list(st[2])
            if join:
                st[0].append(o)
                st[4] = st[4] + cur_readers
            else:
                st[3] = st[0]
                st[4] = cur_readers
                st[0] = [o]
            st[1] = {}
            st[2] = []
        self.ops.append(o)
        if dma is None and fn is not None:
            self.last[eng] = o
        return o

    def barrier(self, engines=("pe", "act", "dve", "pool", "sp")):
        lasts = [self.last[e] for e in self.COMPUTE if e in self.last]
        for e in engines:
            o = Op(e, None, None)
            o.deps = [d for d in lasts if d.eng != e]
            for d in o.deps:
                d.signal = True
            self.ops.append(o)

    def wait_all(self, eng, ops):
        o = Op(eng, None, None)
        o.deps = list(ops)
        for d in o.deps:
            d.signal = True
        self.ops.append(o)

    def assign(self):
        EPOCH = 30000
        cnt = {}
        names = []
        for o in self.ops:
            if o.fn is None or not (o.signal or o.dma is not None):
                continue
            key = ("dma", o.dma) if o.dma is not None else ("eng", o.eng)
            c = cnt.get(key, 0)
            inc = 16 if o.dma is not None else 1
            c += inc
            cnt[key] = c
            ep, val = divmod(c - inc, EPOCH)
            name = (key, ep)
            if name not in names:
                names.append(name)
            o.token = (name, val + inc)
        return names

    def emit(self, eng_name, eng, sems):
        waited = {}
        for o in self.ops:
            if o.eng != eng_name:
                continue
            for d in o.deps:
                if d.token is None:
                    continue
                nm, val = d.token
                if waited.get(nm, 0) < val:
                    eng.wait_ge(sems[nm], val)
                    waited[nm] = val
            if o.fn is None:
                continue
            if o.dma is not None and o.token is not None:
                nm, val = o.token
                if val > 16 and waited.get(nm, 0) < val - 16:
                    eng.wait_ge(sems[nm], val - 16)
                    waited[nm] = val - 16
            ins = o.fn(eng)
            if o.token is not None:
                nm, val = o.token
                ins.then_inc(sems[nm], 16 if o.dma is not None else 1)


def build_program(nseq=NSEQ, stages=ALL_STAGES):
    nc = bass.Bass("TRN2", target_bir_lowering=False)
    sc = Sched()

    def dram_in(name, shape, dt=F32):
        return nc.dram_tensor(name, list(shape), dt, kind="ExternalInput").ap()

    def dram_scr(name, shape, dt=BF16):
        return nc.dram_tensor(name, list(shape), dt, kind="Internal").ap()

    x = dram_in("x", [nseq, S, D])
    wgu = dram_in("ffn_w_gate_up", [2, 2, D, 2 * DFF])
    wdn = dram_in("ffn_w_down", [2, 2, DFF, D])
    w_in = dram_in("hgrn_w_in", [D, 4 * D])
    hwo = dram_in("hgrn_w_out", [D, D])
    kvw = dram_in("kv_w", [D, 256])
    wq = dram_in("attn_w_q", [D, D])
    awo = dram_in("attn_w_out", [D, D])
    cst_d = dram_in("cst", [128, NCST])
    lbl_d = dram_in("lbl", [128, 2, D])
    gq_d = dram_in("gqb", [128, D])
    out = nc.dram_tensor("out", [nseq, S, D], F32, kind="ExternalOutput").ap()

    wgu_b = [dram_scr(f"wgu_b{i}", [NFC, 128, 2048]) for i in range(4)]
    wd_b = [dram_scr(f"wd_b{i}", [128, NFC * D]) for i in range(4)]
    win_b = dram_scr("win_b", [16, 128, 2048])
    hwo_b = dram_scr("hwo_b", [4, 128, 2048])
    kvw_b = dram_scr("kvw_b", [1, 128, 2048])
    wq_b = dram_scr("wq_b", [4, 128, 2048])
    awo_b = dram_scr("awo_b", [4, 128, 2048])

    es = ExitStack()
    with es:
        NW = 53200
        arena = es.enter_context(nc.sbuf_tensor("arena", [128, NW], F32))
        ps = es.enter_context(nc.psum_tensor("ps", [128, 8, 512], F32))
        cur = [0]

        def alloc(nwords):
            o = cur[0]
            cur[0] += nwords
            assert cur[0] <= NW, cur[0]
            return o

        def f32v(off, n):
            return arena[:, off:off + n]

        def bf16v(off, nwords):
            return arena[:, off:off + nwords].bitcast(BF16)

        h_off = alloc(NT * D)
        h = f32v(h_off, NT * D).rearrange("p (t d) -> p t d", t=NT)
        cst = f32v(alloc(NCST), NCST)
        hnT = [bf16v(alloc(2048), 2048).rearrange("p (k n) -> p k n", k=8) for _ in range(2)]
        wblk = [bf16v(alloc(1024), 1024).rearrange("p (k n) -> p k n", k=8) for _ in range(4)]
        xn = [bf16v(alloc(512), 512) for _ in range(2)]
        junk = bf16v(alloc(512), 512)
        ssum = f32v(alloc(16), 16)
        lnv = f32v(alloc(16), 16)
        rstd = f32v(alloc(16), 16)
        cbf = bf16v(alloc(256), 256)
        ident_bf = cbf[:, 0:128]
        uinc_bf = cbf[:, 128:256]
        maskd_bf = cbf[:, 256:384]
        maskp_bf = cbf[:, 384:512]
        cs_tab = f32v(alloc(256), 256).rearrange("p (a t i) -> p a t i", a=2, t=NT)
        kT = bf16v(alloc(2048), 2048).rearrange("p (h n) -> p h n", h=2)
        vsb = bf16v(alloc(1056), 1056).rearrange("p (t h d) -> p t h d", t=NT, h=2)
        esink = f32v(alloc(16), 16)
        cq_sf = [f32v(alloc(1024), 1024) for _ in range(2)]
        cq_sb = [bf16v(alloc(512), 512) for _ in range(2)]
        PH = alloc(0)
        PH_WORDS = NW - PH

        def ph(off, n):
            assert off + n <= PH_WORDS, (off, n, PH_WORDS)
            return PH + off

        ident_f = cst[:, C_IDENT:C_IDENT + 128]
        uinc_f = cst[:, C_UINC:C_UINC + 128]
        uexc_f = cst[:, C_UEXC:C_UEXC + 128]
        cind_f = cst[:, C_CIND:C_CIND + 2]
        eps_c = cst[:, C_EPS:C_EPS + 1]
        one_c = cst[:, C_ONE:C_ONE + 1]

        def pbank(b, n=1):
            return ps[:, b:b + n, :].rearrange("p b n -> p (b n)")

        def pbank_bf(b):
            return ps[:, b, :].bitcast(BF16)

        class WStream:
            def __init__(self, slots, tag):
                self.q = []
                self.loaded = 0
                self.cur = 0
                self.slots = slots
                self.tag = tag
                self.n = len(slots)

            def push(self, items):
                for wn in sorted({w_ for (_, w_) in items}):
                    CQ.require(wn)
                self.q.extend(items)

            def _load(self, i):
                slot = i % self.n
                src, wname = self.q[i]
                dst = self.slots[slot]
                sc.op("sp", lambda e, d=dst, s=src: e.dma_start(out=d.rearrange("p k n -> p (k n)"), in_=s),
                      reads=[("scr", wname)], writes=[(self.tag, slot)], dma=(self.tag, slot))

            def next(self):
                i = self.cur
                while self.loaded < min(len(self.q), i + self.n):
                    self._load(self.loaded)
                    self.loaded += 1
                self.cur += 1
                return self.slots[i % self.n], (self.tag, i % self.n)

        WS = WStream(wblk, "wblk")

        sc.op("sp", lambda e: e.dma_start(out=cst, in_=cst_d), writes=["cst"], dma="cst")
        sc.op("dve", lambda e: e.tensor_copy(out=ident_bf, in_=ident_f), reads=["cst"], writes=["cbf"])
        sc.op("dve", lambda e: e.tensor_copy(out=uinc_bf, in_=uinc_f), reads=["cst"], writes=["cbf"], join=True)
        sc.op("dve", lambda e: e.tensor_copy(out=maskd_bf, in_=cst[:, C_MD:C_MD + 128]), reads=["cst"],
              writes=["cbf"], join=True)
        sc.op("dve", lambda e: e.tensor_copy(out=maskp_bf, in_=cst[:, C_MP:C_MP + 128]), reads=["cst"],
              writes=["cbf"], join=True)

        class ConvQueue:
            def __init__(self):
                self.jobs = []
                self.nrec = 0
                self.castno = 0
                self.sf = cq_sf
                self.sb = cq_sb

            def add(self, wname, loads, casts, stores):
                self.jobs.append((wname, loads, casts, stores))

            def _L(self, j):
                slot = j % 2
                _, loads, _, _ = self.jobs[j]
                for li, (dview, src) in enumerate(loads):
                    sc.op("sp", lambda e, d=dview(self.sf[slot]), s=src: e.dma_start(out=d, in_=s),
                          writes=[("sf", slot)], dma=("sf", slot, li), join=(li > 0))

            def _C(self, j):
                slot = j % 2
                _, _, casts, _ = self.jobs[j]
                for ci, (oview, iview, gc) in enumerate(casts):
                    o_ap, i_ap = oview(self.sb[slot]), iview(self.sf[slot])
                    n = self.castno
                    self.castno += 1
                    if n % 2 == 0:
                        if gc is None:
                            fn = lambda e, o_ap=o_ap, i_ap=i_ap: e.tensor_copy(out=o_ap, in_=i_ap)
                        else:
                            fn = lambda e, o_ap=o_ap, i_ap=i_ap, gc=gc: e.tensor_scalar(
                                out=o_ap, in0=i_ap, scalar1=gc, scalar2=None, op0=ALU.mult)
                        eng = "dve"
                    else:
                        if gc is None:
                            fn = lambda e, o_ap=o_ap, i_ap=i_ap: e.activation(out=o_ap, in_=i_ap, func=AF.Copy)
                        else:
                            fn = lambda e, o_ap=o_ap, i_ap=i_ap, gc=gc: e.activation(out=o_ap, in_=i_ap,
                                                                                    func=AF.Copy, scale=gc)
                        eng = "act"
                    sc.op(eng, fn, reads=[("sf", slot), "cst"], writes=[("sb", slot)], join=(ci > 0))

            def _S(self, j):
                slot = j % 2
                wname, _, _, stores = self.jobs[j]
                for si, (dst, sview) in enumerate(stores):
                    sc.op("sp", lambda e, d=dst, s=sview(self.sb[slot]): e.dma_start(out=d, in_=s),
                          reads=[("sb", slot)], writes=[("scr", wname)], dma=("sbst", slot, si), join=True)

            def pump(self, n=1):
                for _ in range(n):
                    k = self.nrec
                    nj = len(self.jobs)
                    if k > nj + 1:
                        return
                    if k == 0 and nj > 0:
                        self._L(0)
                    if k + 1 < nj:
                        self._L(k + 1)
                    if k < nj:
                        self._C(k)
                    if 1 <= k <= nj:
                        self._S(k - 1)
                    self.nrec += 1

            def require(self, wname):
                last = max([i for i, jb in enumerate(self.jobs) if jb[0] == wname], default=-1)
                forced = 0
                while self.nrec < last + 2:
                    self.pump()
                    forced += 1
                if forced > 2 and DEBUG:
                    print("CQ.require", wname, "forced pumps", forced, "of", len(self.jobs))

        CQ = ConvQueue()

        def gcol(gi, kc):
            return cst[:, C_G + gi * 8 + kc:C_G + gi * 8 + kc + 1]

        def add_ffn_jobs(i4):
            l, j = divmod(i4, 2)
            W = wgu[l, j].rearrange("(k p) n -> p k n", p=128)
            wname = f"ffn{i4}"
            for fc in range(NFC):
                for k0 in (0, 4):
                    loads = [
                        (lambda t: t.rearrange("p (k u c) -> p k u c", k=4, u=2)[:, :, 0, :],
                         W[:, k0:k0 + 4, fc * 128:(fc + 1) * 128]),
                        (lambda t: t.rearrange("p (k u c) -> p k u c", k=4, u=2)[:, :, 1, :],
                         W[:, k0:k0 + 4, DFF + fc * 128:DFF + (fc + 1) * 128]),
                    ]
                    casts = [(lambda t, kk=kk: t[:, kk * 256:(kk + 1) * 256],
                              lambda t, kk=kk: t[:, kk * 256:(kk + 1) * 256], gcol(i4, k0 + kk)) for kk in range(4)]
                    stores = [(wgu_b[i4][fc][:, k0 * 256:(k0 + 4) * 256], lambda t: t)]
                    CQ.add(wname, loads, casts, stores)
            Wd = wdn[l, j].rearrange("(f p) m -> p f m", p=128)
            for f0 in range(NFC):
                loads = [(lambda t: t, Wd[:, f0, :])]
                casts = [(lambda t: t, lambda t: t, None)]
                stores = [(wd_b[i4][:, f0 * D:(f0 + 1) * D], lambda t: t)]
                CQ.add(wname, loads, casts, stores)

        def add_rows_jobs(wname, Wsrc, ncols, dst, gi, ong=False):
            Wv = Wsrc.rearrange("(k p) n -> p k n", p=128)
            dall = dst.rearrange("b p (k c) -> p k b c", k=8)
            if ncols >= 1024:
                nq = ncols // 1024
                for kc in range(8):
                    g_ = cst[:, C_ONG:C_ONG + 1] if ong else (None if gi is None else gcol(gi, kc))
                    for hh in range(nq):
                        loads = [(lambda t: t, Wv[:, kc, hh * 1024:(hh + 1) * 1024])]
                        casts = [(lambda t: t, lambda t: t, g_)]
                        stores = [(dall[:, kc, hh * 4:(hh + 1) * 4, :], lambda t: t.rearrange("p (b c) -> p b c", b=4))]
                        CQ.add(wname, loads, casts, stores)
            else:
                kper = 1024 // ncols
                ncb = ncols // 256
                for k0 in range(0, 8, kper):
                    loads = [(lambda t: t.rearrange("p (k n) -> p k n", k=kper), Wv[:, k0:k0 + kper, :])]
                    casts = []
                    for kk in range(kper):
                        g_ = cst[:, C_ONG:C_ONG + 1] if ong else (None if gi is None else gcol(gi, k0 + kk))
                        casts.append((lambda t, kk=kk: t[:, kk * ncols:(kk + 1) * ncols],
                                      lambda t, kk=kk: t[:, kk * ncols:(kk + 1) * ncols], g_))
                    stores = [(dall[:, k0 + kk, :, :],
                               lambda t, kk=kk: t[:, kk * ncols:(kk + 1) * ncols].rearrange("p (b c) -> p b c", b=ncb))
                              for kk in range(kper)]
                    CQ.add(wname, loads, casts, stores)

        for st in stages:
            if st.startswith("ffn"):
                add_ffn_jobs(int(st[3]) * 2 + int(st[4]))
            elif st == "hgrn":
                add_rows_jobs("win", w_in, 4096, win_b, 4)
                add_rows_jobs("hwo", hwo, 1024, hwo_b, None, ong=True)
            elif st == "kv":
                add_rows_jobs("kvw", kvw, 256, kvw_b, 6)
            elif st == "swa":
                add_rows_jobs("wq", wq, 1024, wq_b, 5)
                add_rows_jobs("awo", awo, 1024, awo_b, None)

        def rope_tables():
            pos_i = arena[:, PH:PH + 16].bitcast(I32)
            pos_f = f32v(ph(16, 16), 16)
            ang = f32v(ph(32, 256), 256).rearrange("p (a t i) -> p a t i", a=2, t=NT)
            kf = f32v(ph(288, 256), 256)
            ki = arena[:, PH + 544:PH + 800].bitcast(I32)
            red = f32v(ph(800, 256), 256)
            sc.op("pool", lambda e: e.iota(pos_i, pattern=[[128, 16]], base=0, channel_multiplier=1),
                  writes=["pos_i"])
            sc.op("dve", lambda e: e.tensor_copy(out=pos_f, in_=pos_i), reads=["pos_i"], writes=["pos_f"])
            first = True
            for i in range(8):
                fr = float(np.float32(500000.0) ** np.float32(-i * (2.0 / 16.0)))
                sc.op("dve", lambda e, i=i, fr=fr: e.tensor_scalar(out=ang[:, 0, :, i], in0=pos_f, scalar1=fr,
                                                                   scalar2=None, op0=ALU.mult),
                      reads=["pos_f"], writes=["ang0"], join=not first)
                first = False
            sc.op("dve", lambda e: e.tensor_scalar(out=ang[:, 1, :, :], in0=ang[:, 0, :, :], scalar1=math.pi / 2,
                                                   scalar2=None, op0=ALU.add), reads=["ang0"], writes=["ang1"])
            angf = f32v(ph(32, 256), 256)
            sc.op("dve", lambda e: e.tensor_scalar(out=kf, in0=angf, scalar1=1.0 / (2 * math.pi), scalar2=None,
                                                   op0=ALU.mult), reads=["ang0", "ang1"], writes=["kf"])
            sc.op("dve", lambda e: e.tensor_copy(out=ki, in_=kf), reads=["kf"], writes=["ki"])
            sc.op("dve", lambda e: e.tensor_copy(out=kf, in_=ki), reads=["ki"], writes=["kf"])
            sc.op("dve", lambda e: e.scalar_tensor_tensor(out=red, in0=kf, scalar=-2 * math.pi, in1=angf,
                                                          op0=ALU.mult, op1=ALU.add),
                  reads=["kf", "ang0", "ang1"], writes=["red"])
            wr = f32v(ph(1056, 256), 256)
            sc.op("dve", lambda e: e.tensor_scalar(out=wr, in0=red, scalar1=math.pi, scalar2=2 * math.pi,
                                                   op0=ALU.is_gt, op1=ALU.mult), reads=["red"], writes=["wr"])
            sc.op("dve", lambda e: e.tensor_tensor(out=red, in0=red, in1=wr, op=ALU.subtract),
                  reads=["red", "wr"], writes=["red"])
            sc.op("dve", lambda e: e.tensor_scalar(out=wr, in0=red, scalar1=-math.pi, scalar2=2 * math.pi,
                                                   op0=ALU.is_lt, op1=ALU.mult), reads=["red"], writes=["wr"])
            sc.op("dve", lambda e: e.tensor_tensor(out=red, in0=red, in1=wr, op=ALU.add),
                  reads=["red", "wr"], writes=["red"])
            sc.op("dve", lambda e: e.tensor_scalar(out=red, in0=red, scalar1=-math.pi, scalar2=math.pi,
                                                   op0=ALU.max, op1=ALU.min), reads=["red"], writes=["red"])
            sc.op("act", lambda e: e.activation(out=cs_tab.rearrange("p a t i -> p (a t i)"), in_=red, func=AF.Sin),
                  reads=["red"], writes=["cs_tab"])

        need_attn = ("kv" in stages) or ("swa" in stages)
        if need_attn:
            rope_tables()
            sc.op("act", lambda e: e.activation(out=esink, in_=cst[:, C_SINK:C_SINK + 16], func=AF.Exp),
                  reads=["cst"], writes=["esink"])
            sc.op("pool", lambda e: e.memset(vsb[:, :, :, 64:66], 1.0), writes=["vones"])
            sc.barrier()

        def sq_stat(tt):
            sc.op("act", lambda e: e.activation(out=junk, in_=h[:, tt, :], func=AF.Square,
                                                accum_out=ssum[:, tt:tt + 1]),
                  reads=[("h", tt)], writes=[("ss", tt)])

        def finish_stats():
            sc.op("act", lambda e: e.activation(out=lnv, in_=ssum, func=AF.Ln, scale=1.0 / D, bias=eps_c),
                  reads=[("ss", t) for t in range(NT)] + ["cst"], writes=["lnv"])
            sc.op("act", lambda e: e.activation(out=rstd, in_=lnv, func=AF.Exp, scale=-0.5),
                  reads=["lnv"], writes=["rstd"])

        tcount = [0]
        hcount = [0]

        def norm_group(g):
            hs = hcount[0] % 2
            hcount[0] += 1
            for t4 in range(4):
                tt = g * 4 + t4
                xs = tcount[0] % 2
                tb = tcount[0] % 2
                tcount[0] += 1
                sc.op("dve", lambda e, tt=tt, xs=xs: e.tensor_scalar(out=xn[xs], in0=h[:, tt, :],
                                                                     scalar1=rstd[:, tt:tt + 1], scalar2=None,
                                                                     op0=ALU.mult),
                      reads=[("h", tt), "rstd"], writes=[("xn", xs)])

                def tr(e, xs=xs, tb=tb):
                    pv = pbank_bf(tb).rearrange("p (k n) -> p k n", k=8)
                    ins = None
                    for k in range(8):
                        ins = e.transpose(out=pv[:, k, :], in_=xn[xs][:, k * 128:(k + 1) * 128], identity=ident_bf)
                    return ins
                sc.op("pe", tr, reads=[("xn", xs), "cbf"], writes=[("ps", tb)])
                sc.op("act", lambda e, hs=hs, t4=t4, tb=tb: e.activation(
                    out=hnT[hs][:, :, t4 * 128:(t4 + 1) * 128],
                    in_=pbank_bf(tb).rearrange("p (k n) -> p k n", k=8), func=AF.Copy),
                    reads=[("ps", tb)], writes=[("hnT", hs)], join=(t4 > 0))
            return hs

        def ffn(i4, last_phase, seq):
            wd_sb = bf16v(ph(0, 11264), 11264).rearrange("p (f m) -> p f m", f=NFC)
            actT = bf16v(ph(11264, 5632), 5632).rearrange("p (f n) -> p f n", f=NFC)
            sil = [f32v(ph(16896 + i * 512, 512), 512) for i in range(2)]
            finish_stats()
            WS.push([(wgu_b[i4][fc], f"ffn{i4}") for _ in range(NG) for fc in range(NFC)])
            gucount = 0
            dcount = 0
            for g in range(NG):
                hs = norm_group(g)
                for fc in range(NFC):
                    wb, wkey = WS.next()
                    if fc == 0:
                        for half in range(2):
                            f0, f1 = (0, 11) if half == 0 else (11, NFC)
                            sc.op("sp", lambda e, f0=f0, f1=f1: e.dma_start(
                                out=wd_sb[:, f0:f1, :].rearrange("p f m -> p (f m)"),
                                in_=wd_b[i4][:, f0 * D:f1 * D]),
                                reads=[("scr", f"ffn{i4}")], writes=["wd"], dma=("wd", half), join=(half == 1))
                    if i4 != 0 or fc % 2 == 0:
                        CQ.pump()
                    gs = gucount % 2
                    gucount += 1
                    bg, bu = 2 + gs, 4 + gs

                    def gu(e, wb=wb, hs=hs, bg=bg, bu=bu):
                        ins = None
                        for k in range(8):
                            ins = e.matmul(pbank(bg), lhsT=wb[:, k, 0:128], rhs=hnT[hs][:, k, :],
                                           start=(k == 0), stop=(k == 7))
                        for k in range(8):
                            ins = e.matmul(pbank(bu), lhsT=wb[:, k, 128:256], rhs=hnT[hs][:, k, :],
                                           start=(k == 0), stop=(k == 7))
                        return ins
                    sc.op("pe", gu, reads=[wkey, ("hnT", hs)], writes=[("ps", bg), ("ps", bu)])
                    sc.op("act", lambda e, gs=gs, bg=bg: e.activation(out=sil[gs], in_=pbank(bg), func=AF.Silu),
                          reads=[("ps", bg)], writes=[("sil", gs)])
                    sc.op("dve", lambda e, gs=gs, bu=bu, fc=fc: e.tensor_tensor(
                        out=actT[:, fc, :], in0=pbank(bu), in1=sil[gs], op=ALU.mult),
                        reads=[("ps", bu), ("sil", gs)], writes=[("actT", fc)])
                for t4 in range(4):
                    tt = g * 4 + t4
                    for half in range(2):
                        db = 6 + dcount % 2
                        dcount += 1

                        def dn(e, t4=t4, half=half, db=db):
                            ins = None
                            for fc in range(NFC):
                                ins = e.matmul(pbank(db), lhsT=actT[:, fc, t4 * 128:(t4 + 1) * 128],
                                               rhs=wd_sb[:, fc, half * 512:(half + 1) * 512],
                                               start=(fc == 0), stop=(fc == NFC - 1))
                            return ins
                        sc.op("pe", dn, reads=[("actT", fc) for fc in range(NFC)] + ["wd"], writes=[("ps", db)])
                        hv = h[:, tt, half * 512:(half + 1) * 512]
                        sc.op("dve", lambda e, hv=hv, db=db: e.scalar_tensor_tensor(
                            out=hv, in0=pbank(db), scalar=0.5, in1=hv, op0=ALU.mult, op1=ALU.add),
                            reads=[("ps", db), ("h", tt)], writes=[("h", tt)])
                    if last_phase:
                        store_tile(seq, tt)
                    else:
                        sq_stat(tt)

        stores = []

        def store_tile(seq, tt):
            o = sc.op("sp", lambda e: e.dma_start(out=out[seq, tt * 128:(tt + 1) * 128, :], in_=h[:, tt, :]),
                      reads=[("h", tt)], writes=[], dma=("st", tt))
            stores.append(o)

        def interleave(g1, g2):
            gens = [g for g in (g1, g2) if g is not None]
            while gens:
                for g in list(gens):
                    try:
                        next(g)
                    except StopIteration:
                        gens.remove(g)

        def hgrn(seq):
            o_ = [0]

            def A_(n):
                o = o_[0]
                o_[0] += n
                return ph(o, n)
            oml = f32v(A_(1024), 1024)
            kk = f32v(A_(1024), 1024)
            lf = f32v(A_(1024), 1024)
            e_off = A_(2048)
            E = [f32v(e_off + i * 1024, 1024) for i in range(2)]
            lbl = f32v(e_off, 2048).rearrange("p (a n) -> p a n", a=2)
            qs = lf
            osq = f32v(A_(1024), 1024)
            q_e = bf16v(A_(512), 512)
            k_e = bf16v(A_(512), 512)
            q_eTm = [bf16v(A_(1024), 1024).rearrange("p (h c t) -> p h c t", h=8, c=2) for _ in range(2)]
            k_eT = [bf16v(A_(512), 512).rearrange("p (h t) -> p h t", h=8) for _ in range(2)]
            k_d = [bf16v(A_(512), 512) for _ in range(2)]
            vv = [bf16v(A_(512), 512) for _ in range(2)]
            gate = [bf16v(A_(512), 512) for _ in range(2)]
            dec = [f32v(A_(16), 16) for _ in range(2)]
            scT = bf16v(A_(512), 512).rearrange("p (h t) -> p h t", h=8)
            S32 = f32v(A_(1024), 1024)
            Sbf = [bf16v(A_(512), 512) for _ in range(2)]
            ssq = f32v(A_(8), 8)
            lno = f32v(A_(8), 8)
            rso = f32v(A_(8), 8)
            yb = bf16v(A_(512), 512)
            yT = bf16v(A_(512), 512).rearrange("p (h t) -> p h t", h=8)
            wb2 = [bf16v(A_(1024), 1024).rearrange("p (k n) -> p k n", k=8) for _ in range(2)]
            WSB = WStream(wb2, "wb2")

            finish_stats()
            WS.push([(win_b[cb], "win") for tt in range(NT)
                     for cb in (4, 5, 6, 7, 0, 1, 2, 3, 8, 9, 10, 11, 12, 13, 14, 15)])
            WSB.push([(hwo_b[cb], "hwo") for tt in range(NT) for cb in range(4)])

            sc.op("sp", lambda e: e.dma_start(out=lbl.rearrange("p a n -> p (a n)"),
                                              in_=lbl_d.rearrange("p a n -> p (a n)")),
                  writes=["lbl"], dma="lbl")
            sc.op("dve", lambda e: e.tensor_tensor(out=kk, in0=lbl[:, 1, :], in1=lbl[:, 0, :], op=ALU.subtract),
                  reads=["lbl"], writes=["kk"])
            sc.op("act", lambda e: e.activation(out=oml, in_=kk, func=AF.Sigmoid), reads=["kk"], writes=["oml"])
            sc.barrier(("act", "dve"))
            sc.op("pool", lambda e: e.memset(S32, 0.0), writes=[("S32", hh) for hh in range(8)])
            sc.op("pool", lambda e: e.memset(Sbf[0], 0.0), writes=[("Sbf", 0)])
            for p_ in range(2):
                sc.op("pool", lambda e, p_=p_: e.memset(q_eTm[p_].rearrange("p h c t -> p (h c t)"), 0.0),
                      writes=[("q_eTm", p_)])

            P01 = pbank(0, 2)
            P23 = pbank(2, 2)
            P45 = pbank(4, 2)
            P67 = pbank(6, 2)
            hs_of = {}

            def proj(hs, t4):
                for i in range(4):
                    wb, wkey = WS.next()
                    bnk = i // 2
                    c0 = (i % 2) * 256

                    def f(e, wb=wb, bnk=bnk, c0=c0):
                        ins = None
                        for k in range(8):
                            ins = e.matmul(ps[:, bnk, c0:c0 + 256], lhsT=hnT[hs][:, k, t4 * 128:(t4 + 1) * 128],
                                           rhs=wb[:, k, :], start=(k == 0), stop=(k == 7))
                        return ins
                    sc.op("pe", f, reads=[wkey, ("hnT", hs)], writes=[("ps", bnk)], join=(i % 2 == 1))
                    yield

            def stageA(tt):
                g, t4 = divmod(tt, 4)
                p = tt % 2
                if t4 == 0:
                    hs_of[g] = norm_group(g)
                    yield
                hs = hs_of[g]
                yield from proj(hs, t4)
                sc.op("act", lambda e: e.activation(out=kk, in_=P01, func=AF.Sigmoid, scale=-1.0),
                      reads=[("ps", 0), ("ps", 1)], writes=["kk"])
                yield
                sc.op("dve", lambda e: e.tensor_tensor(out=kk, in0=kk, in1=oml, op=ALU.mult),
                      reads=["kk", "oml"], writes=["kk"])
                yield
                sc.op("act", lambda e: e.activation(out=lf, in_=kk, func=AF.Ln, scale=-1.0, bias=one_c),
                      reads=["kk", "cst"], writes=["lf"])
                yield

                def cums(e):
                    ins = None
                    for hf in range(2):
                        ins = e.matmul(ps[:, 2 + hf, :], lhsT=uinc_f, rhs=lf[:, hf * 512:(hf + 1) * 512],
                                       start=True, stop=True)
                    for hf in range(2):
                        ins = e.matmul(ps[:, 4 + hf, :], lhsT=uexc_f, rhs=lf[:, hf * 512:(hf + 1) * 512],
                                       start=True, stop=True)
                    for hh in range(8):
                        ins = e.matmul(ps[:, 0, hh * 2:hh * 2 + 2], lhsT=lf[:, hh * 128:(hh + 1) * 128],
                                       rhs=cind_f, start=True, stop=True)
                    return ins
                sc.op("pe", cums, reads=["lf", "cst"], writes=[("ps", 2), ("ps", 3), ("ps", 4), ("ps", 5), ("ps", 0)])
                yield
                sc.op("act", lambda e: e.activation(out=dec[p], in_=ps[:, 0, 0:16], func=AF.Exp),
                      reads=[("ps", 0)], writes=[("dec", p)])
                yield
                CQ.pump(1)
                yield
                yield from proj(hs, t4)
                CQ.pump(1)
                sc.op("act", lambda e: e.activation(out=qs, in_=P01, func=AF.Silu),
                      reads=[("ps", 0), ("ps", 1)], writes=["lf"])
                yield
                sc.op("act", lambda e: e.activation(out=E[0], in_=P23, func=AF.Exp),
                      reads=[("ps", 2), ("ps", 3)], writes=[("E", 0)])
                yield
                sc.op("dve", lambda e: e.tensor_tensor(out=q_e, in0=qs, in1=E[0], op=ALU.mult),
                      reads=["lf", ("E", 0)], writes=["q_e"])
                yield
                sc.op("act", lambda e: e.activation(out=E[1], in_=P23, func=AF.Exp, scale=-1.0),
                      reads=[("ps", 2), ("ps", 3)], writes=[("E", 1)])
                yield
                sc.op("dve", lambda e: e.tensor_tensor(out=k_e, in0=kk, in1=E[1], op=ALU.mult),
                      reads=["kk", ("E", 1)], writes=["k_e"])
                yield
                sc.op("act", lambda e: e.activation(out=E[0], in_=P45, func=AF.Exp),
                      reads=[("ps", 4), ("ps", 5)], writes=[("E", 0)])
                yield
                sc.op("dve", lambda e: e.tensor_tensor(out=k_d[p], in0=kk, in1=E[0], op=ALU.mult),
                      reads=["kk", ("E", 0)], writes=[("k_d", p)])
                yield
                yield from proj(hs, t4)
                CQ.pump(1)
                sc.op("act", lambda e: e.activation(out=vv[p], in_=P01, func=AF.Copy),
                      reads=[("ps", 0), ("ps", 1)], writes=[("vv", p)])
                yield
                yield from proj(hs, t4)
                CQ.pump(1)
                sc.op("act", lambda e: e.activation(out=gate[p], in_=P01, func=AF.Silu),
                      reads=[("ps", 0), ("ps", 1)], writes=[("gate", p)])
                yield

                def trq(e):
                    pv = pbank_bf(2).rearrange("p (k n) -> p k n", k=8)
                    ins = None
                    for k in range(8):
                        ins = e.transpose(out=pv[:, k, :], in_=q_e[:, k * 128:(k + 1) * 128], identity=ident_bf)
                    return ins
                sc.op("pe", trq, reads=["q_e", "cbf"], writes=[("ps", 2)])
                yield
                pv2 = pbank_bf(2).rearrange("p (k n) -> p k n", k=8)
                sc.op("dve", lambda e: e.tensor_copy(out=q_eTm[p][:, :, 0, 0:64], in_=pv2[:, :, 0:64]),
                      reads=[("ps", 2)], writes=[("q_eTm", p)])
                sc.op("dve", lambda e: e.tensor_copy(out=q_eTm[p][:, :, 1, 64:128], in_=pv2[:, :, 64:128]),
                      reads=[("ps", 2)], writes=[("q_eTm", p)], join=True)
                yield

                def trk(e):
                    pv = pbank_bf(3).rearrange("p (k n) -> p k n", k=8)
                    ins = None
                    for k in range(8):
                        ins = e.transpose(out=pv[:, k, :], in_=k_e[:, k * 128:(k + 1) * 128], identity=ident_bf)
                    return ins
                sc.op("pe", trk, reads=["k_e", "cbf"], writes=[("ps", 3)])
                yield
                sc.op("dve", lambda e: e.tensor_copy(out=k_eT[p], in_=pbank_bf(3).rearrange("p (k n) -> p k n", k=8)),
                      reads=[("ps", 3)], writes=[("k_eT", p)])
                yield

            def stageB(tt):
                p = tt % 2
                qm, kt_, kd_, v_, gt_, dc_ = q_eTm[p], k_eT[p], k_d[p], vv[p], gate[p], dec[p]

                def scores(e):
                    ins = None
                    for hh in range(8):
                        o_ap = ps[:, 6 + hh // 4, (hh % 4) * 128:(hh % 4 + 1) * 128]
                        ins = e.matmul(o_ap, lhsT=kt_[:, hh, :], rhs=qm[:, hh, 0, :], start=True, stop=False)
                        ins = e.matmul(o_ap, lhsT=kt_[:, hh, :], rhs=qm[:, hh, 1, :], start=False, stop=True)
                    return ins
                sc.op("pe", scores, reads=[("k_eT", p), ("q_eTm", p)], writes=[("ps", 6), ("ps", 7)])
                yield
                sc.op("dve", lambda e: e.tensor_tensor(out=scT, in0=P67.rearrange("p (h t) -> p h t", h=8),
                                                       in1=bc_mid(uinc_bf, 8), op=ALU.mult),
                      reads=[("ps", 6), ("ps", 7), "cbf"], writes=["scT"])
                yield
                for c in range(2):
                    def upd(e, c=c):
                        ins = None
                        for hh in range(8):
                            ins = e.matmul(ps[:, 6 + hh // 4, (hh % 4) * 128:(hh % 4 + 1) * 128],
                                           lhsT=kd_[c * 64:(c + 1) * 64, hh * 128:(hh + 1) * 128],
                                           rhs=v_[c * 64:(c + 1) * 64, hh * 128:(hh + 1) * 128],
                                           start=True, stop=True)
                        return ins
                    sc.op("pe", upd, reads=[("k_d", p), ("vv", p)], writes=[("ps", 6), ("ps", 7)])
                    yield
                    for hh in range(8):
                        sv = S32[:, hh * 128:(hh + 1) * 128]
                        sc.op("dve", lambda e, sv=sv, hh=hh, c=c: e.scalar_tensor_tensor(
                            out=sv, in0=sv, scalar=dc_[:, hh * 2 + c:hh * 2 + c + 1],
                            in1=ps[:, 6 + hh // 4, (hh % 4) * 128:(hh % 4 + 1) * 128],
                            op0=ALU.mult, op1=ALU.add),
                            reads=[("S32", hh), ("dec", p), ("ps", 6 + hh // 4)], writes=[("S32", hh)])
                        if hh % 4 == 3:
                            yield
                    if c == 0:
                        sc.op("act", lambda e: e.activation(out=Sbf[1], in_=S32, func=AF.Copy),
                              reads=[("S32", hh) for hh in range(8)], writes=[("Sbf", 1)])
                        yield

                def omm(e):
                    ins = None
                    for hh in range(8):
                        o_ap = ps[:, 6 + hh // 4, (hh % 4) * 128:(hh % 4 + 1) * 128]
                        ins = e.matmul(o_ap, lhsT=qm[:, hh, 0, :], rhs=Sbf[0][:, hh * 128:(hh + 1) * 128],
                                       start=True, stop=False)
                        ins = e.matmul(o_ap, lhsT=qm[:, hh, 1, :], rhs=Sbf[1][:, hh * 128:(hh + 1) * 128],
                                       start=False, stop=False)
                        ins = e.matmul(o_ap, lhsT=scT[:, hh, :], rhs=v_[:, hh * 128:(hh + 1) * 128],
                                       start=False, stop=True)
                    return ins
                sc.op("pe", omm, reads=[("q_eTm", p), ("Sbf", 0), ("Sbf", 1), "scT", ("vv", p)],
                      writes=[("ps", 6), ("ps", 7)])
                yield
                sc.op("act", lambda e: e.activation(out=Sbf[0], in_=S32, func=AF.Copy),
                      reads=[("S32", hh) for hh in range(8)], writes=[("Sbf", 0)])
                yield
                sc.op("act", lambda e: e.activation(out=osq, in_=P67, func=AF.Square),
                      reads=[("ps", 6), ("ps", 7)], writes=["osq"])
                yield
                sc.op("dve", lambda e: e.tensor_reduce(out=ssq, in_=osq.rearrange("p (h d) -> p h d", h=8),
                                                       axis=AX.X, op=ALU.add),
                      reads=["osq"], writes=["ssq"])
                yield
                sc.op("act", lambda e: e.activation(out=lno, in_=ssq, func=AF.Ln, scale=1.0 / 128, bias=eps_c),
                      reads=["ssq", "cst"], writes=["lno"])
                yield
                sc.op("act", lambda e: e.activation(out=rso, in_=lno, func=AF.Exp, scale=-0.5),
                      reads=["lno"], writes=["rso"])
                yield
                sc.op("dve", lambda e: e.tensor_tensor(out=osq.rearrange("p (h d) -> p h d", h=8),
                                                       in0=P67.rearrange("p (h d) -> p h d", h=8),
                                                       in1=bc_last(rso, 128), op=ALU.mult),
                      reads=[("ps", 6), ("ps", 7), "rso"], writes=["osq"])
                yield
                sc.op("dve", lambda e: e.tensor_tensor(out=yb, in0=osq, in1=gt_, op=ALU.mult),
                      reads=["osq", ("gate", p)], writes=["yb"])
                yield

                def try_(e):
                    pv = pbank_bf(6).rearrange("p (k n) -> p k n", k=8)
                    ins = None
                    for k in range(8):
                        ins = e.transpose(out=pv[:, k, :], in_=yb[:, k * 128:(k + 1) * 128], identity=ident_bf)
                    return ins
                sc.op("pe", try_, reads=["yb", "cbf"], writes=[("ps", 6)])
                yield
                sc.op("act", lambda e: e.activation(out=yT, in_=pbank_bf(6).rearrange("p (k n) -> p k n", k=8),
                                                    func=AF.Copy), reads=[("ps", 6)], writes=["yT"])
                yield
                for i in range(4):
                    wb, wkey = WSB.next()
                    bnk = 6 + i // 2
                    c0 = (i % 2) * 256

                    def f(e, wb=wb, bnk=bnk, c0=c0):
                        ins = None
                        for k in range(8):
                            ins = e.matmul(ps[:, bnk, c0:c0 + 256], lhsT=yT[:, k, :], rhs=wb[:, k, :],
                                           start=(k == 0), stop=(k == 7))
                        return ins
                    sc.op("pe", f, reads=[wkey, "yT"], writes=[("ps", bnk)], join=(i % 2 == 1))
                    yield
                sc.op("dve", lambda e: e.tensor_tensor(out=h[:, tt, :], in0=P67, in1=h[:, tt, :], op=ALU.add),
                      reads=[("ps", 6), ("ps", 7), ("h", tt)], writes=[("h", tt)])
                sq_stat(tt)
                yield

            interleave(stageA(0), None)
            for tt in range(NT):
                interleave(stageB(tt), stageA(tt + 1) if tt + 1 < NT else None)

        def rope(xv, nh, tt, tmp, key):
            sinv = bc_mid(cs_tab[:, 0, tt, :], nh)
            cosv = bc_mid(cs_tab[:, 1, tt, :], nh)
            x1 = xv[:, :, 0:8]
            x2 = xv[:, :, 8:16]
            sc.op("dve", lambda e: e.tensor_tensor(out=tmp[:, 0], in0=x1, in1=cosv, op=ALU.mult),
                  reads=[key, "cs_tab"], writes=[("rt", 0)])
            sc.op("dve", lambda e: e.tensor_tensor(out=tmp[:, 1], in0=x2, in1=sinv, op=ALU.mult),
                  reads=[key, "cs_tab"], writes=[("rt", 1)])
            sc.op("dve", lambda e: e.tensor_tensor(out=tmp[:, 2], in0=x2, in1=cosv, op=ALU.mult),
                  reads=[key, "cs_tab"], writes=[("rt", 2)])
            sc.op("dve", lambda e: e.tensor_tensor(out=tmp[:, 3], in0=x1, in1=sinv, op=ALU.mult),
                  reads=[key, "cs_tab"], writes=[("rt", 3)])
            sc.op("dve", lambda e: e.tensor_tensor(out=x1, in0=tmp[:, 0], in1=tmp[:, 1], op=ALU.subtract),
                  reads=[("rt", 0), ("rt", 1), ("rt", 3)], writes=[key])
            sc.op("dve", lambda e: e.tensor_tensor(out=x2, in0=tmp[:, 2], in1=tmp[:, 3], op=ALU.add),
                  reads=[("rt", 2), ("rt", 3), key], writes=[key])

        def shared_kv(seq):
            ssk = f32v(ph(0, 2), 2)
            lnk = f32v(ph(2, 2), 2)
            rk = f32v(ph(4, 2), 2)
            kn = f32v(ph(8, 128), 128)
            tmp = f32v(ph(136, 64), 64).rearrange("p (a h i) -> p a h i", a=4, h=2)
            kb = bf16v(ph(200, 128), 128)
            finish_stats()
            WS.push([(kvw_b[0], "kvw") for _ in range(NT)])
            knv = kn.rearrange("p (h d) -> p h d", h=2)
            kbv = kb.rearrange("p (h u d) -> p h u d", h=2, u=2)
            for g in range(NG):
                hs = norm_group(g)
                for t4 in range(4):
                    tt = g * 4 + t4
                    wb, wkey = WS.next()

                    def f(e, wb=wb, hs=hs, t4=t4):
                        ins = None
                        for k in range(8):
                            ins = e.matmul(ps[:, 2, 0:256], lhsT=hnT[hs][:, k, t4 * 128:(t4 + 1) * 128],
                                           rhs=wb[:, k, :], start=(k == 0), stop=(k == 7))
                        return ins
                    sc.op("pe", f, reads=[wkey, ("hnT", hs)], writes=[("ps", 2)])
                    for hh in range(2):
                        sc.op("act", lambda e, hh=hh: e.activation(out=junk[:, 0:64], in_=ps[:, 2, hh * 64:(hh + 1) * 64],
                                                                   func=AF.Square, accum_out=ssk[:, hh:hh + 1]),
                              reads=[("ps", 2)], writes=["ssk"], join=(hh == 1))
                    sc.op("act", lambda e, tt=tt: e.activation(out=vsb[:, tt, :, 0:64],
                                                               in_=ps[:, 2, 128:256].rearrange("p (h d) -> p h d", h=2),
                                                               func=AF.Copy),
                          reads=[("ps", 2)], writes=[("vsb", tt)])
                    sc.op("act", lambda e: e.activation(out=lnk, in_=ssk, func=AF.Ln, scale=1.0 / 64, bias=eps_c),
                          reads=["ssk", "cst"], writes=["lnk"])
                    sc.op("act", lambda e: e.activation(out=rk, in_=lnk, func=AF.Exp, scale=-0.5),
                          reads=["lnk"], writes=["rk"])
                    sc.op("dve", lambda e: e.tensor_tensor(out=knv, in0=ps[:, 2, 0:128].rearrange("p (h d) -> p h d", h=2),
                                                           in1=bc_last(rk, 64), op=ALU.mult),
                          reads=[("ps", 2), "rk"], writes=["kn"])
                    sc.op("dve", lambda e: e.tensor_tensor(out=kn, in0=kn, in1=cst[:, C_GK:C_GK + 128], op=ALU.mult),
                          reads=["kn", "cst"], writes=["kn"])
                    rope(knv, 2, tt, tmp, "kn")
                    sc.op("dve", lambda e: e.tensor_copy(out=kbv[:, :, 0, :], in_=knv), reads=["kn"], writes=["kb"])
                    sc.op("dve", lambda e: e.tensor_copy(out=kbv[:, :, 1, :], in_=knv), reads=["kn"], writes=["kb"],
                          join=True)

                    def trk(e):
                        pv = pbank_bf(3)
                        ins = None
                        for hh in range(2):
                            ins = e.transpose(out=pv[:, hh * 128:(hh + 1) * 128], in_=kb[:, hh * 128:(hh + 1) * 128],
                                              identity=ident_bf)
                        return ins
                    sc.op("pe", trk, reads=["kb", "cbf"], writes=[("ps", 3)])
                    sc.op("act", lambda e, tt=tt: e.activation(
                        out=kT[:, :, tt * 128:(tt + 1) * 128],
                        in_=pbank_bf(3)[:, 0:256].rearrange("p (h n) -> p h n", h=2), func=AF.Copy),
                        reads=[("ps", 3)], writes=[("kT", tt)])

        def swa(seq):
            o_ = [0]

            def A(n):
                o = o_[0]
                o_[0] += n
                return ph(o, n)
            gq = f32v(A(1024), 1024)
            sqf = f32v(A(1024), 1024)
            qn = f32v(A(1024), 1024)
            ssq = f32v(A(16), 16)
            lnq = f32v(A(16), 16)
            rq = f32v(A(16), 16)
            tmp = f32v(A(512), 512).rearrange("p (a h i) -> p a h i", a=4, h=16)
            qb = bf16v(A(512), 512)
            qTm = bf16v(A(1024), 1024).rearrange("p (j c t) -> p j c t", j=8, c=2)
            pe_ = [bf16v(A(512), 512).rearrange("p (h t) -> p h t", h=8) for _ in range(4)]
            den = f32v(A(16), 16)
            rden = f32v(A(16), 16)
            ob = bf16v(A(512), 512)
            obT = bf16v(A(512), 512).rearrange("p (k t) -> p k t", k=8)

            wres = [bf16v(A(1024), 1024).rearrange("p (k n) -> p k n", k=8) for _ in range(8)]
            CQ.require("wq")
            CQ.require("awo")
            for bi in range(8):
                src = wq_b[bi] if bi < 4 else awo_b[bi - 4]
                wn = "wq" if bi < 4 else "awo"
                sc.op("sp", lambda e, d=wres[bi], s_=src: e.dma_start(out=d.rearrange("p k n -> p (k n)"), in_=s_),
                      reads=[("scr", wn)], writes=[("wres", bi)], dma=("wres", bi))
            finish_stats()
            sc.op("sp", lambda e: e.dma_start(out=gq, in_=gq_d), writes=["gq"], dma="gq")
            sc.op("pool", lambda e: e.memset(qTm.rearrange("p j c t -> p (j c t)"), 0.0), writes=["qT"])
            qnv = qn.rearrange("p (h d) -> p h d", h=16)
            P01 = pbank(0, 2)
            pcount = 0
            obanks = (0, 1, 6, 7)
            for g in range(NG):
                hs = norm_group(g)
                for t4 in range(4):
                    tt = g * 4 + t4
                    for i in range(4):
                        wb, wkey = wres[i], ("wres", i)
                        b = i // 2
                        c0 = (i % 2) * 256

                        def f(e, wb=wb, b=b, c0=c0, hs=hs, t4=t4):
                            ins = None
                            for k in range(8):
                                ins = e.matmul(ps[:, b, c0:c0 + 256], lhsT=hnT[hs][:, k, t4 * 128:(t4 + 1) * 128],
                                               rhs=wb[:, k, :], start=(k == 0), stop=(k == 7))
                            return ins
                        sc.op("pe", f, reads=[wkey, ("hnT", hs)], writes=[("ps", b)], join=(i % 2 == 1))
                    CQ.pump(4)
                    sc.op("act", lambda e: e.activation(out=sqf, in_=P01, func=AF.Square),
                          reads=[("ps", 0), ("ps", 1)], writes=["sqf"])
                    sc.op("dve", lambda e: e.tensor_reduce(out=ssq, in_=sqf.rearrange("p (h d) -> p h d", h=16),
                                                           axis=AX.X, op=ALU.add), reads=["sqf"], writes=["ssq"])
                    sc.op("act", lambda e: e.activation(out=lnq, in_=ssq, func=AF.Ln, scale=1.0 / 64, bias=eps_c),
                          reads=["ssq", "cst"], writes=["lnq"])
                    sc.op("act", lambda e: e.activation(out=rq, in_=lnq, func=AF.Exp, scale=-0.5),
                          reads=["lnq"], writes=["rq"])
                    sc.op("dve", lambda e: e.tensor_tensor(out=qnv, in0=P01.rearrange("p (h d) -> p h d", h=16),
                                                           in1=bc_last(rq, 64), op=ALU.mult),
                          reads=[("ps", 0), ("ps", 1), "rq"], writes=["qn"])
                    sc.op("dve", lambda e: e.tensor_tensor(out=qn, in0=qn, in1=gq, op=ALU.mult),
                          reads=["qn", "gq"], writes=["qn"])
                    rope(qnv, 16, tt, tmp, "qn")
                    sc.op("dve", lambda e: e.tensor_copy(out=qb, in_=qn), reads=["qn"], writes=["qb"])

                    def trq(e):
                        pv = pbank_bf(2).rearrange("p (k n) -> p k n", k=8)
                        ins = None
                        for k in range(8):
                            ins = e.transpose(out=pv[:, k, :], in_=qb[:, k * 128:(k + 1) * 128], identity=ident_bf)
                        return ins
                    sc.op("pe", trq, reads=["qb", "cbf"], writes=[("ps", 2)])
                    pv2 = pbank_bf(2).rearrange("p (k n) -> p k n", k=8)
                    sc.op("act", lambda e: e.activation(out=qTm[0:64, :, 0, :], in_=pv2[0:64, :, :], func=AF.Copy),
                          reads=[("ps", 2)], writes=["qT"])
                    sc.op("act", lambda e: e.activation(out=qTm[64:128, :, 1, :], in_=pv2[64:128, :, :], func=AF.Copy),
                          reads=[("ps", 2)], writes=["qT"], join=True)
                    if SWA_DBG < 2:
                        continue
                    kts = [tt - 1, tt] if tt > 0 else [tt]
                    for kvh in range(2):
                        slots = []
                        for ki, kt in enumerate(kts):
                            sb = 2 + 2 * (pcount % 2)
                            pslot = (pcount % 2) * 2 + ki
                            slots.append(pslot)

                            def scm(e, kvh=kvh, kt=kt, sb=sb):
                                ins = None
                                for hl in range(8):
                                    hd = kvh * 8 + hl
                                    j, ehalf = hd // 2, hd % 2
                                    ins = e.matmul(ps[:, sb + hl // 4, (hl % 4) * 128:(hl % 4 + 1) * 128],
                                                   lhsT=kT[:, kvh, kt * 128:(kt + 1) * 128],
                                                   rhs=qTm[:, j, ehalf, :], start=True, stop=True)
                                return ins
                            sc.op("pe", scm, reads=[("kT", kt), "qT"], writes=[("ps", sb), ("ps", sb + 1)])
                            sc.op("act", lambda e, sb=sb, pslot=pslot: e.activation(
                                out=pe_[pslot], in_=pbank(sb, 2).rearrange("p (h t) -> p h t", h=8),
                                func=AF.Exp, scale=0.125),
                                reads=[("ps", sb), ("ps", sb + 1)], writes=[("pe", pslot)])
                            mk = maskd_bf if kt == tt else maskp_bf
                            sc.op(MASK_ENG, lambda e, pslot=pslot, mk=mk: e.tensor_tensor(
                                out=pe_[pslot], in0=pe_[pslot], in1=bc_mid(mk, 8), op=ALU.mult),
                                reads=[("pe", pslot), "cbf"], writes=[("pe", pslot)])
                            pcount += 1
                        if SWA_DBG < 3:
                            continue

                        def pv(e, kvh=kvh, kts=tuple(kts), slots=tuple(slots)):
                            ins = None
                            for hl in range(8):
                                hd = kvh * 8 + hl
                                ob_ = obanks[hd // 4]
                                c0 = (hd % 4) * 65
                                for ki, kt in enumerate(kts):
                                    ins = e.matmul(ps[:, ob_, c0:c0 + 65], lhsT=pe_[slots[ki]][:, hl, :],
                                                   rhs=vsb[:, kt, kvh, 0:65], start=(ki == 0),
                                                   stop=(ki == len(kts) - 1))
                            return ins
                        sc.op("pe", pv, reads=[("pe", sl) for sl in slots] + [("vsb", kt) for kt in kts] + ["vones"],
                              writes=[("ps", obanks[kvh * 2]), ("ps", obanks[kvh * 2 + 1])])
                    if SWA_DBG < 4:
                        continue
                    for pr in range(2):
                        b0 = obanks[pr * 2]
                        ov = ps[:, b0:b0 + 2, 0:260].rearrange("p b (h d) -> p b h d", h=4)
                        sc.op("dve", lambda e, ov=ov, pr=pr: e.tensor_tensor(
                            out=den[:, pr * 8:(pr + 1) * 8].rearrange("p (b h) -> p b h", b=2),
                            in0=ov[:, :, :, 64], in1=esink[:, pr * 8:(pr + 1) * 8].rearrange("p (b h) -> p b h", b=2),
                            op=ALU.add), reads=[("ps", b0), ("ps", b0 + 1), "esink"], writes=["den"], join=(pr == 1))
                    sc.op("dve", lambda e: e.reciprocal(out=rden, in_=den), reads=["den"], writes=["rden"])
                    for pr in range(2):
                        b0 = obanks[pr * 2]
                        for bb in range(2):
                            ov = ps[:, b0 + bb, 0:260].rearrange("p (h d) -> p h d", h=4)[:, :, 0:64]
                            hd0 = pr * 8 + bb * 4
                            sc.op("dve", lambda e, ov=ov, hd0=hd0: e.tensor_tensor(
                                out=ob[:, hd0 * 64:(hd0 + 4) * 64].rearrange("p (h d) -> p h d", h=4),
                                in0=ov, in1=bc_last(rden[:, hd0:hd0 + 4], 64), op=ALU.mult),
                                reads=[("ps", b0 + bb), "rden"], writes=["ob"], join=not (pr == 0 and bb == 0))

                    if SWA_DBG < 5:
                        continue

                    def tro(e):
                        pv_ = pbank_bf(2).rearrange("p (k n) -> p k n", k=8)
                        ins = None
                        for k in range(8):
                            ins = e.transpose(out=pv_[:, k, :], in_=ob[:, k * 128:(k + 1) * 128], identity=ident_bf)
                        return ins
                    sc.op("pe", tro, reads=["ob", "cbf"], writes=[("ps", 2)])
                    sc.op("act", lambda e: e.activation(out=obT, in_=pbank_bf(2).rearrange("p (k n) -> p k n", k=8),
                                                        func=AF.Copy), reads=[("ps", 2)], writes=["obT"])
                    for i in range(4):
                        wb, wkey = wres[4 + i], ("wres", 4 + i)
                        b = 4 + i // 2
                        c0 = (i % 2) * 256

                        def f(e, wb=wb, b=b, c0=c0):
                            ins = None
                            for k in range(8):
                                ins = e.matmul(ps[:, b, c0:c0 + 256], lhsT=obT[:, k, :], rhs=wb[:, k, :],
                                               start=(k == 0), stop=(k == 7))
                            return ins
                        sc.op("pe", f, reads=[wkey, "obT"], writes=[("ps", b)], join=(i % 2 == 1))
                    sc.op("dve", lambda e, tt=tt: e.tensor_tensor(out=h[:, tt, :], in0=pbank(4, 2), in1=h[:, tt, :],
                                                                  op=ALU.add),
                          reads=[("ps", 4), ("ps", 5), ("h", tt)], writes=[("h", tt)])
                    sq_stat(tt)

        for seq in range(nseq):
            for tt in range(NT):
                sc.op("sp", lambda e, seq=seq, tt=tt: e.dma_start(out=h[:, tt, :], in_=x[seq, tt * 128:(tt + 1) * 128, :]),
                      writes=[("h", tt)], dma=("ld", tt))
                sq_stat(tt)
            for si, st in enumerate(stages):
                lastp = (si == len(stages) - 1)
                if st.startswith("ffn"):
                    ffn(int(st[3]) * 2 + int(st[4]), lastp, seq)
                elif st == "hgrn":
                    hgrn(seq)
                elif st == "kv":
                    shared_kv(seq)
                elif st == "swa":
                    swa(seq)
                if lastp and not st.startswith("ffn"):
                    for tt in range(NT):
                        store_tile(seq, tt)
                sc.barrier()
        sc.wait_all("sp", stores)

        names = sc.assign()
        sems = {}
        for i, nm in enumerate(names):
            sems[nm] = es.enter_context(nc.semaphore(f"s{i}"))
        block = es.enter_context(nc.Block())

        @block.sync
        def _(e):
            sc.emit("sp", e, sems)

        @block.tensor
        def _(e):
            sc.emit("pe", e, sems)

        @block.scalar
        def _(e):
            sc.emit("act", e, sems)

        @block.vector
        def _(e):
            sc.emit("dve", e, sems)

        @block.gpsimd
        def _(e):
            sc.emit("pool", e, sems)
    return nc


def make_consts(ffn_norm_g, mix_norm_g, kv_norm_g, hgrn_onorm_g, attn_sinks, k_norm_g):
    c = np.zeros((128, NCST), np.float32)
    gains = [ffn_norm_g[0, 0], ffn_norm_g[0, 1], ffn_norm_g[1, 0], ffn_norm_g[1, 1],
             mix_norm_g[0], mix_norm_g[1], kv_norm_g]
    for gi, gv in enumerate(gains):
        c[:, C_G + gi * 8:C_G + gi * 8 + 8] = np.asarray(gv, np.float32).reshape(8, 128).T
    c[:, C_ONG] = np.asarray(hgrn_onorm_g, np.float32).reshape(128)
    idx = np.arange(128)
    s_ = idx[:, None]
    t_ = idx[None, :]
    c[:, C_IDENT:C_IDENT + 128] = (s_ == t_)
    same = (s_ // 64) == (t_ // 64)
    c[:, C_UINC:C_UINC + 128] = same & (s_ <= t_)
    c[:, C_UEXC:C_UEXC + 128] = same & (s_ > t_)
    c[:, C_CIND] = idx < 64
    c[:, C_CIND + 1] = idx >= 64
    c[:, C_MD:C_MD + 128] = (s_ <= t_)
    c[:, C_MP:C_MP + 128] = (s_ > t_)
    c[:, C_SINK:C_SINK + 16] = np.asarray(attn_sinks, np.float32).reshape(1, 16)
    c[:, C_EPS] = EPS
    c[:, C_ONE] = 1.0
    c[:, C_GK:C_GK + 128] = np.tile(np.asarray(k_norm_g, np.float32).reshape(64), 2)[None, :]
    return c


_PROG_CACHE = {}


def kernel(x, ffn_norm_g, ffn_w_gate_up, ffn_w_down, mix_norm_g, hgrn_w_in, hgrn_lb_logits,
           hgrn_onorm_g, hgrn_w_out, kv_norm_g, kv_w, k_norm_g, attn_w_q, q_norm_g,
           attn_sinks, attn_w_out, _stages=ALL_STAGES, _ncores=NCORES, _nseq=NSEQ):
    f = lambda a: np.ascontiguousarray(np.asarray(a, dtype=np.float32))
    x = f(x)
    key = (tuple(_stages), _nseq)
    if key not in _PROG_CACHE:
        _PROG_CACHE[key] = build_program(_nseq, tuple(_stages))
    nc = _PROG_CACHE[key]
    cst = make_consts(f(ffn_norm_g), f(mix_norm_g), f(kv_norm_g), f(hgrn_onorm_g)[0], f(attn_sinks)[0], f(k_norm_g))
    lbl = np.ascontiguousarray(np.broadcast_to(f(hgrn_lb_logits)[None, :, :], (128, 2, D)))
    gqb = np.ascontiguousarray(np.broadcast_to(np.tile(f(q_norm_g)[0], 16)[None, :], (128, D)))
    shared = {
        "ffn_w_gate_up": f(ffn_w_gate_up), "ffn_w_down": f(ffn_w_down),
        "hgrn_w_in": f(hgrn_w_in)[0], "hgrn_w_out": f(hgrn_w_out)[0], "kv_w": f(kv_w),
        "attn_w_q": f(attn_w_q)[0], "attn_w_out": f(attn_w_out)[0],
        "cst": cst, "lbl": lbl, "gqb": gqb,
    }
    in_maps = []
    for c in range(_ncores):
        m = dict(shared)
        m["x"] = np.ascontiguousarray(x[c * _nseq:(c + 1) * _nseq])
        in_maps.append(m)
    res = run_bass_kernel_spmd(nc, in_maps, core_ids=list(range(_ncores)))
    return np.concatenate([r["out"] for r in res.results], axis=0)
```
